# Optimizing a Trainium2 kernel written in Bass

```python
import math
import jax, jax.numpy as jnp
from jax import lax
import numpy as np

D_MODEL = 1024
BATCH = 4
SEQ = 8192
DEPTH = 2
DEC_BATCH = 8
DEC_SEQ = 2048
PAST_LEN = 128

D_MIX = D_MODEL
MLA_HEADS = 8
MLA_NOPE = 64
MLA_ROPE = 32
MLA_V = 64
Q_LORA = 256
KV_LORA = 128
ROPE_THETA = 10000.0
Q_BLOCK = 128
DIL_HEADS = 8
DIL_HEAD_DIM = 64
DIL_PAIRS = ((128, 1), (512, 4), (2048, 16))
D_FF = 4 * D_MODEL
NORM_EPS = 1e-6
NEG_BIG = -1e30

MLA_OUT = MLA_HEADS * MLA_V
DIL_OUT = DIL_HEADS * DIL_HEAD_DIM
DIL_QKV = DIL_HEADS * DIL_HEAD_DIM
IN_COLS = Q_LORA + KV_LORA + MLA_ROPE + 3 * DIL_QKV

kernel_name = "hybrid_mla_dilated_encoder"


def rms_norm(x, g):
    xf = x.astype(jnp.float32)
    y = xf * lax.rsqrt(jnp.mean(xf * xf, axis=-1, keepdims=True) + NORM_EPS)
    return (y * g.astype(jnp.float32)).astype(x.dtype)


def rope_tables(S, dtype):
    half = MLA_ROPE // 2
    inv_freq = ROPE_THETA ** (-jnp.arange(half, dtype=jnp.float32) / half)
    ang = jnp.arange(S, dtype=jnp.float32)[:, None] * inv_freq[None, :]
    return jnp.cos(ang).astype(dtype), jnp.sin(ang).astype(dtype)


def apply_rope(x, cos, sin):
    half = x.shape[-1] // 2
    x1, x2 = x[..., :half], x[..., half:]
    c = cos[None, :, None, :]
    s = sin[None, :, None, :]
    return jnp.concatenate([x1 * c - x2 * s, x1 * s + x2 * c], axis=-1)


def mla_attention(lat_q, lat_kv, k_rope_raw, g_q_lat, w_uq, g_kv_lat, w_ukv):
    B, S, _ = lat_q.shape
    cos, sin = rope_tables(S, lat_q.dtype)
    q = jnp.einsum('bsr,rhd->bshd', rms_norm(lat_q, g_q_lat), w_uq)
    q = jnp.concatenate([q[..., :MLA_NOPE], apply_rope(q[..., MLA_NOPE:], cos, sin)], axis=-1)
    kv = jnp.einsum('bsr,rhd->bshd', rms_norm(lat_kv, g_kv_lat), w_ukv)
    k_nope, v = kv[..., :MLA_NOPE], kv[..., MLA_NOPE:]
    k_rope = apply_rope(k_rope_raw[:, :, None, :], cos, sin)
    k = jnp.concatenate([k_nope, jnp.broadcast_to(k_rope, (B, S, MLA_HEADS, MLA_ROPE))], axis=-1)
    scale = 1.0 / math.sqrt(MLA_NOPE + MLA_ROPE)
    nq = S // Q_BLOCK
    qb = jnp.moveaxis(q.reshape(B, nq, Q_BLOCK, MLA_HEADS, MLA_NOPE + MLA_ROPE), 1, 0)

    def attend(qblk):
        s = jnp.einsum('bqhd,bkhd->bhqk', qblk, k).astype(jnp.float32) * scale
        p = jax.nn.softmax(s, axis=-1)
        return jnp.einsum('bhqk,bkhd->bqhd', p.astype(v.dtype), v)

    o = lax.map(attend, qb)
    return jnp.moveaxis(o, 0, 1).reshape(B, S, MLA_OUT)


def fold(t, d):
    B, S = t.shape[:2]
    rest = t.shape[2:]
    t = t.reshape((B, S // d, d) + rest)
    t = jnp.moveaxis(t, 2, 1)
    return t.reshape((B * d, S // d) + rest)


def unfold(t, B, d):
    L = t.shape[1]
    rest = t.shape[2:]
    t = t.reshape((B, d, L) + rest)
    t = jnp.moveaxis(t, 1, 2)
    return t.reshape((B, L * d) + rest)


def banded_attention(q, k, v, n, d, slopes):
    Bp, L, H, D = q.shape
    nb = -(-L // n)
    Lp = nb * n
    qp = jnp.pad(q, ((0, 0), (0, Lp - L), (0, 0), (0, 0))).reshape(Bp, nb, n, H, D)
    pad_k = ((0, 0), (n, Lp - L + n), (0, 0), (0, 0))
    kr = jnp.pad(k, pad_k).reshape(Bp, nb + 2, n, H, D)
    vr = jnp.pad(v, pad_k).reshape(Bp, nb + 2, n, H, D)
    kw = jnp.concatenate([kr[:, :-2], kr[:, 1:-1], kr[:, 2:]], axis=2)
    vw = jnp.concatenate([vr[:, :-2], vr[:, 1:-1], vr[:, 2:]], axis=2)
    blk = jnp.arange(nb)[:, None]
    qidx = blk * n + jnp.arange(n)[None, :]
    kidx = blk * n - n + jnp.arange(3 * n)[None, :]
    rel = jnp.abs(kidx[:, None, :] - qidx[:, :, None])
    valid = (rel <= n) & (kidx[:, None, :] >= 0) & (kidx[:, None, :] < L)
    bias = -slopes[:, None, None, None] * (d * rel).astype(jnp.float32)[None]
    bias = jnp.moveaxis(bias, 0, 1)
    s = jnp.einsum('bnqhd,bnkhd->bnhqk', qp, kw).astype(jnp.float32) / math.sqrt(D)
    s = jnp.where(valid[:, None], s + bias[None], NEG_BIG)
    m = jnp.max(s, axis=-1)
    e = jnp.exp(s - m[..., None])
    den = jnp.sum(e, axis=-1)
    o = jnp.einsum('bnhqk,bnkhd->bnqhd', e.astype(v.dtype), vw).astype(jnp.float32)
    den_t = jnp.swapaxes(den, 2, 3)
    o = (o / den_t[..., None]).reshape(Bp, Lp, H, D)[:, :L]
    m_t = jnp.swapaxes(m, 2, 3).reshape(Bp, Lp, H)[:, :L]
    den_t = den_t.reshape(Bp, Lp, H)[:, :L]
    return o, m_t, den_t


def dilated_attention(q, k, v):
    B, S, H, D = q.shape
    slopes = 2.0 ** (-8.0 * (jnp.arange(H, dtype=jnp.float32) + 1.0) / H)
    outs, maxs, dens = [], [], []
    for window, d in DIL_PAIRS:
        n = window // (2 * d)
        o, m, den = banded_attention(fold(q, d), fold(k, d), fold(v, d), n, d, slopes)
        outs.append(unfold(o, B, d))
        maxs.append(unfold(m, B, d))
        dens.append(unfold(den, B, d))
    mx = jnp.stack(maxs)
    w = jnp.stack(dens) * jnp.exp(mx - jnp.max(mx, axis=0, keepdims=True))
    out = jnp.sum(w[..., None] * jnp.stack(outs), axis=0) / jnp.sum(w, axis=0)[..., None]
    return out.astype(q.dtype).reshape(B, S, H * D)


def encoder_layer(x, g_pre_mix, w_in, g_q_lat, w_uq, g_kv_lat, w_ukv, g_out_mla, g_out_dil,
                  w_out, g_post_mix, g_pre_mlp, w_up, w_down, g_post_mlp):
    B, S, _ = x.shape
    z = rms_norm(x, g_pre_mix) @ w_in
    c0 = Q_LORA
    c1 = c0 + KV_LORA
    c2 = c1 + MLA_ROPE
    c3 = c2 + DIL_QKV
    c4 = c3 + DIL_QKV
    a = mla_attention(z[..., :c0], z[..., c0:c1], z[..., c1:c2], g_q_lat, w_uq, g_kv_lat, w_ukv)
    hd = (B, S, DIL_HEADS, DIL_HEAD_DIM)
    b = dilated_attention(z[..., c2:c3].reshape(hd), z[..., c3:c4].reshape(hd), z[..., c4:].reshape(hd))
    mix = jnp.concatenate([rms_norm(a, g_out_mla), rms_norm(b, g_out_dil)], axis=-1) @ w_out
    x = x + rms_norm(mix, g_post_mix)
    u = jnp.square(jax.nn.relu(rms_norm(x, g_pre_mlp) @ w_up))
    return x + rms_norm(u @ w_down, g_post_mlp)


def setup_inputs(seed: int = 0) -> dict:
    key = jax.random.key(seed)
    ks = jax.random.split(key, 20)
    f32 = jnp.float32

    def nrm(k, shape, fan_in):
        return jax.random.normal(k, shape, f32) * (fan_in ** -0.5)

    def gain(k, dim):
        return 1.0 + 0.05 * jax.random.normal(k, (DEPTH, dim), f32)

    return {
        "x_prompt": jax.random.normal(ks[0], (BATCH, SEQ, D_MODEL), f32),
        "x_sample": jax.random.normal(ks[1], (DEC_BATCH, DEC_SEQ, D_MODEL), f32),
        "g_pre_mix": gain(ks[2], D_MODEL),
        "w_in": nrm(ks[3], (DEPTH, D_MODEL, IN_COLS), D_MODEL),
        "g_q_lat": gain(ks[4], Q_LORA),
        "w_uq": nrm(ks[5], (DEPTH, Q_LORA, MLA_HEADS, MLA_NOPE + MLA_ROPE), Q_LORA),
        "g_kv_lat": gain(ks[6], KV_LORA),
        "w_ukv": nrm(ks[7], (DEPTH, KV_LORA, MLA_HEADS, MLA_NOPE + MLA_V), KV_LORA),
        "g_out_mla": gain(ks[8], MLA_OUT),
        "g_out_dil": gain(ks[9], DIL_OUT),
        "w_out": nrm(ks[10], (DEPTH, D_MIX, D_MODEL), D_MIX),
        "g_post_mix": gain(ks[11], D_MODEL),
        "g_pre_mlp": gain(ks[12], D_MODEL),
        "w_up": nrm(ks[13], (DEPTH, D_MODEL, D_FF), D_MODEL),
        "w_down": nrm(ks[14], (DEPTH, D_FF, D_MODEL), D_FF),
        "g_post_mlp": gain(ks[15], D_MODEL),
    }


def reference(x_prompt, x_sample, g_pre_mix, w_in, g_q_lat, w_uq, g_kv_lat, w_ukv, g_out_mla,
              g_out_dil, w_out, g_post_mix, g_pre_mlp, w_up, w_down, g_post_mlp):
    def trunk(x):
        for l in range(DEPTH):
            x = encoder_layer(x, g_pre_mix[l], w_in[l], g_q_lat[l], w_uq[l], g_kv_lat[l], w_ukv[l],
                              g_out_mla[l], g_out_dil[l], w_out[l], g_post_mix[l], g_pre_mlp[l],
                              w_up[l], w_down[l], g_post_mlp[l])
        return x

    y_prompt = trunk(x_prompt)
    y_sample = trunk(x_sample)
    return (y_prompt, y_sample)
```

```python
import math
from contextlib import ExitStack

import numpy as np
import concourse.bass as bass
import concourse.mybir as mybir
from concourse.bass_utils import run_bass_kernel_spmd

F32 = mybir.dt.float32
BF16 = mybir.dt.bfloat16
ALU = mybir.AluOpType
AF = mybir.ActivationFunctionType

D = 1024
DEPTH = 2
H = 8
Q_LORA, KV_LORA, ROPE, NOPE, VD = 256, 128, 32, 64, 64
DQK = NOPE + ROPE
IN_COLS = 1952
C_KR = 384
C_DQ = 416
C_DK = 928
C_DV = 1440
DFF = 4096
EPS = 1e-6
DIL = (1, 4, 16)
ENG = ["pe", "act", "dve", "pool", "sp"]
SAME_ENG_SYNC = True


class Rec:
    def __init__(self, nc, ndma=48):
        self.nc = nc
        self.ops = {e: [] for e in ENG}
        self.cnt = {e: 0 for e in ENG}
        self.seen = {e: {} for e in ENG}
        self.last_w = {}
        self.readers = {}
        self.ndma = ndma
        self.dma_val = [0] * ndma
        self.dma_next = 0
        self.dma_next_sw = 0
        self.nops = 0

    def _need(self, eng, tok):
        if tok is None:
            return
        sid, val = tok
        if sid == eng and (eng == "pe" or not SAME_ENG_SYNC):
            return
        if self.seen[eng].get(sid, 0) >= val:
            return
        self.seen[eng][sid] = val
        self.ops[eng].append(("wait", sid, val))

    def _deps(self, eng, r, w):
        for k in r:
            self._need(eng, self.last_w.get(k))
        for k in w:
            self._need(eng, self.last_w.get(k))
            for t in self.readers.get(k, ()):
                self._need(eng, t)

    def _commit(self, tok, r, w):
        for k in r:
            self.readers.setdefault(k, []).append(tok)
        for k in w:
            self.last_w[k] = tok
            self.readers[k] = []

    def op(self, eng, fn, r=(), w=(), inc=True):
        self._deps(eng, r, w)
        if inc:
            self.cnt[eng] += 1
            tok = (eng, self.cnt[eng])
        else:
            tok = (eng, self.cnt[eng] + 1)
        px = _Proxy()
        fn(px)
        self.ops[eng].append(("op", px.call, inc))
        self._commit(tok, r, w)
        self.nops += 1
        return tok

    def dma(self, eng, out, in_, r=(), w=(), noncontig=False):
        self._deps(eng, r, w)
        if eng == "pool":
            i = self.ndma - 12 + (self.dma_next_sw % 12)
            self.dma_next_sw += 1
        else:
            i = self.dma_next % (self.ndma - 12)
            self.dma_next += 1
        if self.dma_val[i] > 0:
            self._need(eng, (("d", i), self.dma_val[i]))
        self.dma_val[i] += 16
        tok = (("d", i), self.dma_val[i])
        self.ops[eng].append(("dma", out, in_, i, noncontig))
        self._commit(tok, r, w)
        self.nops += 1
        return tok

    def barrier(self):
        for e in ENG:
            for f in ENG:
                if f != e and self.cnt[f] > 0:
                    self._need(e, (f, self.cnt[f]))
            for i in range(self.ndma):
                if self.dma_val[i] > 0:
                    self._need(e, (("d", i), self.dma_val[i]))
        self.last_w = {}
        self.readers = {}

    def emit(self):
        nc = self.nc
        with ExitStack() as st:
            sem = {e: st.enter_context(nc.semaphore("s_" + e)) for e in ENG}
            dsem = [st.enter_context(nc.semaphore("d%d" % i)) for i in range(self.ndma)]
            block = st.enter_context(nc.Block())

            def sem_of(sid):
                return sem[sid] if isinstance(sid, str) else dsem[sid[1]]

            def replay(name, eng):
                for o in self.ops[name]:
                    if o[0] == "wait":
                        eng.wait_ge(sem_of(o[1]), o[2])
                    elif o[0] == "op":
                        mname, a, kw = o[1]
                        ins = getattr(eng, mname)(*a, **kw)
                        if o[2]:
                            ins.then_inc(sem[name], 1)
                    else:
                        _, out, in_, i, nonc = o
                        if nonc:
                            with nc.allow_non_contiguous_dma(reason="small strided load"):
                                eng.dma_start(out=out, in_=in_).then_inc(dsem[i], 16)
                        else:
                            eng.dma_start(out=out, in_=in_).then_inc(dsem[i], 16)

            @block.tensor
            def _(e):
                replay("pe", e)

            @block.scalar
            def _(e):
                replay("act", e)

            @block.vector
            def _(e):
                replay("dve", e)

            @block.gpsimd
            def _(e):
                replay("pool", e)

            @block.sync
            def _(e):
                replay("sp", e)


class _Proxy:
    def __init__(self):
        self.call = None

    def __getattr__(self, name):
        def f(*a, **kw):
            self.call = (name, a, kw)
            return None
        return f


class Ring:
    def __init__(self, items):
        self.items = items
        self.i = 0

    def next(self):
        it = self.items[self.i % len(self.items)]
        self.i += 1
        return it


class Prog:
    def __init__(self, nc, seqs, depth=DEPTH, debug=False):
        self.nc = nc
        self.R = Rec(nc)
        self.seqs = seqs
        self.depth = depth
        self.debug = debug
        self.uid = 0

    def sb(self, st, shape, dt, name=None):
        self.uid += 1
        return st.enter_context(self.nc.sbuf_tensor("%s_%d" % (name or "t", self.uid), list(shape), dt))

    def ps(self, st, shape, dt, name=None):
        self.uid += 1
        return st.enter_context(self.nc.psum_tensor("%s_%d" % (name or "p", self.uid), list(shape), dt))

    def mm(self, out, lhsT, rhs, start, stop, r, w, inc=True):
        return self.R.op("pe", lambda e: e.matmul(out, lhsT, rhs, start=start, stop=stop), r=r, w=w, inc=inc)

    def tr(self, out, in_, ident, r, w, inc=True):
        return self.R.op("pe", lambda e: e.transpose(out, in_, ident), r=r, w=w, inc=inc)

    def copy(self, eng, out, in_, r, w):
        if eng == "act":
            return self.R.op("act", lambda e: e.activation(out=out, in_=in_, func=AF.Copy), r=r, w=w)
        if eng == "pool":
            return self.R.op("pool", lambda e: e.tensor_copy(out, in_), r=r, w=w)
        return self.R.op("dve", lambda e: e.tensor_copy(out, in_), r=r, w=w)

    def rstd(self, ss, ms, rs, n, kss, kms, krs):
        R = self.R
        R.op("dve", lambda e: e.tensor_scalar(out=ms, in0=ss, scalar1=1.0 / n, scalar2=EPS,
                                              op0=ALU.mult, op1=ALU.add), r=[kss], w=[kms])
        nh = self.neghalf[:, 0:ss.shape[1]]
        R.op("pool", lambda e: e.tensor_tensor(out=rs, in0=ms, in1=nh, op=ALU.pow), r=[kms, "const"], w=[krs])

    def load_w(self, dst, src, gcol, kdst, pdim=128):
        R = self.R
        n = dst.shape[1]
        c0 = 0
        while c0 < n:
            c1 = min(n, c0 + 1024)
            stg, kst = self.wstage.next()
            R.dma("sp", stg[0:pdim, 0:c1 - c0], src[:, c0:c1], w=[kst])
            d = dst[:, c0:c1]
            s = stg[0:pdim, 0:c1 - c0]
            if gcol is None:
                R.op("pool", lambda e, d=d, s=s: e.tensor_copy(d, s), r=[kst], w=[kdst])
            else:
                R.op("pool", lambda e, d=d, s=s: e.tensor_scalar(out=d, in0=s, scalar1=gcol, scalar2=1.0,
                                                                 op0=ALU.mult, op1=ALU.mult),
                     r=[kst, "gcols"], w=[kdst])
            c0 = c1

    def build(self):
        nc = self.nc
        R = self.R
        dt = nc.dram_tensor
        W = {}
        for nm, shp in [("g_pre_mix", [DEPTH, D]), ("w_in", [DEPTH, D, IN_COLS]), ("g_q_lat", [DEPTH, Q_LORA]),
                        ("w_uq", [DEPTH, Q_LORA, H * DQK]), ("g_kv_lat", [DEPTH, KV_LORA]),
                        ("w_ukv", [DEPTH, KV_LORA, H * 128]), ("g_out_mla", [DEPTH, 512]),
                        ("g_out_dil", [DEPTH, 512]), ("w_out", [DEPTH, D, D]), ("g_post_mix", [DEPTH, D]),
                        ("g_pre_mlp", [DEPTH, D]), ("w_up", [DEPTH, D, DFF]), ("w_down", [DEPTH, DFF, D]),
                        ("g_post_mlp", [DEPTH, D])]:
            W[nm] = dt(nm, shp, F32, kind="ExternalInput").ap()
        self.W = W
        ident_d = dt("ident", [128, 128], F32, kind="ExternalInput").ap()
        bmask_d = dt("bmask", [128, 384], F32, kind="ExternalInput").ap()
        for s in self.seqs:
            S = s["S"]
            nm = s["name"]
            s["x"] = dt("x_" + nm, [S, D], F32, kind="ExternalInput").ap()
            s["rope"] = dt("rope_" + nm, [2, 32, S], F32, kind="ExternalInput").ap()
            s["y"] = dt("y_" + nm, [s["nq_last"], D], F32, kind="ExternalOutput").ap()
            kind = "ExternalOutput" if self.debug else "Internal"
            s["xa"] = dt("xa_" + nm, [S, D], F32, kind=kind).ap()
            s["xb"] = dt("xb_" + nm, [S, D], F32, kind=kind).ap()
            s["a"] = dt("a_" + nm, [S, 512], F32, kind=kind).ap()
            s["b"] = dt("b_" + nm, [S, 512], F32, kind=kind).ap()
            s["qdT"] = dt("qdT_" + nm, [512, S], BF16).ap()
            s["kdT"] = dt("kdT_" + nm, [512, S], BF16).ap()
            s["vd"] = dt("vd_" + nm, [S, 512], BF16).ap()

        with ExitStack() as g:
            self.ident_f = self.sb(g, [128, 128], F32, "identf")
            self.ident_b = self.sb(g, [128, 128], BF16, "identb")
            self.bmask = self.sb(g, [128, 384], F32, "bmask")
            self.neghalf = self.sb(g, [128, 8], F32, "neghalf")
            self.gcols = self.sb(g, [128, DEPTH, 32], F32, "gcols")
            ws = [self.sb(g, [128, 1024], F32, "wstage") for _ in range(2)]
            self.wstage = Ring([(ws[0], "wst0"), (ws[1], "wst1")])
            R.dma("sp", self.ident_f[:, :], ident_d, w=["const"])
            R.dma("sp", self.bmask[:, :], bmask_d, w=["const"])
            R.op("dve", lambda e: e.tensor_copy(self.ident_b[:, :], self.ident_f[:, :]), r=["const"], w=["const"])
            R.op("pool", lambda e: e.memset(self.neghalf[:, :], -0.5), w=["const"])
            for l in range(self.depth):
                for nm, off, k in [("g_pre_mix", 0, 8), ("g_pre_mlp", 8, 8), ("g_out_mla", 16, 4),
                                   ("g_out_dil", 20, 4), ("g_q_lat", 24, 2), ("g_kv_lat", 26, 1)]:
                    src = W[nm][l, :].rearrange("(k p) -> p k", p=128)
                    R.dma("sp", self.gcols[:, l, off:off + k], src, w=["gcols"], noncontig=True)
            R.barrier()

            for l in range(self.depth):
                last = (l == self.depth - 1)
                with ExitStack() as stl:
                    pre_w = None
                    for si, s in enumerate(self.seqs):
                        src = s["x"] if l == 0 else s["xb"]
                        nq = s["nq_last"] if last else s["S"]
                        self.attn_phases(l, s, src, pre_w=pre_w, do_dilated=False)
                        pre_w = None
                        if si + 1 < len(self.seqs) and self.seqs[si + 1]["S"] <= 2048:
                            pre_w = self.load_w_in(stl, l)
                        self.dilated(l, s, nq)
                with ExitStack() as stcd:
                    self.phase_c(l, None)
                    w_up = self.sb(stcd, [128, 8, DFF], BF16, "w_up")
                    self.phase_d(l, last, w_up, load_up=True)
            R.barrier()
        R.emit()

    def load_w_in(self, st, l):
        R, W, gc = self.R, self.W, self.gcols
        w_in = self.sb(st, [128, 8, IN_COLS], BF16, "w_in")
        w_kr = self.sb(st, [128, 8, 96], BF16, "w_kr")
        w_krs = self.sb(st, [128, 8, 96], BF16, "w_krs")
        self.uid += 1
        k = "_%d" % self.uid
        R.op("pool", lambda e: e.memset(w_kr[:, :, :], 0.0), w=["w_kr" + k])
        R.op("pool", lambda e: e.memset(w_krs[:, :, :], 0.0), w=["w_krs" + k])
        for kc in range(8):
            self.load_w(w_in[:, kc, :], W["w_in"][l, kc * 128:(kc + 1) * 128, :], gc[:, l, kc:kc + 1], "w_in" + k)
        R.op("pool", lambda e: e.tensor_copy(w_kr[:, :, 64:96], w_in[:, :, C_KR:C_KR + 32]),
             r=["w_in" + k], w=["w_kr" + k])
        R.op("pool", lambda e: e.tensor_copy(w_krs[:, :, 64:80], w_in[:, :, C_KR + 16:C_KR + 32]),
             r=["w_in" + k], w=["w_krs" + k])
        R.op("pool", lambda e: e.tensor_copy(w_krs[:, :, 80:96], w_in[:, :, C_KR:C_KR + 16]),
             r=["w_in" + k], w=["w_krs" + k])
        return (w_in, w_kr, w_krs, k)

    def attn_phases(self, l, s, xsrc, pre_w=None, do_dilated=True):
        nc, R, W = self.nc, self.R, self.W
        S = s["S"]
        NT = S // 128
        NB = S // 512
        last = (l == self.depth - 1)
        nq = s["nq_last"] if last else S
        gc = self.gcols
        with ExitStack() as st:
            qlatT = self.sb(st, [128, 2, S], BF16, "qlatT")
            kvlatT = self.sb(st, [128, S], BF16, "kvlatT")
            KT = [self.sb(st, [128, S], BF16, "KT%d" % i) for i in range(2)]
            ropet = [self.sb(st, [128, 2, 512], F32, "ropet%d" % i) for i in range(1)]
            ropering = Ring([(ropet[0], "rope0")])
            self.phase_a(l, s, xsrc, st, qlatT, kvlatT, pre_w, KT, ropering)
            self.mla(l, s, nq, st, qlatT, kvlatT, None, KT, None, None, None, None, ropering)
            R.barrier()
        if do_dilated:
            self.dilated(l, s, nq)

    def phase_a(self, l, s, xsrc, st_outer, qlatT, kvlatT, vall, KT, ropering):
        nc, R, W = self.nc, self.R, self.W
        S = s["S"]
        NB = S // 512
        gc = self.gcols
        with ExitStack() as st:
            if vall is None:
                w_in, w_kr, w_krs, wk = self.load_w_in(st, l)
            else:
                w_in, w_kr, w_krs, wk = vall
            kw_in, kw_kr, kw_krs = "w_in" + wk, "w_kr" + wk, "w_krs" + wk
            xin = [self.sb(st, [128, 4, D], F32, "xin%d" % i) for i in range(2)]
            xn = self.sb(st, [128, 4, D], BF16, "xn")
            junk = self.sb(st, [128, 256], BF16, "junk")
            hTs = [self.sb(st, [128, 8, 512], BF16, "hT%d" % i) for i in range(2)]
            stat = self.sb(st, [128, 3, 16], F32, "stat")
            latn = self.sb(st, [128, 4, 384], BF16, "latn")
            qk_st = [self.sb(st, [128, 8, 512], BF16, "qkst%d" % i) for i in range(1)]
            v_st = [self.sb(st, [128, 4, 512], BF16, "vst%d" % i) for i in range(1)]
            kr_t = self.sb(st, [128, 2, 512], F32, "krt")
            psT = [self.ps(st, [128, 1024], BF16, "psT%d" % i) for i in range(2)]
            psL = [self.ps(st, [128, 512], F32, "psL%d" % i) for i in range(2)]
            psM = [self.ps(st, [128, 512], F32, "psM%d" % i) for i in range(3)]
            psLT = self.ps(st, [128, 1024], BF16, "psLT")
            psTr = Ring([(psT[0], "psT0"), (psT[1], "psT1")])
            psLr = Ring([(psL[0], "psL0"), (psL[1], "psL1")])
            psMr = Ring([(psM[i], "psM%d" % i) for i in range(3)])

            evac_eng = Ring(["act", "dve"])
            rope_state = {}

            def item_load(b):
                xb_, kx = xin[b % 2], "xin%d" % (b % 2)
                t0 = b * 512
                R.dma("sp", xb_[:, :, :], xsrc[t0:t0 + 512, :].rearrange("(j p) d -> p j d", p=128),
                      r=[("x", s["name"], b)], w=[kx])

            def item_norm_act(b):
                xb_, kx = xin[b % 2], "xin%d" % (b % 2)
                for j in range(4):
                    R.op("act", lambda e, j=j: e.activation(out=xn[:, j, :], in_=xb_[:, j, :], func=AF.Square,
                                                            accum_out=stat[:, 0, j:j + 1]),
                         r=[kx], w=["xn", "ss_x"])

            def item_norm(b, act=True):
                xb_, kx = xin[b % 2], "xin%d" % (b % 2)
                if act:
                    item_norm_act(b)
                self.rstd(stat[:, 0, 0:4], stat[:, 1, 0:4], stat[:, 2, 0:4], D, "ss_x", "ms_x", "rs_x")
                for j in range(4):
                    R.op("dve", lambda e, j=j: e.tensor_scalar(out=xn[:, j, :], in0=xb_[:, j, :],
                                                               scalar1=stat[:, 2, j:j + 1], scalar2=None,
                                                               op0=ALU.mult), r=[kx, "rs_x"], w=["xn"])

            def item_tr(b, half):
                hT, khT = hTs[b % 2], "hT%d" % (b % 2)
                for c2 in (2 * half, 2 * half + 1):
                    pt, kpt = psTr.next()
                    for cc in range(2):
                        c = c2 * 2 + cc
                        for j in range(4):
                            self.tr(pt[:, cc * 512 + j * 128: cc * 512 + (j + 1) * 128],
                                    xn[:, j, c * 128:(c + 1) * 128], self.ident_b[:, :],
                                    r=["xn", "const"], w=[kpt], inc=(cc == 1 and j == 3))
                    self.copy(evac_eng.next(), hT[:, c2 * 2:c2 * 2 + 2, :],
                              pt[:, :].rearrange("p (c t) -> p c t", c=2), r=[kpt], w=[khT])

            def item_lat(b, j):
                hT, khT = hTs[b % 2], "hT%d" % (b % 2)
                pl, kpl = psLr.next()
                for kc in range(8):
                    self.mm(pl[:, 0:384], hT[:, kc, j * 128:(j + 1) * 128], w_in[:, kc, 0:384],
                            start=(kc == 0), stop=(kc == 7), r=[khT, kw_in], w=[kpl], inc=(kc == 7))
                R.op("act", lambda e: e.activation(out=junk[:, 0:256], in_=pl[:, 0:256], func=AF.Square,
                                                   accum_out=stat[:, 0, 4 + j:5 + j]), r=[kpl], w=["ss_l"])
                R.op("act", lambda e: e.activation(out=junk[:, 0:128], in_=pl[:, 256:384], func=AF.Square,
                                                   accum_out=stat[:, 0, 8 + j:9 + j]), r=[kpl], w=["ss_l"])
                R.op("dve", lambda e: e.tensor_scalar(out=stat[:, 1, 4 + j:5 + j], in0=stat[:, 0, 4 + j:5 + j],
                                                      scalar1=1.0 / Q_LORA, scalar2=EPS, op0=ALU.mult,
                                                      op1=ALU.add), r=["ss_l"], w=["ms_l"])
                R.op("dve", lambda e: e.tensor_scalar(out=stat[:, 1, 8 + j:9 + j], in0=stat[:, 0, 8 + j:9 + j],
                                                      scalar1=1.0 / KV_LORA, scalar2=EPS, op0=ALU.mult,
                                                      op1=ALU.add), r=["ss_l"], w=["ms_l"])
                R.op("pool", lambda e: e.tensor_tensor(out=stat[:, 2, 4 + j:12 + j:4], in0=stat[:, 1, 4 + j:12 + j:4],
                                                       in1=self.neghalf[:, 0:2], op=ALU.pow),
                     r=["ms_l", "const"], w=["rs_l"])
                return (pl, kpl)

            def item_latnorm(b, j, plk):
                pl, kpl = plk
                R.op("dve", lambda e: e.tensor_scalar(out=latn[:, j, 0:256], in0=pl[:, 0:256],
                                                      scalar1=stat[:, 2, 4 + j:5 + j], scalar2=None,
                                                      op0=ALU.mult), r=[kpl, "rs_l"], w=["latn"])
                R.op("dve", lambda e: e.tensor_scalar(out=latn[:, j, 256:384], in0=pl[:, 256:384],
                                                      scalar1=stat[:, 2, 8 + j:9 + j], scalar2=None,
                                                      op0=ALU.mult), r=[kpl, "rs_l"], w=["latn"])

            def item_lattr(b):
                t0 = b * 512
                for c in range(3):
                    for j in range(4):
                        self.tr(psLT[:, j * 128:(j + 1) * 128], latn[:, j, c * 128:(c + 1) * 128], self.ident_b[:, :],
                                r=["latn", "const"], w=["psLT"], inc=(j == 3))
                    dst = qlatT[:, c, t0:t0 + 512] if c < 2 else kvlatT[:, t0:t0 + 512]
                    self.copy(evac_eng.next(), dst, psLT[:, 0:512], r=["psLT"], w=["qlatT" if c < 2 else "kvlatT"])

            def item_krope_mm(b):
                hT, khT = hTs[b % 2], "hT%d" % (b % 2)
                t0 = b * 512
                rt, krt = ropering.next()
                R.dma("sp", rt[64:96, :, :], s["rope"][:, :, t0:t0 + 512].rearrange("a r t -> r a t"), w=[krt])
                pa, kpa = psMr.next()
                pb, kpb = psMr.next()
                for kc in range(8):
                    self.mm(pa[0:96, :], w_kr[:, kc, :], hT[:, kc, :], start=(kc == 0), stop=(kc == 7),
                            r=[khT, kw_kr], w=[kpa], inc=(kc == 7))
                for kc in range(8):
                    self.mm(pb[0:96, :], w_krs[:, kc, :], hT[:, kc, :], start=(kc == 0), stop=(kc == 7),
                            r=[khT, kw_krs], w=[kpb], inc=(kc == 7))
                R.op("dve", lambda e: e.tensor_tensor(out=kr_t[64:96, 0, :], in0=pa[64:96, :],
                                                      in1=rt[64:96, 0, :], op=ALU.mult), r=[kpa, krt], w=["krt0"])
                R.op("dve", lambda e: e.tensor_tensor(out=kr_t[64:96, 1, :], in0=pb[64:96, :],
                                                      in1=rt[64:96, 1, :], op=ALU.mult), r=[kpb, krt], w=["krt1"])

            def item_krope_fin(b):
                t0 = b * 512
                for i in range(2):
                    R.op("dve", lambda e, i=i: e.tensor_tensor(out=KT[i][64:96, t0:t0 + 512], in0=kr_t[64:96, 0, :],
                                                               in1=kr_t[64:96, 1, :], op=ALU.add),
                         r=["krt0", "krt1"], w=["KTr%d" % i])

            def grp_qk(b, oc):
                hT, khT = hTs[b % 2], "hT%d" % (b % 2)
                qs, kqs = qk_st[0], "qkst0"
                t0 = b * 512
                pm, kpm = psMr.next()
                col = C_DQ + oc * 128
                for kc in range(8):
                    self.mm(pm[:, :], w_in[:, kc, col:col + 128], hT[:, kc, :], start=(kc == 0), stop=(kc == 7),
                            r=[khT, kw_in], w=[kpm], inc=(kc == 7))
                self.copy(evac_eng.next(), qs[:, oc, :], pm[:, :], r=[kpm], w=[kqs])
                if oc == 3:
                    R.dma("pool", s["qdT"][:, t0:t0 + 512].rearrange("(c p) t -> p c t", p=128), qs[:, 0:4, :],
                          r=[kqs], w=[("qdT", s["name"])])
                if oc == 7:
                    R.dma("pool", s["kdT"][:, t0:t0 + 512].rearrange("(c p) t -> p c t", p=128), qs[:, 4:8, :],
                          r=[kqs], w=[("kdT", s["name"])])

            def grp_v(b, j):
                hT, khT = hTs[b % 2], "hT%d" % (b % 2)
                vs, kvs = v_st[0], "vst0"
                t0 = b * 512
                pm, kpm = psMr.next()
                for kc in range(8):
                    self.mm(pm[:, :], hT[:, kc, j * 128:(j + 1) * 128], w_in[:, kc, C_DV:C_DV + 512],
                            start=(kc == 0), stop=(kc == 7), r=[khT, kw_in], w=[kpm], inc=(kc == 7))
                self.copy(evac_eng.next(), vs[:, j, :], pm[:, :], r=[kpm], w=[kvs])
                if j == 3:
                    R.dma("pool", s["vd"][t0:t0 + 512, :].rearrange("(j p) c -> p j c", p=128), vs[:, :, :],
                          r=[kvs], w=[("vd", s["name"])])

            item_load(0)
            if NB > 1:
                item_load(1)
            item_norm(0)
            item_tr(0, 0)
            item_tr(0, 1)
            for b in range(NB):
                groups = [lambda oc=oc: grp_qk(b, oc) for oc in range(8)] + [lambda j=j: grp_v(b, j) for j in range(4)]
                pls = {}
                chain = []
                chain.append(lambda: pls.__setitem__(0, item_lat(b, 0)))
                chain.append(lambda: pls.__setitem__(1, item_lat(b, 1)))
                if b + 1 < NB:
                    chain.append(lambda: item_norm_act(b + 1))
                chain.append(lambda: item_latnorm(b, 0, pls[0]))
                chain.append(lambda: (item_latnorm(b, 1, pls[1]), pls.__setitem__(2, item_lat(b, 2))))
                chain.append(lambda: pls.__setitem__(3, item_lat(b, 3)))
                chain.append(lambda: item_krope_mm(b))
                chain.append(lambda: item_latnorm(b, 2, pls[2]))
                chain.append(lambda: (item_latnorm(b, 3, pls[3]), item_krope_fin(b)))
                if b + 1 < NB:
                    chain.append(lambda: item_norm(b + 1, act=False))
                chain.append(lambda: item_lattr(b))
                if b + 1 < NB:
                    chain.append(lambda: item_tr(b + 1, 0))
                if b + 1 < NB:
                    chain.append(lambda: item_tr(b + 1, 1))
                if b + 2 < NB:
                    chain.append(lambda: item_load(b + 2))
                gi = 0
                for ci, c in enumerate(chain):
                    c()
                    if gi < len(groups):
                        groups[gi]()
                        gi += 1
                while gi < len(groups):
                    groups[gi]()
                    gi += 1
            R.barrier()

    def mla(self, l, s, nq, st_outer, qlatT, kvlatT, vall, KT, QT, w_uq, w_uqs, w_ukvk, ropering):
        nc, R, W = self.nc, self.R, self.W
        S = s["S"]
        NT = S // 128
        NB = S // 512
        NQB = nq // 512
        scale = 1.0 / math.sqrt(DQK)
        with ExitStack() as st:
            psS = [self.ps(st, [128, 1024], F32, "psS%d" % i) for i in range(3)]
            psO = self.ps(st, [128, 512], F32, "psO")
            psB = self.ps(st, [128, 512], F32, "psB")
            kpo, kpb = "psO", "psB"
            pT = [self.sb(st, [128, 1024], BF16, "pT%d" % i) for i in range(4)]
            oT = self.sb(st, [128, 512], F32, "oT")
            a_sb = [self.sb(st, [128, 4, 64], F32, "a_sb%d" % i) for i in range(2)]
            rcp = self.sb(st, [128, 4], F32, "rcp")
            qr_t = self.sb(st, [128, 2, 256], F32, "qrt")
            psSr = Ring([(psS[i], "psS%d" % i) for i in range(3)])
            pTr = Ring([(pT[i], "pT%d" % i) for i in range(4)])

            w_uq = self.sb(st, [128, 2, H * DQK], BF16, "w_uq")
            w_uqs = self.sb(st, [128, 2, H * DQK], BF16, "w_uqs")
            w_ukvk = self.sb(st, [128, H, 64], BF16, "w_ukvk")
            w_ukvv = self.sb(st, [128, H, 64], BF16, "w_ukvv")
            gc = self.gcols
            R.op("pool", lambda e: e.memset(w_uqs[:, :, :], 0.0), w=["w_uqs"])
            for c in range(2):
                self.load_w(w_uq[:, c, :], W["w_uq"][l, c * 128:(c + 1) * 128, :], gc[:, l, 24 + c:25 + c], "w_uq")
            w_uq4 = w_uq[:, :, :].rearrange("p c (h d) -> p c h d", d=DQK)
            w_uqs4 = w_uqs[:, :, :].rearrange("p c (h d) -> p c h d", d=DQK)
            for c in range(2):
                R.op("pool", lambda e, c=c: e.tensor_copy(w_uqs4[:, c, :, 64:80], w_uq4[:, c, :, 80:96]),
                     r=["w_uq"], w=["w_uqs"])
                R.op("pool", lambda e, c=c: e.tensor_copy(w_uqs4[:, c, :, 80:96], w_uq4[:, c, :, 64:80]),
                     r=["w_uq"], w=["w_uqs"])
            stg, kst = self.wstage.next()
            R.dma("sp", stg[:, 0:1024], W["w_ukv"][l, :, :], w=[kst])
            stg3 = stg[:, 0:1024].rearrange("p (h d) -> p h d", d=128)
            R.op("pool", lambda e: e.tensor_scalar(out=w_ukvk[:, :, :], in0=stg3[:, :, 0:64], scalar1=gc[:, l, 26:27],
                                                   scalar2=1.0, op0=ALU.mult, op1=ALU.mult),
                 r=[kst, "gcols"], w=["w_ukvk"])
            R.op("pool", lambda e: e.tensor_scalar(out=w_ukvv[:, :, :], in0=stg3[:, :, 64:128], scalar1=gc[:, l, 26:27],
                                                   scalar2=1.0, op0=ALU.mult, op1=ALU.mult),
                 r=[kst, "gcols"], w=["w_ukvv"])

            vall = self.sb(st, [128, NT, 4, 65], BF16, "vall")
            QT = [self.sb(st, [128, S], BF16, "QT%d" % i) for i in range(2)]
            R.op("pool", lambda e: e.memset(vall[:, :, :, 64:65], 1.0), w=["vall"])

            bring = Ring([(psB, "psB"), (psS[0][:, 0:512], "psS0"), (psS[1][:, 0:512], "psS1"),
                          (psS[2][:, 0:512], "psS2")])
            bevac = Ring(["dve", "act"])
            qrts = [qr_t, self.sb(st, [128, 2, 256], F32, "qrtb")]
            qrtr = Ring([(qrts[0], "qrtA"), (qrts[1], "qrtB")])

            def build_v(g):
                for t in range(NT):
                    pt, kpt = bring.next()
                    self.mm(pt[:, 0:256], kvlatT[:, t * 128:(t + 1) * 128],
                            w_ukvv[:, 4 * g:4 * g + 4, :].rearrange("p h d -> p (h d)"),
                            start=True, stop=True, r=["kvlatT", "w_ukvv"], w=[kpt])
                    self.copy(bevac.next(), vall[:, t, :, 0:64], pt[:, 0:256].rearrange("p (h d) -> p h d", d=64),
                              r=[kpt], w=["vall"])

            def head_steps(h, burst=False):
                kt_, kkt = KT[h % 2], "KTn%d" % (h % 2)
                qt_, kqt = QT[h % 2], "QT%d" % (h % 2)
                steps = []

                def kstep(b):
                    pt, kpt = bring.next() if burst else (psB, kpb)
                    self.mm(pt[0:64, :], w_ukvk[:, h, :], kvlatT[:, b * 512:(b + 1) * 512], start=True, stop=True,
                            r=["kvlatT", "w_ukvk"], w=[kpt])
                    self.copy(bevac.next() if burst else "dve", kt_[0:64, b * 512:(b + 1) * 512], pt[0:64, :],
                              r=[kpt], w=[kkt])

                def qstep(b2, state):
                    t0 = b2 * 256
                    if b2 % 2 == 0:
                        rt, krt = ropering.next()
                        R.dma("sp", rt[64:96, :, :], s["rope"][:, :, t0:t0 + 512].rearrange("a r t -> r a t"), w=[krt])
                        state["rt"] = (rt, krt)
                    rt, krt = state["rt"]
                    ro = (b2 % 2) * 256
                    pt, kpt = bring.next() if burst else (psB, kpb)
                    qr, kqr = qrtr.next()
                    pa = pt[0:96, 0:256]
                    pb = pt[0:96, 256:512]
                    for c in range(2):
                        self.mm(pa, w_uq[:, c, h * DQK:(h + 1) * DQK], qlatT[:, c, t0:t0 + 256],
                                start=(c == 0), stop=(c == 1), r=["qlatT", "w_uq"], w=[kpt], inc=False)
                    for c in range(2):
                        self.mm(pb, w_uqs[:, c, h * DQK:(h + 1) * DQK], qlatT[:, c, t0:t0 + 256],
                                start=(c == 0), stop=(c == 1), r=["qlatT", "w_uqs"], w=[kpt], inc=(c == 1))
                    self.copy("act" if burst else "dve", qt_[0:64, t0:t0 + 256], pt[0:64, 0:256], r=[kpt], w=[kqt])
                    R.op("dve", lambda e: e.tensor_tensor(out=qr[64:96, 0, :], in0=pt[64:96, 0:256],
                                                          in1=rt[64:96, 0, ro:ro + 256], op=ALU.mult),
                         r=[kpt, krt], w=[kqr + "0"])
                    R.op("dve", lambda e: e.tensor_tensor(out=qr[64:96, 1, :], in0=pt[64:96, 256:512],
                                                          in1=rt[64:96, 1, ro:ro + 256], op=ALU.mult),
                         r=[kpt, krt], w=[kqr + "1"])
                    R.op("dve", lambda e: e.tensor_tensor(out=qt_[64:96, t0:t0 + 256], in0=qr[64:96, 0, :],
                                                          in1=qr[64:96, 1, :], op=ALU.add),
                         r=[kqr + "0", kqr + "1"], w=[kqt])

                state = {}
                for b in range(NB):
                    steps.append(lambda b=b: kstep(b))
                for b2 in range(2 * NQB):
                    steps.append(lambda b2=b2: qstep(b2, state))
                return steps

            NP = NT // 2
            n_iter = NQB * NP
            for st_ in head_steps(0, burst=True):
                st_()
            for h in range(H):
                if h % 4 == 0:
                    build_v(h // 4)
                burst_next = (S <= 2048)
                nxt = head_steps(h + 1) if (h + 1 < H and not burst_next) else []
                every = max(1, n_iter // (len(nxt) + 1)) if nxt else 0
                kt_ = KT[h % 2]
                qt_ = QT[h % 2]
                kk = ["KTn%d" % (h % 2), "KTr%d" % (h % 2)]
                kq = "QT%d" % (h % 2)

                def issue_s(qb, p):
                    q0 = qb * 512
                    ps_, kps = psSr.next()
                    for i in range(2):
                        k0 = (2 * p + i) * 128
                        self.mm(ps_[:, i * 512:(i + 1) * 512], kt_[0:96, k0:k0 + 128], qt_[0:96, q0:q0 + 512],
                                start=True, stop=True, r=kk + [kq], w=[kps], inc=(i == 1))
                    pt_, kpt = pTr.next()
                    R.op("act", lambda e: e.activation(out=pt_[:, :], in_=ps_[:, :], func=AF.Exp, scale=scale),
                         r=[kps], w=[kpt])
                    return (qb, p, pt_, kpt)

                def issue_o(item):
                    qb, p, pt_, kpt = item
                    for i in range(2):
                        t = 2 * p + i
                        self.mm(psO[0:65, :], vall[:, t, h % 4, :], pt_[:, i * 512:(i + 1) * 512],
                                start=(t == 0), stop=(t == NT - 1), r=["vall", kpt], w=[kpo],
                                inc=(i == 1))
                    if p == NP - 1:
                        epilogue(qb)

                def epilogue(qb):
                    q0 = qb * 512
                    R.op("dve", lambda e: e.tensor_copy(oT[0:65, :], psO[0:65, :]), r=[kpo], w=["oT"])
                    for j in range(4):
                        self.tr(psB[:, j * 65:(j + 1) * 65], oT[0:65, j * 128:(j + 1) * 128], self.ident_f[0:65, 0:65],
                                r=["oT", "const"], w=[kpb], inc=(j == 3))
                    pb3 = psB[:, 0:260].rearrange("p (j d) -> p j d", d=65)
                    R.op("dve", lambda e: e.reciprocal(rcp[:, :], pb3[:, :, 64]), r=[kpb], w=["rcp"])
                    asb, kasb = a_sb[qb % 2], "a_sb%d" % (qb % 2)
                    R.op("dve", lambda e: e.tensor_tensor(
                        out=asb[:, :, :], in0=pb3[:, :, 0:64],
                        in1=rcp[:, :].unsqueeze(2).broadcast_to([128, 4, 64]), op=ALU.mult),
                        r=[kpb, "rcp"], w=[kasb])
                    R.dma("pool", s["a"][q0:q0 + 512, h * 64:(h + 1) * 64].rearrange("(j p) d -> p j d", p=128),
                          asb[:, :, :], r=[kasb], w=[("a", s["name"])])

                pend = []
                it = 0
                for qb in range(NQB):
                    for p in range(NP):
                        pend.append(issue_s(qb, p))
                        if len(pend) > 2:
                            issue_o(pend.pop(0))
                        it += 1
                        if nxt and it % every == 0:
                            nxt.pop(0)()
                while pend:
                    issue_o(pend.pop(0))
                while nxt:
                    nxt.pop(0)()
                if burst_next and h + 1 < H:
                    for st_ in head_steps(h + 1, burst=True):
                        st_()

    def dilated(self, l, s, nq):
        nc, R = self.nc, self.R
        S = s["S"]
        NT = S // 128
        with ExitStack() as st:
            acc = self.sb(st, [128, 2, S], F32, "dacc")
            qds = [self.sb(st, [128, S], BF16, "qd%d" % i) for i in range(1)]
            kds = [self.sb(st, [128, S], BF16, "kd%d" % i) for i in range(1)]
            vres = self.sb(st, [128, 3, NT, 2, 65], BF16, "vres")
            biasT = self.sb(st, [128, 6, 384], BF16, "biasT")
            pexp = [self.sb(st, [128, 2, 384], BF16, "pexp%d" % i) for i in range(3)]
            b_sb = [self.sb(st, [128, 4, 128], F32, "b_sb%d" % i) for i in range(5)]
            rcps = [self.sb(st, [128, 4], F32, "rcpd%d" % i) for i in range(2)]
            psS = [self.ps(st, [128, 2, 512], F32, "dpsS%d" % i) for i in range(2)]
            psO = [self.ps(st, [128, 512], F32, "dpsO%d" % i) for i in range(2)]
            psB = [self.ps(st, [128, 512], F32, "dpsB%d" % i) for i in range(2)]
            psSr = Ring([(psS[i], "dpsS%d" % i) for i in range(2)])
            psOr = Ring([(psO[i], "dpsO%d" % i) for i in range(2)])
            psBr = Ring([(psB[i], "dpsB%d" % i) for i in range(2)])
            per = Ring([(pexp[i], "pexp%d" % i) for i in range(3)])
            finr = Ring([(psB[0], "dpsB0"), (psB[1], "dpsB1"), (psO[0], "dpsO0"), (psO[1], "dpsO1"),
                         (psS[0][:, 0, :], "dpsS0"), (psS[1][:, 0, :], "dpsS1")])
            R.op("pool", lambda e: e.memset(vres[:, :, :, :, 64:65], 1.0), w=["vres0", "vres1", "vres2"])
            for hp in range(4):
                for hh_ in range(2):
                    for bi, d in enumerate(DIL):
                        R.op("dve", lambda e, hh_=hh_, bi=bi, d=d: e.tensor_scalar(
                            out=biasT[:, hh_ * 3 + bi, :], in0=self.bmask[:, :],
                            scalar1=8.0 * d * 2.0 ** (-(hp * 2 + hh_ + 1)),
                            scalar2=None, op0=ALU.mult), r=["const"], w=["biasT"])
                qd, kd = qds[0], kds[0]
                kqd, kkd = "qd0", "kd0"
                R.dma("sp", qd[:, :], s["qdT"][hp * 128:(hp + 1) * 128, :], r=[("qdT", s["name"])], w=[kqd])
                R.dma("sp", kd[:, :], s["kdT"][hp * 128:(hp + 1) * 128, :], r=[("kdT", s["name"])], w=[kkd])
                for bi, d in enumerate(DIL):
                    L = S // d
                    NU = L // 128
                    for r_ in range(d):
                        src = s["vd"][:, hp * 128:(hp + 1) * 128].rearrange("(u i dd) (hh c) -> dd i u hh c",
                                                                            i=128, dd=d, hh=2)[r_]
                        u0 = 0
                        while u0 < NU:
                            u1 = min(NU, u0 + 16)
                            for hh_ in range(2):
                                R.dma("sp", vres[:, bi, r_ * NU + u0:r_ * NU + u1, hh_, 0:64], src[:, u0:u1, hh_, :],
                                      r=[("vd", s["name"])], w=["vres%d" % bi])
                            u0 = u1
                its = []
                for bi, d in enumerate(DIL):
                    for r_ in range(d):
                        for jb in range((nq // d) // 128):
                            its.append((bi, d, r_, jb))

                def issue_s(it):
                    bi, d, r_, jb = it
                    NU = (S // d) // 128
                    tiles = [u for u in (jb - 1, jb, jb + 1) if 0 <= u < NU]
                    nt = len(tiles)
                    d0 = tiles[0] - (jb - 1)
                    qcols = slice(r_ + d * 128 * jb, r_ + d * 128 * jb + d * 127 + 1, d)
                    ps_, kps = psSr.next()
                    for hh in range(2):
                        h = hp * 2 + hh
                        self.mm(ps_[:, hh, 0:nt * 128], self.ident_b[:, :],
                                biasT[:, hh * 3 + bi, d0 * 128:(d0 + nt) * 128], start=True, stop=False,
                                r=["const", "biasT"], w=[kps], inc=False)
                    for ti, u in enumerate(tiles):
                        kcols = slice(r_ + d * 128 * u, r_ + d * 128 * u + d * 127 + 1, d)
                        for hh in range(2):
                            self.mm(ps_[:, hh, ti * 128:(ti + 1) * 128], kd[hh * 64:(hh + 1) * 64, kcols],
                                    qd[hh * 64:(hh + 1) * 64, qcols], start=False, stop=(ti == nt - 1),
                                    r=[kkd, kqd], w=[kps], inc=(ti == nt - 1 and hh == 1))
                    pe_, kpe = per.next()
                    R.op("act", lambda e: e.activation(out=pe_[:, :, 0:nt * 128], in_=ps_[:, :, 0:nt * 128],
                                                       func=AF.Exp, scale=0.125), r=[kps], w=[kpe])
                    return (it, tiles, pe_, kpe, qcols)

                def issue_pv(ctx):
                    (bi, d, r_, jb), tiles, pe_, kpe, qcols = ctx
                    NU = (S // d) // 128
                    nt = len(tiles)
                    po, kpo = psOr.next()
                    for hh in range(2):
                        for ti, u in enumerate(tiles):
                            self.mm(po[0:65, hh * 128:(hh + 1) * 128], vres[:, bi, r_ * NU + u, hh, :],
                                    pe_[:, hh, ti * 128:(ti + 1) * 128], start=(ti == 0), stop=(ti == nt - 1),
                                    r=["vres%d" % bi, kpe], w=[kpo], inc=(ti == nt - 1 and hh == 1))
                    accv = acc[0:65, :, qcols]
                    pov = po[0:65, 0:256].rearrange("p (h q) -> p h q", h=2)
                    if bi == 0:
                        R.op("dve", lambda e: e.tensor_copy(accv, pov), r=[kpo], w=["acc"])
                    else:
                        R.op("dve", lambda e: e.tensor_tensor(out=accv, in0=accv, in1=pov, op=ALU.add),
                             r=[kpo, "acc"], w=["acc"])

                pend = []
                for it in its:
                    pend.append(issue_s(it))
                    if len(pend) > 1:
                        issue_pv(pend.pop(0))
                while pend:
                    issue_pv(pend.pop(0))
                for qb in range(nq // 512):
                    q0 = qb * 512
                    bsb, kbsb = b_sb[qb % len(b_sb)], "b_sb%d" % (qb % len(b_sb))
                    for hh in range(2):
                        pb, kpb = finr.next()
                        for j in range(4):
                            self.tr(pb[:, j * 65:(j + 1) * 65], acc[0:65, hh, q0 + j * 128:q0 + (j + 1) * 128],
                                    self.ident_f[0:65, 0:65], r=["acc", "const"], w=[kpb], inc=(j == 3))
                        pb3 = pb[:, 0:260].rearrange("p (j d) -> p j d", d=65)
                        rc, krc = rcps[hh], "rcpd%d" % hh
                        R.op("dve", lambda e, pb3=pb3, rc=rc: e.reciprocal(rc[:, :], pb3[:, :, 64]), r=[kpb], w=[krc])
                        R.op("dve", lambda e, pb3=pb3, rc=rc, hh=hh: e.tensor_tensor(
                            out=bsb[:, :, hh * 64:(hh + 1) * 64], in0=pb3[:, :, 0:64],
                            in1=rc[:, :].unsqueeze(2).broadcast_to([128, 4, 64]), op=ALU.mult),
                            r=[kpb, krc], w=[kbsb])
                    R.dma("pool", s["b"][q0:q0 + 512, hp * 128:(hp + 1) * 128].rearrange("(j p) d -> p j d", p=128),
                          bsb[:, :, :], r=[kbsb], w=[("b", s["name"])])
            R.barrier()

    def phase_c(self, l, w_up):
        nc, R, W = self.nc, self.R, self.W
        gc = self.gcols
        last = (l == self.depth - 1)
        with ExitStack() as st:
            w_out = self.sb(st, [128, 8, D], BF16, "w_out")
            grep = self.sb(st, [128, D], F32, "gpm")
            ab = [self.sb(st, [128, 4, D], F32, "ab%d" % i) for i in range(3)]
            xin = [self.sb(st, [128, 4, D], F32, "xc%d" % i) for i in range(3)]
            abn = self.sb(st, [128, 4, D], BF16, "abn")
            junk = self.sb(st, [128, D], BF16, "junkc")
            mTs = [self.sb(st, [128, 8, 512], BF16, "mT%d" % i) for i in range(2)]
            stat = self.sb(st, [128, 3, 16], F32, "statc")
            tmp = self.sb(st, [128, D], F32, "tmpc")
            psT = [self.ps(st, [128, 1024], BF16, "cpsT%d" % i) for i in range(2)]
            psY = [self.ps(st, [128, 1024], F32, "cpsY%d" % i) for i in range(2)]
            psTr = Ring([(psT[0], "cpsT0"), (psT[1], "cpsT1")])
            for kc in range(8):
                g = gc[:, l, 16 + kc:17 + kc]
                self.load_w(w_out[:, kc, :], W["w_out"][l, kc * 128:(kc + 1) * 128, :], g, "w_out")
            R.dma("sp", grep[:, :], W["g_post_mix"][l, :].partition_broadcast(128), w=["grep"])
            evac_eng = Ring(["act", "dve"])
            blocks = []
            for s in self.seqs:
                nq = s["nq_last"] if last else s["S"]
                xsrc = s["x"] if l == 0 else s["xb"]
                for b in range(nq // 512):
                    blocks.append((s, xsrc, b * 512))
            NBk = len(blocks)

            def c_load(i):
                s, xsrc, t0 = blocks[i]
                ab_, kab = ab[i % 3], "ab%d" % (i % 3)
                x_, kx = xin[i % 3], "xc%d" % (i % 3)
                R.dma("sp", ab_[:, :, 0:512], s["a"][t0:t0 + 512, :].rearrange("(j p) d -> p j d", p=128),
                      r=[("a", s["name"])], w=[kab])
                R.dma("sp", ab_[:, :, 512:1024], s["b"][t0:t0 + 512, :].rearrange("(j p) d -> p j d", p=128),
                      r=[("b", s["name"])], w=[kab])
                R.dma("sp", x_[:, :, :], xsrc[t0:t0 + 512, :].rearrange("(j p) d -> p j d", p=128),
                      r=[("xb", s["name"])], w=[kx])

            def c_norm(i):
                ab_, kab = ab[i % 3], "ab%d" % (i % 3)
                for j in range(4):
                    for half in range(2):
                        R.op("act", lambda e, j=j, half=half: e.activation(
                            out=junk[:, 0:512], in_=ab_[:, j, half * 512:(half + 1) * 512], func=AF.Square,
                            accum_out=stat[:, 0, half * 4 + j:half * 4 + j + 1]), r=[kab], w=["ss_c"])
                self.rstd(stat[:, 0, 0:8], stat[:, 1, 0:8], stat[:, 2, 0:8], 512, "ss_c", "ms_c", "rs_c")
                for j in range(4):
                    for half in range(2):
                        R.op("dve", lambda e, j=j, half=half: e.tensor_scalar(
                            out=abn[:, j, half * 512:(half + 1) * 512], in0=ab_[:, j, half * 512:(half + 1) * 512],
                            scalar1=stat[:, 2, half * 4 + j:half * 4 + j + 1], scalar2=None, op0=ALU.mult),
                            r=[kab, "rs_c"], w=["abn"])

            def c_tr(i, half_):
                mT, kmT = mTs[i % 2], "mT%d" % (i % 2)
                for c2 in (2 * half_, 2 * half_ + 1):
                    pt, kpt = psTr.next()
                    for cc in range(2):
                        c = c2 * 2 + cc
                        for j in range(4):
                            self.tr(pt[:, cc * 512 + j * 128: cc * 512 + (j + 1) * 128],
                                    abn[:, j, c * 128:(c + 1) * 128], self.ident_b[:, :],
                                    r=["abn", "const"], w=[kpt], inc=(cc == 1 and j == 3))
                    self.copy(evac_eng.next(), mT[:, c2 * 2:c2 * 2 + 2, :],
                              pt[:, :].rearrange("p (c t) -> p c t", c=2), r=[kpt], w=[kmT])

            def c_mm(i, j):
                mT, kmT = mTs[i % 2], "mT%d" % (i % 2)
                x_, kx = xin[i % 3], "xc%d" % (i % 3)
                py, kpy = psY[j % 2], "cpsY%d" % (j % 2)
                for half in range(2):
                    for kc in range(8):
                        self.mm(py[:, half * 512:(half + 1) * 512], mT[:, kc, j * 128:(j + 1) * 128],
                                w_out[:, kc, half * 512:(half + 1) * 512], start=(kc == 0), stop=(kc == 7),
                                r=[kmT, "w_out"], w=[kpy], inc=(kc == 7 and half == 1))
                R.op("act", lambda e: e.activation(out=junk[:, :], in_=py[:, :], func=AF.Square,
                                                   accum_out=stat[:, 0, 8 + j:9 + j]), r=[kpy], w=["ss_y"])
                self.rstd(stat[:, 0, 8 + j:9 + j], stat[:, 1, 8 + j:9 + j], stat[:, 2, 8 + j:9 + j], D,
                          "ss_y", "ms_y", "rs_y")
                R.op("dve", lambda e: e.scalar_tensor_tensor(
                    out=tmp[:, :], in0=py[:, :], scalar=stat[:, 2, 8 + j:9 + j], in1=grep[:, :],
                    op0=ALU.mult, op1=ALU.mult), r=[kpy, "rs_y", "grep"], w=["tmpc"])
                R.op("dve", lambda e: e.tensor_tensor(out=x_[:, j, :], in0=x_[:, j, :], in1=tmp[:, :],
                                                      op=ALU.add), r=["tmpc", kx], w=[kx])

            def c_store(i):
                s, xsrc, t0 = blocks[i]
                x_, kx = xin[i % 3], "xc%d" % (i % 3)
                R.dma("pool", s["xa"][t0:t0 + 512, :].rearrange("(j p) d -> p j d", p=128), x_[:, :, :],
                      r=[kx], w=[("xa", s["name"])])

            wup_chunks = [(kc, c0) for kc in range(8) for c0 in range(0, DFF, 1024)]
            per_blk = -(-len(wup_chunks) // max(1, NBk - 1))

            def c_wup(n):
                for _ in range(n):
                    if not wup_chunks:
                        return
                    kc, c0 = wup_chunks.pop(0)
                    self.load_w(w_up[:, kc, c0:c0 + 1024], W["w_up"][l, kc * 128:(kc + 1) * 128, c0:c0 + 1024],
                                gc[:, l, 8 + kc:9 + kc], "w_up")

            c_load(0)
            if NBk > 1:
                c_load(1)
            if NBk > 2:
                c_load(2)
            c_norm(0)
            c_tr(0, 0)
            c_tr(0, 1)
            for i in range(NBk):
                c_mm(i, 0)
                if i + 1 < NBk:
                    c_norm(i + 1)
                c_mm(i, 1)
                if i + 1 < NBk:
                    c_tr(i + 1, 0)
                c_mm(i, 2)
                if i + 1 < NBk:
                    c_tr(i + 1, 1)
                c_mm(i, 3)
                c_store(i)
                if i + 3 < NBk:
                    c_load(i + 3)
            if w_up is not None:
                c_wup(len(wup_chunks))
            R.barrier()

    def phase_d(self, l, last, w_up, load_up=False):
        nc, R, W = self.nc, self.R, self.W
        gc = self.gcols
        TB = 256
        NJ = TB // 128
        with ExitStack() as st:
            w_dn = self.sb(st, [128, 32, D], BF16, "w_dn")
            grep = self.sb(st, [128, D], F32, "gpo")
            xin = [self.sb(st, [128, NJ, D], F32, "xd%d" % i) for i in range(2)]
            xn = self.sb(st, [128, NJ, D], BF16, "xnd")
            junk = self.sb(st, [128, D], BF16, "junkd")
            hTs = [self.sb(st, [128, 8, TB], BF16, "hTd%d" % i) for i in range(2)]
            uT = self.sb(st, [128, 32, TB], BF16, "uT")
            rl = [self.sb(st, [128, TB], F32, "rl%d" % i) for i in range(2)]
            stat = self.sb(st, [128, 3, 8], F32, "statd")
            tmp = self.sb(st, [128, D], F32, "tmpd")
            psT = [self.ps(st, [128, 1024], BF16, "dpT%d" % i) for i in range(2)]
            psU = [self.ps(st, [128, 512], F32, "dpU%d" % i) for i in range(2)]
            psY = [self.ps(st, [128, 1024], F32, "dpY%d" % i) for i in range(2)]
            psTr = Ring([(psT[0], "dpT0"), (psT[1], "dpT1")])
            psUr = Ring([(psU[0], "dpU0"), (psU[1], "dpU1")])
            rlr = Ring([(rl[0], "rl0"), (rl[1], "rl1")])
            ws2 = [self.sb(st, [128, 1024], F32, "wstD%d" % i) for i in range(3)]
            old_ring = self.wstage
            self.wstage = Ring(old_ring.items + [(ws2[i], "wstD%d" % i) for i in range(3)])
            dblocks = []
            for s in self.seqs:
                nq_ = s["nq_last"] if last else s["S"]
                for b in range(nq_ // TB):
                    dblocks.append((s, b * TB))
            preloaded = set()
            for i in range(min(2, len(dblocks))):
                s_, t0_ = dblocks[i]
                R.dma("sp", xin[i % 2][:, :, :], s_["xa"][t0_:t0_ + TB, :].rearrange("(j p) d -> p j d", p=128),
                      r=[("xa", s_["name"])], w=["xd%d" % (i % 2)])
                preloaded.add(i)
            if load_up:
                for kc in range(8):
                    self.load_w(w_up[:, kc, :], W["w_up"][l, kc * 128:(kc + 1) * 128, :], gc[:, l, 8 + kc:9 + kc], "w_up")
            for oc in range(32):
                self.load_w(w_dn[:, oc, :], W["w_down"][l, oc * 128:(oc + 1) * 128, :], None, "w_dn")
            self.wstage = old_ring
            R.dma("sp", grep[:, :], W["g_post_mlp"][l, :].partition_broadcast(128), w=["grepd"])
            evac_eng = Ring(["act", "dve"])
            ND = len(dblocks)

            def d_load(i):
                if i in preloaded or i >= ND:
                    return
                s_, t0_ = dblocks[i]
                R.dma("sp", xin[i % 2][:, :, :], s_["xa"][t0_:t0_ + TB, :].rearrange("(j p) d -> p j d", p=128),
                      r=[("xa", s_["name"])], w=["xd%d" % (i % 2)])

            def d_norm(i):
                x_, kx = xin[i % 2], "xd%d" % (i % 2)
                for j in range(NJ):
                    R.op("act", lambda e, j=j: e.activation(out=junk[:, :], in_=x_[:, j, :], func=AF.Square,
                                                            accum_out=stat[:, 0, j:j + 1]), r=[kx], w=["ss_d"])
                self.rstd(stat[:, 0, 0:NJ], stat[:, 1, 0:NJ], stat[:, 2, 0:NJ], D, "ss_d", "ms_d", "rs_d")
                for j in range(NJ):
                    R.op("dve", lambda e, j=j: e.tensor_scalar(out=xn[:, j, :], in0=x_[:, j, :],
                                                               scalar1=stat[:, 2, j:j + 1], scalar2=None,
                                                               op0=ALU.mult), r=[kx, "rs_d"], w=["xnd"])

            def d_tr(i):
                hT, khT = hTs[i % 2], "hTd%d" % (i % 2)
                for c4 in range(2):
                    pt, kpt = psTr.next()
                    for cc in range(4):
                        c = c4 * 4 + cc
                        for j in range(NJ):
                            self.tr(pt[:, cc * TB + j * 128: cc * TB + (j + 1) * 128],
                                    xn[:, j, c * 128:(c + 1) * 128], self.ident_b[:, :],
                                    r=["xnd", "const"], w=[kpt], inc=(cc == 3 and j == NJ - 1))
                    self.copy(evac_eng.next(), hT[:, c4 * 4:c4 * 4 + 4, :],
                              pt[:, :].rearrange("p (c t) -> p c t", c=4), r=[kpt], w=[khT])

            d_norm(0)
            d_tr(0)
            for i in range(ND):
                s, t0 = dblocks[i]
                dst = s["y"] if last else s["xb"]
                x_, kx = xin[i % 2], "xd%d" % (i % 2)
                hT, khT = hTs[i % 2], "hTd%d" % (i % 2)
                d_load(i + 1)
                for oc in range(32):
                    pu, kpu = psUr.next()
                    for kc in range(8):
                        self.mm(pu[:, 0:TB], w_up[:, kc, oc * 128:(oc + 1) * 128], hT[:, kc, :],
                                start=(kc == 0), stop=(kc == 7), r=[khT, "w_up"], w=[kpu], inc=(kc == 7))
                    r_, krl = rlr.next()
                    R.op("act", lambda e: e.activation(out=r_[:, :], in_=pu[:, 0:TB], func=AF.Relu), r=[kpu], w=[krl])
                    R.op("dve", lambda e: e.tensor_tensor(out=uT[:, oc, :], in0=r_[:, :], in1=r_[:, :], op=ALU.mult),
                         r=[krl], w=["uT"])
                    if oc == 12 and i + 1 < ND:
                        d_norm(i + 1)
                if i + 1 < ND:
                    d_tr(i + 1)
                for j in range(NJ):
                    py, kpy = psY[j % 2], "dpY%d" % (j % 2)
                    for half in range(2):
                        for oc in range(32):
                            self.mm(py[:, half * 512:(half + 1) * 512], uT[:, oc, j * 128:(j + 1) * 128],
                                    w_dn[:, oc, half * 512:(half + 1) * 512], start=(oc == 0), stop=(oc == 31),
                                    r=["uT", "w_dn"], w=[kpy], inc=(oc == 31 and half == 1))
                    R.op("act", lambda e: e.activation(out=junk[:, :], in_=py[:, :], func=AF.Square,
                                                       accum_out=stat[:, 0, 4 + j:5 + j]), r=[kpy], w=["ss_y"])
                    self.rstd(stat[:, 0, 4 + j:5 + j], stat[:, 1, 4 + j:5 + j], stat[:, 2, 4 + j:5 + j], D,
                              "ss_y", "ms_y", "rs_y")
                    R.op("dve", lambda e: e.scalar_tensor_tensor(
                        out=tmp[:, :], in0=py[:, :], scalar=stat[:, 2, 4 + j:5 + j], in1=grep[:, :],
                        op0=ALU.mult, op1=ALU.mult), r=[kpy, "rs_y", "grepd"], w=["tmpd"])
                    R.op("dve", lambda e: e.tensor_tensor(out=x_[:, j, :], in0=x_[:, j, :], in1=tmp[:, :],
                                                          op=ALU.add), r=["tmpd", kx], w=[kx])
                R.dma("pool", dst[t0:t0 + TB, :].rearrange("(j p) d -> p j d", p=128), x_[:, :, :],
                      r=[kx], w=[("xb", s["name"])])
            R.barrier()


def rope_tables(S, reverse=False):
    half = ROPE // 2
    inv_freq = (np.float32(10000.0) ** (-np.arange(half, dtype=np.float32) / np.float32(half))).astype(np.float32)
    pos = np.arange(S, dtype=np.float32)
    if reverse:
        pos = pos[::-1].copy()
    ang = (pos[:, None] * inv_freq[None, :]).astype(np.float32)
    c = np.cos(ang).astype(np.float32).T
    sn = np.sin(ang).astype(np.float32).T
    cosT = np.concatenate([c, c], axis=0)
    sinT = np.concatenate([-sn, sn], axis=0)
    return np.ascontiguousarray(np.stack([cosT, sinT], axis=0))


def band_mask():
    k = np.arange(128)[:, None, None]
    dl = np.arange(3)[None, :, None]
    q = np.arange(128)[None, None, :]
    rel = np.abs(128 * (dl - 1) + k - q)
    m = np.where(rel <= 64, -rel.astype(np.float32), np.float32(-1e30)).astype(np.float32)
    return np.ascontiguousarray(m.reshape(128, 384))


_CACHE = {}


def get_prog(seq_cfg, depth, debug=False):
    key = (tuple((s["name"], s["S"], s["nq_last"]) for s in seq_cfg), depth, debug)
    if key not in _CACHE:
        nc = bass.Bass("TRN2", target_bir_lowering=False)
        p = Prog(nc, [dict(s) for s in seq_cfg], depth=depth, debug=debug)
        p.build()
        _CACHE[key] = (nc, p)
    return _CACHE[key]


def kernel(x_prompt, x_sample, g_pre_mix, w_in, g_q_lat, w_uq, g_kv_lat, w_ukv, g_out_mla, g_out_dil,
           w_out, g_post_mix, g_pre_mlp, w_up, w_down, g_post_mlp):
    f = lambda a: np.ascontiguousarray(np.asarray(a, dtype=np.float32))
    x_prompt, x_sample = f(x_prompt), f(x_sample)
    B, S, _ = x_prompt.shape
    BS, SS, _ = x_sample.shape
    seq_cfg = [dict(name="p", S=S, nq_last=S // 2), dict(name="s", S=SS, nq_last=SS)]
    nc, prog = get_prog(seq_cfg, DEPTH)
    shared = {
        "g_pre_mix": f(g_pre_mix), "w_in": f(w_in), "g_q_lat": f(g_q_lat),
        "w_uq": f(w_uq).reshape(DEPTH, Q_LORA, H * DQK), "g_kv_lat": f(g_kv_lat),
        "w_ukv": f(w_ukv).reshape(DEPTH, KV_LORA, H * 128), "g_out_mla": f(g_out_mla), "g_out_dil": f(g_out_dil),
        "w_out": f(w_out), "g_post_mix": f(g_post_mix), "g_pre_mlp": f(g_pre_mlp), "w_up": f(w_up),
        "w_down": f(w_down), "g_post_mlp": f(g_post_mlp),
        "ident": np.eye(128, dtype=np.float32), "bmask": band_mask(),
    }
    rope_f, rope_r, rope_s = rope_tables(S), rope_tables(S, reverse=True), rope_tables(SS)
    in_maps = []
    for c in range(8):
        m = dict(shared)
        xp = x_prompt[c // 2]
        if c % 2 == 1:
            xp = np.ascontiguousarray(xp[::-1])
        m["x_p"] = xp
        m["rope_p"] = rope_r if c % 2 == 1 else rope_f
        m["x_s"] = x_sample[c]
        m["rope_s"] = rope_s
        in_maps.append(m)
    res = run_bass_kernel_spmd(nc, in_maps, core_ids=list(range(8)))
    yp = np.empty((B, S, D), np.float32)
    ys = np.empty((BS, SS, D), np.float32)
    for c in range(8):
        r = res.results[c]
        half = np.asarray(r["y_p"], dtype=np.float32)
        if c % 2 == 0:
            yp[c // 2, :S // 2] = half
        else:
            yp[c // 2, S // 2:] = half[::-1]
        ys[c] = np.asarray(r["y_s"], dtype=np.float32)
    return (yp, ys)
```

```python
import math
from contextlib import ExitStack

import numpy as np
import concourse.bass as bass
import concourse.mybir as mybir
from concourse.bass_utils import run_bass_kernel_spmd

F32 = mybir.dt.float32
BF16 = mybir.dt.bfloat16
ALU = mybir.AluOpType
AF = mybir.ActivationFunctionType

D = 1024
DEPTH = 2
H = 8
Q_LORA, KV_LORA, ROPE, NOPE, VD = 256, 128, 32, 64, 64
DQK = NOPE + ROPE
IN_COLS = 1952
C_KR = 384
C_DQ = 416
C_DK = 928
C_DV = 1440
DFF = 4096
EPS = 1e-6
DIL = (1, 4, 16)
ENG = ["pe", "act", "dve", "pool", "sp"]
SAME_ENG_SYNC = True


class Rec:
    def __init__(self, nc, ndma=48):
        self.nc = nc
        self.ops = {e: [] for e in ENG}
        self.cnt = {e: 0 for e in ENG}
        self.seen = {e: {} for e in ENG}
        self.last_w = {}
        self.readers = {}
        self.ndma = ndma
        self.dma_val = [0] * ndma
        self.dma_next = 0
        self.dma_next_sw = 0
        self.nops = 0

    def _need(self, eng, tok):
        if tok is None:
            return
        sid, val = tok
        if sid == eng and (eng == "pe" or not SAME_ENG_SYNC):
            return
        if self.seen[eng].get(sid, 0) >= val:
            return
        self.seen[eng][sid] = val
        self.ops[eng].append(("wait", sid, val))

    def _deps(self, eng, r, w):
        for k in r:
            self._need(eng, self.last_w.get(k))
        for k in w:
            self._need(eng, self.last_w.get(k))
            for t in self.readers.get(k, ()):
                self._need(eng, t)

    def _commit(self, tok, r, w):
        for k in r:
            self.readers.setdefault(k, []).append(tok)
        for k in w:
            self.last_w[k] = tok
            self.readers[k] = []

    def op(self, eng, fn, r=(), w=(), inc=True):
        self._deps(eng, r, w)
        if inc:
            self.cnt[eng] += 1
            tok = (eng, self.cnt[eng])
        else:
            tok = (eng, self.cnt[eng] + 1)
        px = _Proxy()
        fn(px)
        self.ops[eng].append(("op", px.call, inc))
        self._commit(tok, r, w)
        self.nops += 1
        return tok

    def dma(self, eng, out, in_, r=(), w=(), noncontig=False):
        self._deps(eng, r, w)
        if eng == "pool":
            i = self.ndma - 12 + (self.dma_next_sw % 12)
            self.dma_next_sw += 1
        else:
            i = self.dma_next % (self.ndma - 12)
            self.dma_next += 1
        if self.dma_val[i] > 0:
            self._need(eng, (("d", i), self.dma_val[i]))
        self.dma_val[i] += 16
        tok = (("d", i), self.dma_val[i])
        self.ops[eng].append(("dma", out, in_, i, noncontig))
        self._commit(tok, r, w)
        self.nops += 1
        return tok

    def barrier(self):
        for e in ENG:
            for f in ENG:
                if f != e and self.cnt[f] > 0:
                    self._need(e, (f, self.cnt[f]))
            for i in range(self.ndma):
                if self.dma_val[i] > 0:
                    self._need(e, (("d", i), self.dma_val[i]))
        self.last_w = {}
        self.readers = {}

    def emit(self):
        nc = self.nc
        with ExitStack() as st:
            sem = {e: st.enter_context(nc.semaphore("s_" + e)) for e in ENG}
            dsem = [st.enter_context(nc.semaphore("d%d" % i)) for i in range(self.ndma)]
            block = st.enter_context(nc.Block())

            def sem_of(sid):
                return sem[sid] if isinstance(sid, str) else dsem[sid[1]]

            def replay(name, eng):
                for o in self.ops[name]:
                    if o[0] == "wait":
                        eng.wait_ge(sem_of(o[1]), o[2])
                    elif o[0] == "op":
                        mname, a, kw = o[1]
                        ins = getattr(eng, mname)(*a, **kw)
                        if o[2]:
                            ins.then_inc(sem[name], 1)
                    else:
                        _, out, in_, i, nonc = o
                        if nonc:
                            with nc.allow_non_contiguous_dma(reason="small strided load"):
                                eng.dma_start(out=out, in_=in_).then_inc(dsem[i], 16)
                        else:
                            eng.dma_start(out=out, in_=in_).then_inc(dsem[i], 16)

            @block.tensor
            def _(e):
                replay("pe", e)

            @block.scalar
            def _(e):
                replay("act", e)

            @block.vector
            def _(e):
                replay("dve", e)

            @block.gpsimd
            def _(e):
                replay("pool", e)

            @block.sync
            def _(e):
                replay("sp", e)


class _Proxy:
    def __init__(self):
        self.call = None

    def __getattr__(self, name):
        def f(*a, **kw):
            self.call = (name, a, kw)
            return None
        return f


class Ring:
    def __init__(self, items):
        self.items = items
        self.i = 0

    def next(self):
        it = self.items[self.i % len(self.items)]
        self.i += 1
        return it


class Prog:
    def __init__(self, nc, seqs, depth=DEPTH, debug=False):
        self.nc = nc
        self.R = Rec(nc)
        self.seqs = seqs
        self.depth = depth
        self.debug = debug
        self.uid = 0

    def sb(self, st, shape, dt, name=None):
        self.uid += 1
        return st.enter_context(self.nc.sbuf_tensor("%s_%d" % (name or "t", self.uid), list(shape), dt))

    def ps(self, st, shape, dt, name=None):
        self.uid += 1
        return st.enter_context(self.nc.psum_tensor("%s_%d" % (name or "p", self.uid), list(shape), dt))

    def mm(self, out, lhsT, rhs, start, stop, r, w, inc=True):
        return self.R.op("pe", lambda e: e.matmul(out, lhsT, rhs, start=start, stop=stop), r=r, w=w, inc=inc)

    def tr(self, out, in_, ident, r, w, inc=True):
        return self.R.op("pe", lambda e: e.transpose(out, in_, ident), r=r, w=w, inc=inc)

    def copy(self, eng, out, in_, r, w):
        if eng == "act":
            return self.R.op("act", lambda e: e.activation(out=out, in_=in_, func=AF.Copy), r=r, w=w)
        if eng == "pool":
            return self.R.op("pool", lambda e: e.tensor_copy(out, in_), r=r, w=w)
        return self.R.op("dve", lambda e: e.tensor_copy(out, in_), r=r, w=w)

    def rstd(self, ss, ms, rs, n, kss, kms, krs):
        R = self.R
        R.op("dve", lambda e: e.tensor_scalar(out=ms, in0=ss, scalar1=1.0 / n, scalar2=EPS,
                                              op0=ALU.mult, op1=ALU.add), r=[kss], w=[kms])
        nh = self.neghalf[:, 0:ss.shape[1]]
        R.op("pool", lambda e: e.tensor_tensor(out=rs, in0=ms, in1=nh, op=ALU.pow), r=[kms, "const"], w=[krs])

    def load_w(self, dst, src, gcol, kdst, pdim=128):
        R = self.R
        n = dst.shape[1]
        c0 = 0
        while c0 < n:
            c1 = min(n, c0 + 1024)
            stg, kst = self.wstage.next()
            R.dma("sp", stg[0:pdim, 0:c1 - c0], src[:, c0:c1], w=[kst])
            d = dst[:, c0:c1]
            s = stg[0:pdim, 0:c1 - c0]
            if gcol is None:
                R.op("pool", lambda e, d=d, s=s: e.tensor_copy(d, s), r=[kst], w=[kdst])
            else:
                R.op("pool", lambda e, d=d, s=s: e.tensor_scalar(out=d, in0=s, scalar1=gcol, scalar2=1.0,
                                                                 op0=ALU.mult, op1=ALU.mult),
                     r=[kst, "gcols"], w=[kdst])
            c0 = c1

    def build(self):
        nc = self.nc
        R = self.R
        dt = nc.dram_tensor
        W = {}
        for nm, shp in [("g_pre_mix", [DEPTH, D]), ("w_in", [DEPTH, D, IN_COLS]), ("g_q_lat", [DEPTH, Q_LORA]),
                        ("w_uq", [DEPTH, Q_LORA, H * DQK]), ("g_kv_lat", [DEPTH, KV_LORA]),
                        ("w_ukv", [DEPTH, KV_LORA, H * 128]), ("g_out_mla", [DEPTH, 512]),
                        ("g_out_dil", [DEPTH, 512]), ("w_out", [DEPTH, D, D]), ("g_post_mix", [DEPTH, D]),
                        ("g_pre_mlp", [DEPTH, D]), ("w_up", [DEPTH, D, DFF]), ("w_down", [DEPTH, DFF, D]),
                        ("g_post_mlp", [DEPTH, D])]:
            W[nm] = dt(nm, shp, F32, kind="ExternalInput").ap()
        self.W = W
        ident_d = dt("ident", [128, 128], F32, kind="ExternalInput").ap()
        bmask_d = dt("bmask", [128, 384], F32, kind="ExternalInput").ap()
        for s in self.seqs:
            S = s["S"]
            nm = s["name"]
            s["x"] = dt("x_" + nm, [S, D], F32, kind="ExternalInput").ap()
            s["rope"] = dt("rope_" + nm, [2, 32, S], F32, kind="ExternalInput").ap()
            s["y"] = dt("y_" + nm, [s["nq_last"], D], F32, kind="ExternalOutput").ap()
            kind = "ExternalOutput" if self.debug else "Internal"
            s["xa"] = dt("xa_" + nm, [S, D], F32, kind=kind).ap()
            s["xb"] = dt("xb_" + nm, [S, D], F32, kind=kind).ap()
            s["a"] = dt("a_" + nm, [S, 512], F32, kind=kind).ap()
            s["b"] = dt("b_" + nm, [S, 512], F32, kind=kind).ap()
            s["qdT"] = dt("qdT_" + nm, [512, S], BF16).ap()
            s["kdT"] = dt("kdT_" + nm, [512, S], BF16).ap()
            s["vd"] = dt("vd_" + nm, [S, 512], BF16).ap()

        with ExitStack() as g:
            self.ident_f = self.sb(g, [128, 128], F32, "identf")
            self.ident_b = self.sb(g, [128, 128], BF16, "identb")
            self.bmask = self.sb(g, [128, 384], F32, "bmask")
            self.neghalf = self.sb(g, [128, 8], F32, "neghalf")
            self.gcols = self.sb(g, [128, DEPTH, 32], F32, "gcols")
            ws = [self.sb(g, [128, 1024], F32, "wstage") for _ in range(2)]
            self.wstage = Ring([(ws[0], "wst0"), (ws[1], "wst1")])
            R.dma("sp", self.ident_f[:, :], ident_d, w=["const"])
            R.dma("sp", self.bmask[:, :], bmask_d, w=["const"])
            R.op("dve", lambda e: e.tensor_copy(self.ident_b[:, :], self.ident_f[:, :]), r=["const"], w=["const"])
            R.op("pool", lambda e: e.memset(self.neghalf[:, :], -0.5), w=["const"])
            for l in range(self.depth):
                for nm, off, k in [("g_pre_mix", 0, 8), ("g_pre_mlp", 8, 8), ("g_out_mla", 16, 4),
                                   ("g_out_dil", 20, 4), ("g_q_lat", 24, 2), ("g_kv_lat", 26, 1)]:
                    src = W[nm][l, :].rearrange("(k p) -> p k", p=128)
                    R.dma("sp", self.gcols[:, l, off:off + k], src, w=["gcols"], noncontig=True)
            R.barrier()

            for l in range(self.depth):
                last = (l == self.depth - 1)
                with ExitStack() as stl:
                    pre_w = None
                    for si, s in enumerate(self.seqs):
                        src = s["x"] if l == 0 else s["xb"]
                        nq = s["nq_last"] if last else s["S"]
                        self.attn_phases(l, s, src, pre_w=pre_w, do_dilated=False)
                        pre_w = None
                        if si + 1 < len(self.seqs) and self.seqs[si + 1]["S"] <= 2048:
                            pre_w = self.load_w_in(stl, l)
                        self.dilated(l, s, nq)
                with ExitStack() as stcd:
                    self.phase_c(l, None)
                    w_up = self.sb(stcd, [128, 8, DFF], BF16, "w_up")
                    self.phase_d(l, last, w_up, load_up=True)
            R.barrier()
        R.emit()

    def load_w_in(self, st, l):
        R, W, gc = self.R, self.W, self.gcols
        w_in = self.sb(st, [128, 8, IN_COLS], BF16, "w_in")
        w_kr = self.sb(st, [128, 8, 96], BF16, "w_kr")
        w_krs = self.sb(st, [128, 8, 96], BF16, "w_krs")
        self.uid += 1
        k = "_%d" % self.uid
        R.op("pool", lambda e: e.memset(w_kr[:, :, :], 0.0), w=["w_kr" + k])
        R.op("pool", lambda e: e.memset(w_krs[:, :, :], 0.0), w=["w_krs" + k])
        for kc in range(8):
            self.load_w(w_in[:, kc, :], W["w_in"][l, kc * 128:(kc + 1) * 128, :], gc[:, l, kc:kc + 1], "w_in" + k)
        R.op("pool", lambda e: e.tensor_copy(w_kr[:, :, 64:96], w_in[:, :, C_KR:C_KR + 32]),
             r=["w_in" + k], w=["w_kr" + k])
        R.op("pool", lambda e: e.tensor_copy(w_krs[:, :, 64:80], w_in[:, :, C_KR + 16:C_KR + 32]),
             r=["w_in" + k], w=["w_krs" + k])
        R.op("pool", lambda e: e.tensor_copy(w_krs[:, :, 80:96], w_in[:, :, C_KR:C_KR + 16]),
             r=["w_in" + k], w=["w_krs" + k])
        return (w_in, w_kr, w_krs, k)

    def attn_phases(self, l, s, xsrc, pre_w=None, do_dilated=True):
        nc, R, W = self.nc, self.R, self.W
        S = s["S"]
        NT = S // 128
        NB = S // 512
        last = (l == self.depth - 1)
        nq = s["nq_last"] if last else S
        gc = self.gcols
        with ExitStack() as st:
            qlatT = self.sb(st, [128, 2, S], BF16, "qlatT")
            kvlatT = self.sb(st, [128, S], BF16, "kvlatT")
            KT = [self.sb(st, [128, S], BF16, "KT%d" % i) for i in range(2)]
            ropet = [self.sb(st, [128, 2, 512], F32, "ropet%d" % i) for i in range(1)]
            ropering = Ring([(ropet[0], "rope0")])
            self.phase_a(l, s, xsrc, st, qlatT, kvlatT, pre_w, KT, ropering)
            self.mla(l, s, nq, st, qlatT, kvlatT, None, KT, None, None, None, None, ropering)
            R.barrier()
        if do_dilated:
            self.dilated(l, s, nq)

    def phase_a(self, l, s, xsrc, st_outer, qlatT, kvlatT, vall, KT, ropering):
        nc, R, W = self.nc, self.R, self.W
        S = s["S"]
        NB = S // 512
        gc = self.gcols
        with ExitStack() as st:
            if vall is None:
                w_in, w_kr, w_krs, wk = self.load_w_in(st, l)
            else:
                w_in, w_kr, w_krs, wk = vall
            kw_in, kw_kr, kw_krs = "w_in" + wk, "w_kr" + wk, "w_krs" + wk
            xin = [self.sb(st, [128, 4, D], F32, "xin%d" % i) for i in range(2)]
            xn = self.sb(st, [128, 4, D], BF16, "xn")
            junk = self.sb(st, [128, 256], BF16, "junk")
            hTs = [self.sb(st, [128, 8, 512], BF16, "hT%d" % i) for i in range(2)]
            stat = self.sb(st, [128, 3, 16], F32, "stat")
            latn = self.sb(st, [128, 4, 384], BF16, "latn")
            qk_st = [self.sb(st, [128, 8, 512], BF16, "qkst%d" % i) for i in range(1)]
            v_st = [self.sb(st, [128, 4, 512], BF16, "vst%d" % i) for i in range(1)]
            kr_t = self.sb(st, [128, 2, 512], F32, "krt")
            psT = [self.ps(st, [128, 1024], BF16, "psT%d" % i) for i in range(2)]
            psL = [self.ps(st, [128, 512], F32, "psL%d" % i) for i in range(2)]
            psM = [self.ps(st, [128, 512], F32, "psM%d" % i) for i in range(3)]
            psLT = self.ps(st, [128, 1024], BF16, "psLT")
            psTr = Ring([(psT[0], "psT0"), (psT[1], "psT1")])
            psLr = Ring([(psL[0], "psL0"), (psL[1], "psL1")])
            psMr = Ring([(psM[i], "psM%d" % i) for i in range(3)])

            evac_eng = Ring(["act", "dve"])
            rope_state = {}

            def item_load(b):
                xb_, kx = xin[b % 2], "xin%d" % (b % 2)
                t0 = b * 512
                R.dma("sp", xb_[:, :, :], xsrc[t0:t0 + 512, :].rearrange("(j p) d -> p j d", p=128),
                      r=[("x", s["name"], b)], w=[kx])

            def item_norm_act(b):
                xb_, kx = xin[b % 2], "xin%d" % (b % 2)
                for j in range(4):
                    R.op("act", lambda e, j=j: e.activation(out=xn[:, j, :], in_=xb_[:, j, :], func=AF.Square,
                                                            accum_out=stat[:, 0, j:j + 1]),
                         r=[kx], w=["xn", "ss_x"])

            def item_norm(b, act=True):
                xb_, kx = xin[b % 2], "xin%d" % (b % 2)
                if act:
                    item_norm_act(b)
                self.rstd(stat[:, 0, 0:4], stat[:, 1, 0:4], stat[:, 2, 0:4], D, "ss_x", "ms_x", "rs_x")
                for j in range(4):
                    R.op("dve", lambda e, j=j: e.tensor_scalar(out=xn[:, j, :], in0=xb_[:, j, :],
                                                               scalar1=stat[:, 2, j:j + 1], scalar2=None,
                                                               op0=ALU.mult), r=[kx, "rs_x"], w=["xn"])

            def item_tr(b, half):
                hT, khT = hTs[b % 2], "hT%d" % (b % 2)
                for c2 in (2 * half, 2 * half + 1):
                    pt, kpt = psTr.next()
                    for cc in range(2):
                        c = c2 * 2 + cc
                        for j in range(4):
                            self.tr(pt[:, cc * 512 + j * 128: cc * 512 + (j + 1) * 128],
                                    xn[:, j, c * 128:(c + 1) * 128], self.ident_b[:, :],
                                    r=["xn", "const"], w=[kpt], inc=(cc == 1 and j == 3))
                    self.copy(evac_eng.next(), hT[:, c2 * 2:c2 * 2 + 2, :],
                              pt[:, :].rearrange("p (c t) -> p c t", c=2), r=[kpt], w=[khT])

            def item_lat(b, j):
                hT, khT = hTs[b % 2], "hT%d" % (b % 2)
                pl, kpl = psLr.next()
                for kc in range(8):
                    self.mm(pl[:, 0:384], hT[:, kc, j * 128:(j + 1) * 128], w_in[:, kc, 0:384],
                            start=(kc == 0), stop=(kc == 7), r=[khT, kw_in], w=[kpl], inc=(kc == 7))
                R.op("act", lambda e: e.activation(out=junk[:, 0:256], in_=pl[:, 0:256], func=AF.Square,
                                                   accum_out=stat[:, 0, 4 + j:5 + j]), r=[kpl], w=["ss_l"])
                R.op("act", lambda e: e.activation(out=junk[:, 0:128], in_=pl[:, 256:384], func=AF.Square,
                                                   accum_out=stat[:, 0, 8 + j:9 + j]), r=[kpl], w=["ss_l"])
                R.op("dve", lambda e: e.tensor_scalar(out=stat[:, 1, 4 + j:5 + j], in0=stat[:, 0, 4 + j:5 + j],
                                                      scalar1=1.0 / Q_LORA, scalar2=EPS, op0=ALU.mult,
                                                      op1=ALU.add), r=["ss_l"], w=["ms_l"])
                R.op("dve", lambda e: e.tensor_scalar(out=stat[:, 1, 8 + j:9 + j], in0=stat[:, 0, 8 + j:9 + j],
                                                      scalar1=1.0 / KV_LORA, scalar2=EPS, op0=ALU.mult,
                                                      op1=ALU.add), r=["ss_l"], w=["ms_l"])
                R.op("pool", lambda e: e.tensor_tensor(out=stat[:, 2, 4 + j:12 + j:4], in0=stat[:, 1, 4 + j:12 + j:4],
                                                       in1=self.neghalf[:, 0:2], op=ALU.pow),
                     r=["ms_l", "const"], w=["rs_l"])
                return (pl, kpl)

            def item_latnorm(b, j, plk):
                pl, kpl = plk
                R.op("dve", lambda e: e.tensor_scalar(out=latn[:, j, 0:256], in0=pl[:, 0:256],
                                                      scalar1=stat[:, 2, 4 + j:5 + j], scalar2=None,
                                                      op0=ALU.mult), r=[kpl, "rs_l"], w=["latn"])
                R.op("dve", lambda e: e.tensor_scalar(out=latn[:, j, 256:384], in0=pl[:, 256:384],
                                                      scalar1=stat[:, 2, 8 + j:9 + j], scalar2=None,
                                                      op0=ALU.mult), r=[kpl, "rs_l"], w=["latn"])

            def item_lattr(b):
                t0 = b * 512
                for c in range(3):
                    for j in range(4):
                        self.tr(psLT[:, j * 128:(j + 1) * 128], latn[:, j, c * 128:(c + 1) * 128], self.ident_b[:, :],
                                r=["latn", "const"], w=["psLT"], inc=(j == 3))
                    dst = qlatT[:, c, t0:t0 + 512] if c < 2 else kvlatT[:, t0:t0 + 512]
                    self.copy(evac_eng.next(), dst, psLT[:, 0:512], r=["psLT"], w=["qlatT" if c < 2 else "kvlatT"])

            def item_krope_mm(b):
                hT, khT = hTs[b % 2], "hT%d" % (b % 2)
                t0 = b * 512
                rt, krt = ropering.next()
                R.dma("sp", rt[64:96, :, :], s["rope"][:, :, t0:t0 + 512].rearrange("a r t -> r a t"), w=[krt])
                pa, kpa = psMr.next()
                pb, kpb = psMr.next()
                for kc in range(8):
                    self.mm(pa[0:96, :], w_kr[:, kc, :], hT[:, kc, :], start=(kc == 0), stop=(kc == 7),
                            r=[khT, kw_kr], w=[kpa], inc=(kc == 7))
                for kc in range(8):
                    self.mm(pb[0:96, :], w_krs[:, kc, :], hT[:, kc, :], start=(kc == 0), stop=(kc == 7),
                            r=[khT, kw_krs], w=[kpb], inc=(kc == 7))
                R.op("dve", lambda e: e.tensor_tensor(out=kr_t[64:96, 0, :], in0=pa[64:96, :],
                                                      in1=rt[64:96, 0, :], op=ALU.mult), r=[kpa, krt], w=["krt0"])
                R.op("dve", lambda e: e.tensor_tensor(out=kr_t[64:96, 1, :], in0=pb[64:96, :],
                                                      in1=rt[64:96, 1, :], op=ALU.mult), r=[kpb, krt], w=["krt1"])

            def item_krope_fin(b):
                t0 = b * 512
                for i in range(2):
                    R.op("dve", lambda e, i=i: e.tensor_tensor(out=KT[i][64:96, t0:t0 + 512], in0=kr_t[64:96, 0, :],
                                                               in1=kr_t[64:96, 1, :], op=ALU.add),
                         r=["krt0", "krt1"], w=["KTr%d" % i])

            def grp_qk(b, oc):
                hT, khT = hTs[b % 2], "hT%d" % (b % 2)
                qs, kqs = qk_st[0], "qkst0"
                t0 = b * 512
                pm, kpm = psMr.next()
                col = C_DQ + oc * 128
                for kc in range(8):
                    self.mm(pm[:, :], w_in[:, kc, col:col + 128], hT[:, kc, :], start=(kc == 0), stop=(kc == 7),
                            r=[khT, kw_in], w=[kpm], inc=(kc == 7))
                self.copy(evac_eng.next(), qs[:, oc, :], pm[:, :], r=[kpm], w=[kqs])
                if oc == 3:
                    R.dma("pool", s["qdT"][:, t0:t0 + 512].rearrange("(c p) t -> p c t", p=128), qs[:, 0:4, :],
                          r=[kqs], w=[("qdT", s["name"])])
                if oc == 7:
                    R.dma("pool", s["kdT"][:, t0:t0 + 512].rearrange("(c p) t -> p c t", p=128), qs[:, 4:8, :],
                          r=[kqs], w=[("kdT", s["name"])])

            def grp_v(b, j):
                hT, khT = hTs[b % 2], "hT%d" % (b % 2)
                vs, kvs = v_st[0], "vst0"
                t0 = b * 512
                pm, kpm = psMr.next()
                for kc in range(8):
                    self.mm(pm[:, :], hT[:, kc, j * 128:(j + 1) * 128], w_in[:, kc, C_DV:C_DV + 512],
                            start=(kc == 0), stop=(kc == 7), r=[khT, kw_in], w=[kpm], inc=(kc == 7))
                self.copy(evac_eng.next(), vs[:, j, :], pm[:, :], r=[kpm], w=[kvs])
                if j == 3:
                    R.dma("pool", s["vd"][t0:t0 + 512, :].rearrange("(j p) c -> p j c", p=128), vs[:, :, :],
                          r=[kvs], w=[("vd", s["name"])])

            item_load(0)
            if NB > 1:
                item_load(1)
            item_norm(0)
            item_tr(0, 0)
            item_tr(0, 1)
            for b in range(NB):
                groups = [lambda oc=oc: grp_qk(b, oc) for oc in range(8)] + [lambda j=j: grp_v(b, j) for j in range(4)]
                pls = {}
                chain = []
                chain.append(lambda: pls.__setitem__(0, item_lat(b, 0)))
                chain.append(lambda: pls.__setitem__(1, item_lat(b, 1)))
                if b + 1 < NB:
                    chain.append(lambda: item_norm_act(b + 1))
                chain.append(lambda: item_latnorm(b, 0, pls[0]))
                chain.append(lambda: (item_latnorm(b, 1, pls[1]), pls.__setitem__(2, item_lat(b, 2))))
                chain.append(lambda: pls.__setitem__(3, item_lat(b, 3)))
                chain.append(lambda: item_krope_mm(b))
                chain.append(lambda: item_latnorm(b, 2, pls[2]))
                chain.append(lambda: (item_latnorm(b, 3, pls[3]), item_krope_fin(b)))
                if b + 1 < NB:
                    chain.append(lambda: item_norm(b + 1, act=False))
                chain.append(lambda: item_lattr(b))
                if b + 1 < NB:
                    chain.append(lambda: item_tr(b + 1, 0))
                if b + 1 < NB:
                    chain.append(lambda: item_tr(b + 1, 1))
                if b + 2 < NB:
                    chain.append(lambda: item_load(b + 2))
                gi = 0
                for ci, c in enumerate(chain):
                    c()
                    if gi < len(groups):
                        groups[gi]()
                        gi += 1
                while gi < len(groups):
                    groups[gi]()
                    gi += 1
            R.barrier()

    def mla(self, l, s, nq, st_outer, qlatT, kvlatT, vall, KT, QT, w_uq, w_uqs, w_ukvk, ropering):
        nc, R, W = self.nc, self.R, self.W
        S = s["S"]
        NT = S // 128
        NB = S // 512
        NQB = nq // 512
        scale = 1.0 / math.sqrt(DQK)
        with ExitStack() as st:
            psS = [self.ps(st, [128, 1024], F32, "psS%d" % i) for i in range(3)]
            psO = self.ps(st, [128, 512], F32, "psO")
            psB = self.ps(st, [128, 512], F32, "psB")
            kpo, kpb = "psO", "psB"
            pT = [self.sb(st, [128, 1024], BF16, "pT%d" % i) for i in range(4)]
            oT = self.sb(st, [128, 512], F32, "oT")
            a_sb = [self.sb(st, [128, 4, 64], F32, "a_sb%d" % i) for i in range(2)]
            rcp = self.sb(st, [128, 4], F32, "rcp")
            qr_t = self.sb(st, [128, 2, 256], F32, "qrt")
            psSr = Ring([(psS[i], "psS%d" % i) for i in range(3)])
            pTr = Ring([(pT[i], "pT%d" % i) for i in range(4)])

            w_uq = self.sb(st, [128, 2, H * DQK], BF16, "w_uq")
            w_uqs = self.sb(st, [128, 2, H * DQK], BF16, "w_uqs")
            w_ukvk = self.sb(st, [128, H, 64], BF16, "w_ukvk")
            w_ukvv = self.sb(st, [128, H, 64], BF16, "w_ukvv")
            gc = self.gcols
            R.op("pool", lambda e: e.memset(w_uqs[:, :, :], 0.0), w=["w_uqs"])
            for c in range(2):
                self.load_w(w_uq[:, c, :], W["w_uq"][l, c * 128:(c + 1) * 128, :], gc[:, l, 24 + c:25 + c], "w_uq")
            w_uq4 = w_uq[:, :, :].rearrange("p c (h d) -> p c h d", d=DQK)
            w_uqs4 = w_uqs[:, :, :].rearrange("p c (h d) -> p c h d", d=DQK)
            for c in range(2):
                R.op("pool", lambda e, c=c: e.tensor_copy(w_uqs4[:, c, :, 64:80], w_uq4[:, c, :, 80:96]),
                     r=["w_uq"], w=["w_uqs"])
                R.op("pool", lambda e, c=c: e.tensor_copy(w_uqs4[:, c, :, 80:96], w_uq4[:, c, :, 64:80]),
                     r=["w_uq"], w=["w_uqs"])
            stg, kst = self.wstage.next()
            R.dma("sp", stg[:, 0:1024], W["w_ukv"][l, :, :], w=[kst])
            stg3 = stg[:, 0:1024].rearrange("p (h d) -> p h d", d=128)
            R.op("pool", lambda e: e.tensor_scalar(out=w_ukvk[:, :, :], in0=stg3[:, :, 0:64], scalar1=gc[:, l, 26:27],
                                                   scalar2=1.0, op0=ALU.mult, op1=ALU.mult),
                 r=[kst, "gcols"], w=["w_ukvk"])
            R.op("pool", lambda e: e.tensor_scalar(out=w_ukvv[:, :, :], in0=stg3[:, :, 64:128], scalar1=gc[:, l, 26:27],
                                                   scalar2=1.0, op0=ALU.mult, op1=ALU.mult),
                 r=[kst, "gcols"], w=["w_ukvv"])

            vall = self.sb(st, [128, NT, 4, 65], BF16, "vall")
            QT = [self.sb(st, [128, S], BF16, "QT%d" % i) for i in range(2)]
            R.op("pool", lambda e: e.memset(vall[:, :, :, 64:65], 1.0), w=["vall"])

            bring = Ring([(psB, "psB"), (psS[0][:, 0:512], "psS0"), (psS[1][:, 0:512], "psS1"),
                          (psS[2][:, 0:512], "psS2")])
            bevac = Ring(["dve", "act"])
            qrts = [qr_t, self.sb(st, [128, 2, 256], F32, "qrtb")]
            qrtr = Ring([(qrts[0], "qrtA"), (qrts[1], "qrtB")])

            def build_v(g):
                for t in range(NT):
                    pt, kpt = bring.next()
                    self.mm(pt[:, 0:256], kvlatT[:, t * 128:(t + 1) * 128],
                            w_ukvv[:, 4 * g:4 * g + 4, :].rearrange("p h d -> p (h d)"),
                            start=True, stop=True, r=["kvlatT", "w_ukvv"], w=[kpt])
                    self.copy(bevac.next(), vall[:, t, :, 0:64], pt[:, 0:256].rearrange("p (h d) -> p h d", d=64),
                              r=[kpt], w=["vall"])

            def head_steps(h, burst=False):
                kt_, kkt = KT[h % 2], "KTn%d" % (h % 2)
                qt_, kqt = QT[h % 2], "QT%d" % (h % 2)
                steps = []

                def kstep(b):
                    pt, kpt = bring.next() if burst else (psB, kpb)
                    self.mm(pt[0:64, :], w_ukvk[:, h, :], kvlatT[:, b * 512:(b + 1) * 512], start=True, stop=True,
                            r=["kvlatT", "w_ukvk"], w=[kpt])
                    self.copy(bevac.next() if burst else "dve", kt_[0:64, b * 512:(b + 1) * 512], pt[0:64, :],
                              r=[kpt], w=[kkt])

                def qstep(b2, state):
                    t0 = b2 * 256
                    if b2 % 2 == 0:
                        rt, krt = ropering.next()
                        R.dma("sp", rt[64:96, :, :], s["rope"][:, :, t0:t0 + 512].rearrange("a r t -> r a t"), w=[krt])
                        state["rt"] = (rt, krt)
                    rt, krt = state["rt"]
                    ro = (b2 % 2) * 256
                    pt, kpt = bring.next() if burst else (psB, kpb)
                    qr, kqr = qrtr.next()
                    pa = pt[0:96, 0:256]
                    pb = pt[0:96, 256:512]
                    for c in range(2):
                        self.mm(pa, w_uq[:, c, h * DQK:(h + 1) * DQK], qlatT[:, c, t0:t0 + 256],
                                start=(c == 0), stop=(c == 1), r=["qlatT", "w_uq"], w=[kpt], inc=False)
                    for c in range(2):
                        self.mm(pb, w_uqs[:, c, h * DQK:(h + 1) * DQK], qlatT[:, c, t0:t0 + 256],
                                start=(c == 0), stop=(c == 1), r=["qlatT", "w_uqs"], w=[kpt], inc=(c == 1))
                    self.copy("act" if burst else "dve", qt_[0:64, t0:t0 + 256], pt[0:64, 0:256], r=[kpt], w=[kqt])
                    R.op("dve", lambda e: e.tensor_tensor(out=qr[64:96, 0, :], in0=pt[64:96, 0:256],
                                                          in1=rt[64:96, 0, ro:ro + 256], op=ALU.mult),
                         r=[kpt, krt], w=[kqr + "0"])
                    R.op("dve", lambda e: e.tensor_tensor(out=qr[64:96, 1, :], in0=pt[64:96, 256:512],
                                                          in1=rt[64:96, 1, ro:ro + 256], op=ALU.mult),
                         r=[kpt, krt], w=[kqr + "1"])
                    R.op("dve", lambda e: e.tensor_tensor(out=qt_[64:96, t0:t0 + 256], in0=qr[64:96, 0, :],
                                                          in1=qr[64:96, 1, :], op=ALU.add),
                         r=[kqr + "0", kqr + "1"], w=[kqt])

                state = {}
                for b in range(NB):
                    steps.append(lambda b=b: kstep(b))
                for b2 in range(2 * NQB):
                    steps.append(lambda b2=b2: qstep(b2, state))
                return steps

            NP = NT // 2
            n_iter = NQB * NP
            for st_ in head_steps(0, burst=True):
                st_()
            for h in range(H):
                if h % 4 == 0:
                    build_v(h // 4)
                burst_next = (S <= 2048)
                nxt = head_steps(h + 1) if (h + 1 < H and not burst_next) else []
                every = max(1, n_iter // (len(nxt) + 1)) if nxt else 0
                kt_ = KT[h % 2]
                qt_ = QT[h % 2]
                kk = ["KTn%d" % (h % 2), "KTr%d" % (h % 2)]
                kq = "QT%d" % (h % 2)

                def issue_s(qb, p):
                    q0 = qb * 512
                    ps_, kps = psSr.next()
                    for i in range(2):
                        k0 = (2 * p + i) * 128
                        self.mm(ps_[:, i * 512:(i + 1) * 512], kt_[0:96, k0:k0 + 128], qt_[0:96, q0:q0 + 512],
                                start=True, stop=True, r=kk + [kq], w=[kps], inc=(i == 1))
                    pt_, kpt = pTr.next()
                    R.op("act", lambda e: e.activation(out=pt_[:, :], in_=ps_[:, :], func=AF.Exp, scale=scale),
                         r=[kps], w=[kpt])
                    return (qb, p, pt_, kpt)

                def issue_o(item):
                    qb, p, pt_, kpt = item
                    for i in range(2):
                        t = 2 * p + i
                        self.mm(psO[0:65, :], vall[:, t, h % 4, :], pt_[:, i * 512:(i + 1) * 512],
                                start=(t == 0), stop=(t == NT - 1), r=["vall", kpt], w=[kpo],
                                inc=(i == 1))
                    if p == NP - 1:
                        epilogue(qb)

                def epilogue(qb):
                    q0 = qb * 512
                    R.op("dve", lambda e: e.tensor_copy(oT[0:65, :], psO[0:65, :]), r=[kpo], w=["oT"])
                    for j in range(4):
                        self.tr(psB[:, j * 65:(j + 1) * 65], oT[0:65, j * 128:(j + 1) * 128], self.ident_f[0:65, 0:65],
                                r=["oT", "const"], w=[kpb], inc=(j == 3))
                    pb3 = psB[:, 0:260].rearrange("p (j d) -> p j d", d=65)
                    R.op("dve", lambda e: e.reciprocal(rcp[:, :], pb3[:, :, 64]), r=[kpb], w=["rcp"])
                    asb, kasb = a_sb[qb % 2], "a_sb%d" % (qb % 2)
                    R.op("dve", lambda e: e.tensor_tensor(
                        out=asb[:, :, :], in0=pb3[:, :, 0:64],
                        in1=rcp[:, :].unsqueeze(2).broadcast_to([128, 4, 64]), op=ALU.mult),
                        r=[kpb, "rcp"], w=[kasb])
                    R.dma("pool", s["a"][q0:q0 + 512, h * 64:(h + 1) * 64].rearrange("(j p) d -> p j d", p=128),
                          asb[:, :, :], r=[kasb], w=[("a", s["name"])])

                pend = []
                it = 0
                for qb in range(NQB):
                    for p in range(NP):
                        pend.append(issue_s(qb, p))
                        if len(pend) > 2:
                            issue_o(pend.pop(0))
                        it += 1
                        if nxt and it % every == 0:
                            nxt.pop(0)()
                while pend:
                    issue_o(pend.pop(0))
                while nxt:
                    nxt.pop(0)()
                if burst_next and h + 1 < H:
                    for st_ in head_steps(h + 1, burst=True):
                        st_()

    def dilated(self, l, s, nq):
        nc, R = self.nc, self.R
        S = s["S"]
        NT = S // 128
        with ExitStack() as st:
            acc = self.sb(st, [128, 2, S], F32, "dacc")
            qds = [self.sb(st, [128, S], BF16, "qd%d" % i) for i in range(1)]
            kds = [self.sb(st, [128, S], BF16, "kd%d" % i) for i in range(1)]
            vres = self.sb(st, [128, 3, NT, 2, 65], BF16, "vres")
            biasT = self.sb(st, [128, 6, 384], BF16, "biasT")
            pexp = [self.sb(st, [128, 2, 384], BF16, "pexp%d" % i) for i in range(3)]
            b_sb = [self.sb(st, [128, 4, 128], F32, "b_sb%d" % i) for i in range(5)]
            rcps = [self.sb(st, [128, 4], F32, "rcpd%d" % i) for i in range(2)]
            psS = [self.ps(st, [128, 2, 512], F32, "dpsS%d" % i) for i in range(2)]
            psO = [self.ps(st, [128, 512], F32, "dpsO%d" % i) for i in range(2)]
            psB = [self.ps(st, [128, 512], F32, "dpsB%d" % i) for i in range(2)]
            psSr = Ring([(psS[i], "dpsS%d" % i) for i in range(2)])
            psOr = Ring([(psO[i], "dpsO%d" % i) for i in range(2)])
            psBr = Ring([(psB[i], "dpsB%d" % i) for i in range(2)])
            per = Ring([(pexp[i], "pexp%d" % i) for i in range(3)])
            finr = Ring([(psB[0], "dpsB0"), (psB[1], "dpsB1"), (psO[0], "dpsO0"), (psO[1], "dpsO1"),
                         (psS[0][:, 0, :], "dpsS0"), (psS[1][:, 0, :], "dpsS1")])
            R.op("pool", lambda e: e.memset(vres[:, :, :, :, 64:65], 1.0), w=["vres0", "vres1", "vres2"])
            for hp in range(4):
                for hh_ in range(2):
                    for bi, d in enumerate(DIL):
                        R.op("dve", lambda e, hh_=hh_, bi=bi, d=d: e.tensor_scalar(
                            out=biasT[:, hh_ * 3 + bi, :], in0=self.bmask[:, :],
                            scalar1=8.0 * d * 2.0 ** (-(hp * 2 + hh_ + 1)),
                            scalar2=None, op0=ALU.mult), r=["const"], w=["biasT"])
                qd, kd = qds[0], kds[0]
                kqd, kkd = "qd0", "kd0"
                R.dma("sp", qd[:, :], s["qdT"][hp * 128:(hp + 1) * 128, :], r=[("qdT", s["name"])], w=[kqd])
                R.dma("sp", kd[:, :], s["kdT"][hp * 128:(hp + 1) * 128, :], r=[("kdT", s["name"])], w=[kkd])
                for bi, d in enumerate(DIL):
                    L = S // d
                    NU = L // 128
                    for r_ in range(d):
                        src = s["vd"][:, hp * 128:(hp + 1) * 128].rearrange("(u i dd) (hh c) -> dd i u hh c",
                                                                            i=128, dd=d, hh=2)[r_]
                        u0 = 0
                        while u0 < NU:
                            u1 = min(NU, u0 + 16)
                            for hh_ in range(2):
                                R.dma("sp", vres[:, bi, r_ * NU + u0:r_ * NU + u1, hh_, 0:64], src[:, u0:u1, hh_, :],
                                      r=[("vd", s["name"])], w=["vres%d" % bi])
                            u0 = u1
                its = []
                for bi, d in enumerate(DIL):
                    for r_ in range(d):
                        for jb in range((nq // d) // 128):
                            its.append((bi, d, r_, jb))

                def issue_s(it):
                    bi, d, r_, jb = it
                    NU = (S // d) // 128
                    tiles = [u for u in (jb - 1, jb, jb + 1) if 0 <= u < NU]
                    nt = len(tiles)
                    d0 = tiles[0] - (jb - 1)
                    qcols = slice(r_ + d * 128 * jb, r_ + d * 128 * jb + d * 127 + 1, d)
                    ps_, kps = psSr.next()
                    for hh in range(2):
                        h = hp * 2 + hh
                        self.mm(ps_[:, hh, 0:nt * 128], self.ident_b[:, :],
                                biasT[:, hh * 3 + bi, d0 * 128:(d0 + nt) * 128], start=True, stop=False,
                                r=["const", "biasT"], w=[kps], inc=False)
                    for ti, u in enumerate(tiles):
                        kcols = slice(r_ + d * 128 * u, r_ + d * 128 * u + d * 127 + 1, d)
                        for hh in range(2):
                            self.mm(ps_[:, hh, ti * 128:(ti + 1) * 128], kd[hh * 64:(hh + 1) * 64, kcols],
                                    qd[hh * 64:(hh + 1) * 64, qcols], start=False, stop=(ti == nt - 1),
                                    r=[kkd, kqd], w=[kps], inc=(ti == nt - 1 and hh == 1))
                    pe_, kpe = per.next()
                    R.op("act", lambda e: e.activation(out=pe_[:, :, 0:nt * 128], in_=ps_[:, :, 0:nt * 128],
                                                       func=AF.Exp, scale=0.125), r=[kps], w=[kpe])
                    return (it, tiles, pe_, kpe, qcols)

                def issue_pv(ctx):
                    (bi, d, r_, jb), tiles, pe_, kpe, qcols = ctx
                    NU = (S // d) // 128
                    nt = len(tiles)
                    po, kpo = psOr.next()
                    for hh in range(2):
                        for ti, u in enumerate(tiles):
                            self.mm(po[0:65, hh * 128:(hh + 1) * 128], vres[:, bi, r_ * NU + u, hh, :],
                                    pe_[:, hh, ti * 128:(ti + 1) * 128], start=(ti == 0), stop=(ti == nt - 1),
                                    r=["vres%d" % bi, kpe], w=[kpo], inc=(ti == nt - 1 and hh == 1))
                    accv = acc[0:65, :, qcols]
                    pov = po[0:65, 0:256].rearrange("p (h q) -> p h q", h=2)
                    if bi == 0:
                        R.op("dve", lambda e: e.tensor_copy(accv, pov), r=[kpo], w=["acc"])
                    else:
                        R.op("dve", lambda e: e.tensor_tensor(out=accv, in0=accv, in1=pov, op=ALU.add),
                             r=[kpo, "acc"], w=["acc"])

                pend = []
                for it in its:
                    pend.append(issue_s(it))
                    if len(pend) > 1:
                        issue_pv(pend.pop(0))
                while pend:
                    issue_pv(pend.pop(0))
                for qb in range(nq // 512):
                    q0 = qb * 512
                    bsb, kbsb = b_sb[qb % len(b_sb)], "b_sb%d" % (qb % len(b_sb))
                    for hh in range(2):
                        pb, kpb = finr.next()
                        for j in range(4):
                            self.tr(pb[:, j * 65:(j + 1) * 65], acc[0:65, hh, q0 + j * 128:q0 + (j + 1) * 128],
                                    self.ident_f[0:65, 0:65], r=["acc", "const"], w=[kpb], inc=(j == 3))
                        pb3 = pb[:, 0:260].rearrange("p (j d) -> p j d", d=65)
                        rc, krc = rcps[hh], "rcpd%d" % hh
                        R.op("dve", lambda e, pb3=pb3, rc=rc: e.reciprocal(rc[:, :], pb3[:, :, 64]), r=[kpb], w=[krc])
                        R.op("dve", lambda e, pb3=pb3, rc=rc, hh=hh: e.tensor_tensor(
                            out=bsb[:, :, hh * 64:(hh + 1) * 64], in0=pb3[:, :, 0:64],
                            in1=rc[:, :].unsqueeze(2).broadcast_to([128, 4, 64]), op=ALU.mult),
                            r=[kpb, krc], w=[kbsb])
                    R.dma("pool", s["b"][q0:q0 + 512, hp * 128:(hp + 1) * 128].rearrange("(j p) d -> p j d", p=128),
                          bsb[:, :, :], r=[kbsb], w=[("b", s["name"])])
            R.barrier()

    def phase_c(self, l, w_up):
        nc, R, W = self.nc, self.R, self.W
        gc = self.gcols
        last = (l == self.depth - 1)
        with ExitStack() as st:
            w_out = self.sb(st, [128, 8, D], BF16, "w_out")
            grep = self.sb(st, [128, D], F32, "gpm")
            ab = [self.sb(st, [128, 4, D], F32, "ab%d" % i) for i in range(4)]
            xin = [self.sb(st, [128, 4, D], F32, "xc%d" % i) for i in range(4)]
            abn = self.sb(st, [128, 4, D], BF16, "abn")
            junk = self.sb(st, [128, D], BF16, "junkc")
            mTs = [self.sb(st, [128, 8, 512], BF16, "mT%d" % i) for i in range(2)]
            stat = self.sb(st, [128, 3, 16], F32, "statc")
            tmp = self.sb(st, [128, D], F32, "tmpc")
            psT = [self.ps(st, [128, 1024], BF16, "cpsT%d" % i) for i in range(2)]
            psY = [self.ps(st, [128, 1024], F32, "cpsY%d" % i) for i in range(2)]
            psTr = Ring([(psT[0], "cpsT0"), (psT[1], "cpsT1")])
            for kc in range(8):
                g = gc[:, l, 16 + kc:17 + kc]
                self.load_w(w_out[:, kc, :], W["w_out"][l, kc * 128:(kc + 1) * 128, :], g, "w_out")
            R.dma("sp", grep[:, :], W["g_post_mix"][l, :].partition_broadcast(128), w=["grep"])
            evac_eng = Ring(["act", "dve"])
            blocks = []
            for s in self.seqs:
                nq = s["nq_last"] if last else s["S"]
                xsrc = s["x"] if l == 0 else s["xb"]
                for b in range(nq // 512):
                    blocks.append((s, xsrc, b * 512))
            NBk = len(blocks)

            def c_load(i):
                s, xsrc, t0 = blocks[i]
                ab_, kab = ab[i % 4], "ab%d" % (i % 4)
                x_, kx = xin[i % 4], "xc%d" % (i % 4)
                R.dma("sp", ab_[:, :, 0:512], s["a"][t0:t0 + 512, :].rearrange("(j p) d -> p j d", p=128),
                      r=[("a", s["name"])], w=[kab])
                R.dma("sp", ab_[:, :, 512:1024], s["b"][t0:t0 + 512, :].rearrange("(j p) d -> p j d", p=128),
                      r=[("b", s["name"])], w=[kab])
                R.dma("sp", x_[:, :, :], xsrc[t0:t0 + 512, :].rearrange("(j p) d -> p j d", p=128),
                      r=[("xb", s["name"])], w=[kx])

            def c_norm(i):
                ab_, kab = ab[i % 4], "ab%d" % (i % 4)
                for j in range(4):
                    for half in range(2):
                        R.op("act", lambda e, j=j, half=half: e.activation(
                            out=junk[:, 0:512], in_=ab_[:, j, half * 512:(half + 1) * 512], func=AF.Square,
                            accum_out=stat[:, 0, half * 4 + j:half * 4 + j + 1]), r=[kab], w=["ss_c"])
                self.rstd(stat[:, 0, 0:8], stat[:, 1, 0:8], stat[:, 2, 0:8], 512, "ss_c", "ms_c", "rs_c")
                for j in range(4):
                    for half in range(2):
                        R.op("dve", lambda e, j=j, half=half: e.tensor_scalar(
                            out=abn[:, j, half * 512:(half + 1) * 512], in0=ab_[:, j, half * 512:(half + 1) * 512],
                            scalar1=stat[:, 2, half * 4 + j:half * 4 + j + 1], scalar2=None, op0=ALU.mult),
                            r=[kab, "rs_c"], w=["abn"])

            def c_tr(i, half_):
                mT, kmT = mTs[i % 2], "mT%d" % (i % 2)
                for c2 in (2 * half_, 2 * half_ + 1):
                    pt, kpt = psTr.next()
                    for cc in range(2):
                        c = c2 * 2 + cc
                        for j in range(4):
                            self.tr(pt[:, cc * 512 + j * 128: cc * 512 + (j + 1) * 128],
                                    abn[:, j, c * 128:(c + 1) * 128], self.ident_b[:, :],
                                    r=["abn", "const"], w=[kpt], inc=(cc == 1 and j == 3))
                    self.copy(evac_eng.next(), mT[:, c2 * 2:c2 * 2 + 2, :],
                              pt[:, :].rearrange("p (c t) -> p c t", c=2), r=[kpt], w=[kmT])

            def c_mm(i, j):
                mT, kmT = mTs[i % 2], "mT%d" % (i % 2)
                x_, kx = xin[i % 4], "xc%d" % (i % 4)
                py, kpy = psY[j % 2], "cpsY%d" % (j % 2)
                for half in range(2):
                    for kc in range(8):
                        self.mm(py[:, half * 512:(half + 1) * 512], mT[:, kc, j * 128:(j + 1) * 128],
                                w_out[:, kc, half * 512:(half + 1) * 512], start=(kc == 0), stop=(kc == 7),
                                r=[kmT, "w_out"], w=[kpy], inc=(kc == 7 and half == 1))
                R.op("act", lambda e: e.activation(out=junk[:, :], in_=py[:, :], func=AF.Square,
                                                   accum_out=stat[:, 0, 8 + j:9 + j]), r=[kpy], w=["ss_y"])
                self.rstd(stat[:, 0, 8 + j:9 + j], stat[:, 1, 8 + j:9 + j], stat[:, 2, 8 + j:9 + j], D,
                          "ss_y", "ms_y", "rs_y")
                R.op("dve", lambda e: e.scalar_tensor_tensor(
                    out=tmp[:, :], in0=py[:, :], scalar=stat[:, 2, 8 + j:9 + j], in1=grep[:, :],
                    op0=ALU.mult, op1=ALU.mult), r=[kpy, "rs_y", "grep"], w=["tmpc"])
                R.op("dve", lambda e: e.tensor_tensor(out=x_[:, j, :], in0=x_[:, j, :], in1=tmp[:, :],
                                                      op=ALU.add), r=["tmpc", kx], w=[kx])

            def c_store(i):
                s, xsrc, t0 = blocks[i]
                x_, kx = xin[i % 4], "xc%d" % (i % 4)
                R.dma("pool", s["xa"][t0:t0 + 512, :].rearrange("(j p) d -> p j d", p=128), x_[:, :, :],
                      r=[kx], w=[("xa", s["name"])])

            wup_chunks = [(kc, c0) for kc in range(8) for c0 in range(0, DFF, 1024)]
            per_blk = -(-len(wup_chunks) // max(1, NBk - 1))

            def c_wup(n):
                for _ in range(n):
                    if not wup_chunks:
                        return
                    kc, c0 = wup_chunks.pop(0)
                    self.load_w(w_up[:, kc, c0:c0 + 1024], W["w_up"][l, kc * 128:(kc + 1) * 128, c0:c0 + 1024],
                                gc[:, l, 8 + kc:9 + kc], "w_up")

            c_load(0)
            if NBk > 1:
                c_load(1)
            if NBk > 2:
                c_load(2)
            if NBk > 3:
                c_load(3)
            c_norm(0)
            c_tr(0, 0)
            c_tr(0, 1)
            for i in range(NBk):
                c_mm(i, 0)
                if i + 1 < NBk:
                    c_norm(i + 1)
                c_mm(i, 1)
                if i + 1 < NBk:
                    c_tr(i + 1, 0)
                c_mm(i, 2)
                if i + 1 < NBk:
                    c_tr(i + 1, 1)
                c_mm(i, 3)
                c_store(i)
                if i + 4 < NBk:
                    c_load(i + 4)
            if w_up is not None:
                c_wup(len(wup_chunks))
            R.barrier()

    def phase_d(self, l, last, w_up, load_up=False):
        nc, R, W = self.nc, self.R, self.W
        gc = self.gcols
        TB = 256
        NJ = TB // 128
        with ExitStack() as st:
            w_dn = self.sb(st, [128, 32, D], BF16, "w_dn")
            grep = self.sb(st, [128, D], F32, "gpo")
            xin = [self.sb(st, [128, NJ, D], F32, "xd%d" % i) for i in range(2)]
            xn = self.sb(st, [128, NJ, D], BF16, "xnd")
            junk = self.sb(st, [128, D], BF16, "junkd")
            hTs = [self.sb(st, [128, 8, TB], BF16, "hTd%d" % i) for i in range(2)]
            uT = self.sb(st, [128, 32, TB], BF16, "uT")
            rl = [self.sb(st, [128, TB], F32, "rl%d" % i) for i in range(2)]
            stat = self.sb(st, [128, 3, 8], F32, "statd")
            tmp = self.sb(st, [128, D], F32, "tmpd")
            psT = [self.ps(st, [128, 1024], BF16, "dpT%d" % i) for i in range(2)]
            psU = [self.ps(st, [128, 512], F32, "dpU%d" % i) for i in range(2)]
            psY = [self.ps(st, [128, 1024], F32, "dpY%d" % i) for i in range(2)]
            psTr = Ring([(psT[0], "dpT0"), (psT[1], "dpT1")])
            psUr = Ring([(psU[0], "dpU0"), (psU[1], "dpU1")])
            rlr = Ring([(rl[0], "rl0"), (rl[1], "rl1")])
            ws2 = [self.sb(st, [128, 1024], F32, "wstD%d" % i) for i in range(3)]
            old_ring = self.wstage
            self.wstage = Ring(old_ring.items + [(ws2[i], "wstD%d" % i) for i in range(3)])
            dblocks = []
            for s in self.seqs:
                nq_ = s["nq_last"] if last else s["S"]
                for b in range(nq_ // TB):
                    dblocks.append((s, b * TB))
            preloaded = set()
            for i in range(min(2, len(dblocks))):
                s_, t0_ = dblocks[i]
                R.dma("sp", xin[i % 2][:, :, :], s_["xa"][t0_:t0_ + TB, :].rearrange("(j p) d -> p j d", p=128),
                      r=[("xa", s_["name"])], w=["xd%d" % (i % 2)])
                preloaded.add(i)
            if load_up:
                for kc in range(8):
                    self.load_w(w_up[:, kc, :], W["w_up"][l, kc * 128:(kc + 1) * 128, :], gc[:, l, 8 + kc:9 + kc], "w_up")
            for oc in range(32):
                self.load_w(w_dn[:, oc, :], W["w_down"][l, oc * 128:(oc + 1) * 128, :], None, "w_dn")
            self.wstage = old_ring
            R.dma("sp", grep[:, :], W["g_post_mlp"][l, :].partition_broadcast(128), w=["grepd"])
            evac_eng = Ring(["act", "dve"])
            ND = len(dblocks)

            def d_load(i):
                if i in preloaded or i >= ND:
                    return
                s_, t0_ = dblocks[i]
                R.dma("sp", xin[i % 2][:, :, :], s_["xa"][t0_:t0_ + TB, :].rearrange("(j p) d -> p j d", p=128),
                      r=[("xa", s_["name"])], w=["xd%d" % (i % 2)])

            def d_norm(i):
                x_, kx = xin[i % 2], "xd%d" % (i % 2)
                for j in range(NJ):
                    R.op("act", lambda e, j=j: e.activation(out=junk[:, :], in_=x_[:, j, :], func=AF.Square,
                                                            accum_out=stat[:, 0, j:j + 1]), r=[kx], w=["ss_d"])
                self.rstd(stat[:, 0, 0:NJ], stat[:, 1, 0:NJ], stat[:, 2, 0:NJ], D, "ss_d", "ms_d", "rs_d")
                for j in range(NJ):
                    R.op("dve", lambda e, j=j: e.tensor_scalar(out=xn[:, j, :], in0=x_[:, j, :],
                                                               scalar1=stat[:, 2, j:j + 1], scalar2=None,
                                                               op0=ALU.mult), r=[kx, "rs_d"], w=["xnd"])

            def d_tr(i):
                hT, khT = hTs[i % 2], "hTd%d" % (i % 2)
                for c4 in range(2):
                    pt, kpt = psTr.next()
                    for cc in range(4):
                        c = c4 * 4 + cc
                        for j in range(NJ):
                            self.tr(pt[:, cc * TB + j * 128: cc * TB + (j + 1) * 128],
                                    xn[:, j, c * 128:(c + 1) * 128], self.ident_b[:, :],
                                    r=["xnd", "const"], w=[kpt], inc=(cc == 3 and j == NJ - 1))
                    self.copy(evac_eng.next(), hT[:, c4 * 4:c4 * 4 + 4, :],
                              pt[:, :].rearrange("p (c t) -> p c t", c=4), r=[kpt], w=[khT])

            d_norm(0)
            d_tr(0)
            for i in range(ND):
                s, t0 = dblocks[i]
                dst = s["y"] if last else s["xb"]
                x_, kx = xin[i % 2], "xd%d" % (i % 2)
                hT, khT = hTs[i % 2], "hTd%d" % (i % 2)
                d_load(i + 1)
                for oc in range(32):
                    pu, kpu = psUr.next()
                    for kc in range(8):
                        self.mm(pu[:, 0:TB], w_up[:, kc, oc * 128:(oc + 1) * 128], hT[:, kc, :],
                                start=(kc == 0), stop=(kc == 7), r=[khT, "w_up"], w=[kpu], inc=(kc == 7))
                    r_, krl = rlr.next()
                    R.op("act", lambda e: e.activation(out=r_[:, :], in_=pu[:, 0:TB], func=AF.Relu), r=[kpu], w=[krl])
                    R.op("dve", lambda e: e.tensor_tensor(out=uT[:, oc, :], in0=r_[:, :], in1=r_[:, :], op=ALU.mult),
                         r=[krl], w=["uT"])
                    if oc == 12 and i + 1 < ND:
                        d_norm(i + 1)
                if i + 1 < ND:
                    d_tr(i + 1)
                for j in range(NJ):
                    py, kpy = psY[j % 2], "dpY%d" % (j % 2)
                    for half in range(2):
                        for oc in range(32):
                            self.mm(py[:, half * 512:(half + 1) * 512], uT[:, oc, j * 128:(j + 1) * 128],
                                    w_dn[:, oc, half * 512:(half + 1) * 512], start=(oc == 0), stop=(oc == 31),
                                    r=["uT", "w_dn"], w=[kpy], inc=(oc == 31 and half == 1))
                    R.op("act", lambda e: e.activation(out=junk[:, :], in_=py[:, :], func=AF.Square,
                                                       accum_out=stat[:, 0, 4 + j:5 + j]), r=[kpy], w=["ss_y"])
                    self.rstd(stat[:, 0, 4 + j:5 + j], stat[:, 1, 4 + j:5 + j], stat[:, 2, 4 + j:5 + j], D,
                              "ss_y", "ms_y", "rs_y")
                    R.op("dve", lambda e: e.scalar_tensor_tensor(
                        out=tmp[:, :], in0=py[:, :], scalar=stat[:, 2, 4 + j:5 + j], in1=grep[:, :],
                        op0=ALU.mult, op1=ALU.mult), r=[kpy, "rs_y", "grepd"], w=["tmpd"])
                    R.op("dve", lambda e: e.tensor_tensor(out=x_[:, j, :], in0=x_[:, j, :], in1=tmp[:, :],
                                                          op=ALU.add), r=["tmpd", kx], w=[kx])
                R.dma("pool", dst[t0:t0 + TB, :].rearrange("(j p) d -> p j d", p=128), x_[:, :, :],
                      r=[kx], w=[("xb", s["name"])])
            R.barrier()


def rope_tables(S, reverse=False):
    half = ROPE // 2
    inv_freq = (np.float32(10000.0) ** (-np.arange(half, dtype=np.float32) / np.float32(half))).astype(np.float32)
    pos = np.arange(S, dtype=np.float32)
    if reverse:
        pos = pos[::-1].copy()
    ang = (pos[:, None] * inv_freq[None, :]).astype(np.float32)
    c = np.cos(ang).astype(np.float32).T
    sn = np.sin(ang).astype(np.float32).T
    cosT = np.concatenate([c, c], axis=0)
    sinT = np.concatenate([-sn, sn], axis=0)
    return np.ascontiguousarray(np.stack([cosT, sinT], axis=0))


def band_mask():
    k = np.arange(128)[:, None, None]
    dl = np.arange(3)[None, :, None]
    q = np.arange(128)[None, None, :]
    rel = np.abs(128 * (dl - 1) + k - q)
    m = np.where(rel <= 64, -rel.astype(np.float32), np.float32(-1e30)).astype(np.float32)
    return np.ascontiguousarray(m.reshape(128, 384))


_CACHE = {}


def get_prog(seq_cfg, depth, debug=False):
    key = (tuple((s["name"], s["S"], s["nq_last"]) for s in seq_cfg), depth, debug)
    if key not in _CACHE:
        nc = bass.Bass("TRN2", target_bir_lowering=False)
        p = Prog(nc, [dict(s) for s in seq_cfg], depth=depth, debug=debug)
        p.build()
        _CACHE[key] = (nc, p)
    return _CACHE[key]


def kernel(x_prompt, x_sample, g_pre_mix, w_in, g_q_lat, w_uq, g_kv_lat, w_ukv, g_out_mla, g_out_dil,
           w_out, g_post_mix, g_pre_mlp, w_up, w_down, g_post_mlp):
    f = lambda a: np.ascontiguousarray(np.asarray(a, dtype=np.float32))
    x_prompt, x_sample = f(x_prompt), f(x_sample)
    B, S, _ = x_prompt.shape
    BS, SS, _ = x_sample.shape
    seq_cfg = [dict(name="p", S=S, nq_last=S // 2), dict(name="s", S=SS, nq_last=SS)]
    nc, prog = get_prog(seq_cfg, DEPTH)
    shared = {
        "g_pre_mix": f(g_pre_mix), "w_in": f(w_in), "g_q_lat": f(g_q_lat),
        "w_uq": f(w_uq).reshape(DEPTH, Q_LORA, H * DQK), "g_kv_lat": f(g_kv_lat),
        "w_ukv": f(w_ukv).reshape(DEPTH, KV_LORA, H * 128), "g_out_mla": f(g_out_mla), "g_out_dil": f(g_out_dil),
        "w_out": f(w_out), "g_post_mix": f(g_post_mix), "g_pre_mlp": f(g_pre_mlp), "w_up": f(w_up),
        "w_down": f(w_down), "g_post_mlp": f(g_post_mlp),
        "ident": np.eye(128, dtype=np.float32), "bmask": band_mask(),
    }
    rope_f, rope_r, rope_s = rope_tables(S), rope_tables(S, reverse=True), rope_tables(SS)
    in_maps = []
    for c in range(8):
        m = dict(shared)
        xp = x_prompt[c // 2]
        if c % 2 == 1:
            xp = np.ascontiguousarray(xp[::-1])
        m["x_p"] = xp
        m["rope_p"] = rope_r if c % 2 == 1 else rope_f
        m["x_s"] = x_sample[c]
        m["rope_s"] = rope_s
        in_maps.append(m)
    res = run_bass_kernel_spmd(nc, in_maps, core_ids=list(range(8)))
    yp = np.empty((B, S, D), np.float32)
    ys = np.empty((BS, SS, D), np.float32)
    for c in range(8):
        r = res.results[c]
        half = np.asarray(r["y_p"], dtype=np.float32)
        if c % 2 == 0:
            yp[c // 2, :S // 2] = half
        else:
            yp[c // 2, S // 2:] = half[::-1]
        ys[c] = np.asarray(r["y_s"], dtype=np.float32)
    return (yp, ys)
```

```python
import math
from contextlib import ExitStack

import numpy as np
import concourse.bass as bass
import concourse.mybir as mybir
from concourse.bass_utils import run_bass_kernel_spmd

F32 = mybir.dt.float32
BF16 = mybir.dt.bfloat16
ALU = mybir.AluOpType
AF = mybir.ActivationFunctionType

D = 1024
DEPTH = 2
H = 8
Q_LORA, KV_LORA, ROPE, NOPE, VD = 256, 128, 32, 64, 64
DQK = NOPE + ROPE
IN_COLS = 1952
C_KR = 384
C_DQ = 416
C_DK = 928
C_DV = 1440
DFF = 4096
EPS = 1e-6
DIL = (1, 4, 16)
ENG = ["pe", "act", "dve", "pool", "sp"]
SAME_ENG_SYNC = True


class Rec:
    def __init__(self, nc, ndma=48):
        self.nc = nc
        self.ops = {e: [] for e in ENG}
        self.cnt = {e: 0 for e in ENG}
        self.seen = {e: {} for e in ENG}
        self.last_w = {}
        self.readers = {}
        self.ndma = ndma
        self.dma_val = [0] * ndma
        self.dma_next = 0
        self.dma_next_sw = 0
        self.nops = 0

    def _need(self, eng, tok):
        if tok is None:
            return
        sid, val = tok
        if sid == eng and (eng == "pe" or not SAME_ENG_SYNC):
            return
        if self.seen[eng].get(sid, 0) >= val:
            return
        self.seen[eng][sid] = val
        self.ops[eng].append(("wait", sid, val))

    def _deps(self, eng, r, w):
        for k in r:
            self._need(eng, self.last_w.get(k))
        for k in w:
            self._need(eng, self.last_w.get(k))
            for t in self.readers.get(k, ()):
                self._need(eng, t)

    def _commit(self, tok, r, w):
        for k in r:
            self.readers.setdefault(k, []).append(tok)
        for k in w:
            self.last_w[k] = tok
            self.readers[k] = []

    def op(self, eng, fn, r=(), w=(), inc=True):
        self._deps(eng, r, w)
        if inc:
            self.cnt[eng] += 1
            tok = (eng, self.cnt[eng])
        else:
            tok = (eng, self.cnt[eng] + 1)
        px = _Proxy()
        fn(px)
        self.ops[eng].append(("op", px.call, inc))
        self._commit(tok, r, w)
        self.nops += 1
        return tok

    def dma(self, eng, out, in_, r=(), w=(), noncontig=False):
        self._deps(eng, r, w)
        if eng == "pool":
            i = self.ndma - 12 + (self.dma_next_sw % 12)
            self.dma_next_sw += 1
        else:
            i = self.dma_next % (self.ndma - 12)
            self.dma_next += 1
        if self.dma_val[i] > 0:
            self._need(eng, (("d", i), self.dma_val[i]))
        self.dma_val[i] += 16
        tok = (("d", i), self.dma_val[i])
        self.ops[eng].append(("dma", out, in_, i, noncontig))
        self._commit(tok, r, w)
        self.nops += 1
        return tok

    def barrier(self):
        for e in ENG:
            for f in ENG:
                if f != e and self.cnt[f] > 0:
                    self._need(e, (f, self.cnt[f]))
            for i in range(self.ndma):
                if self.dma_val[i] > 0:
                    self._need(e, (("d", i), self.dma_val[i]))
        self.last_w = {}
        self.readers = {}

    def emit(self):
        nc = self.nc
        with ExitStack() as st:
            sem = {e: st.enter_context(nc.semaphore("s_" + e)) for e in ENG}
            dsem = [st.enter_context(nc.semaphore("d%d" % i)) for i in range(self.ndma)]
            block = st.enter_context(nc.Block())

            def sem_of(sid):
                return sem[sid] if isinstance(sid, str) else dsem[sid[1]]

            def replay(name, eng):
                for o in self.ops[name]:
                    if o[0] == "wait":
                        eng.wait_ge(sem_of(o[1]), o[2])
                    elif o[0] == "op":
                        mname, a, kw = o[1]
                        ins = getattr(eng, mname)(*a, **kw)
                        if o[2]:
                            ins.then_inc(sem[name], 1)
                    else:
                        _, out, in_, i, nonc = o
                        if nonc:
                            with nc.allow_non_contiguous_dma(reason="small strided load"):
                                eng.dma_start(out=out, in_=in_).then_inc(dsem[i], 16)
                        else:
                            eng.dma_start(out=out, in_=in_).then_inc(dsem[i], 16)

            @block.tensor
            def _(e):
                replay("pe", e)

            @block.scalar
            def _(e):
                replay("act", e)

            @block.vector
            def _(e):
                replay("dve", e)

            @block.gpsimd
            def _(e):
                replay("pool", e)

            @block.sync
            def _(e):
                replay("sp", e)


class _Proxy:
    def __init__(self):
        self.call = None

    def __getattr__(self, name):
        def f(*a, **kw):
            self.call = (name, a, kw)
            return None
        return f


class Ring:
    def __init__(self, items):
        self.items = items
        self.i = 0

    def next(self):
        it = self.items[self.i % len(self.items)]
        self.i += 1
        return it


class Prog:
    def __init__(self, nc, seqs, depth=DEPTH, debug=False):
        self.nc = nc
        self.R = Rec(nc)
        self.seqs = seqs
        self.depth = depth
        self.debug = debug
        self.uid = 0

    def sb(self, st, shape, dt, name=None):
        self.uid += 1
        return st.enter_context(self.nc.sbuf_tensor("%s_%d" % (name or "t", self.uid), list(shape), dt))

    def ps(self, st, shape, dt, name=None):
        self.uid += 1
        return st.enter_context(self.nc.psum_tensor("%s_%d" % (name or "p", self.uid), list(shape), dt))

    def mm(self, out, lhsT, rhs, start, stop, r, w, inc=True):
        return self.R.op("pe", lambda e: e.matmul(out, lhsT, rhs, start=start, stop=stop), r=r, w=w, inc=inc)

    def tr(self, out, in_, ident, r, w, inc=True):
        return self.R.op("pe", lambda e: e.transpose(out, in_, ident), r=r, w=w, inc=inc)

    def copy(self, eng, out, in_, r, w):
        if eng == "act":
            return self.R.op("act", lambda e: e.activation(out=out, in_=in_, func=AF.Copy), r=r, w=w)
        if eng == "pool":
            return self.R.op("pool", lambda e: e.tensor_copy(out, in_), r=r, w=w)
        return self.R.op("dve", lambda e: e.tensor_copy(out, in_), r=r, w=w)

    def rstd(self, ss, ms, rs, n, kss, kms, krs):
        R = self.R
        R.op("dve", lambda e: e.tensor_scalar(out=ms, in0=ss, scalar1=1.0 / n, scalar2=EPS,
                                              op0=ALU.mult, op1=ALU.add), r=[kss], w=[kms])
        nh = self.neghalf[:, 0:ss.shape[1]]
        R.op("pool", lambda e: e.tensor_tensor(out=rs, in0=ms, in1=nh, op=ALU.pow), r=[kms, "const"], w=[krs])

    def load_w(self, dst, src, gcol, kdst, pdim=128):
        R = self.R
        n = dst.shape[1]
        c0 = 0
        while c0 < n:
            c1 = min(n, c0 + 1024)
            stg, kst = self.wstage.next()
            R.dma("sp", stg[0:pdim, 0:c1 - c0], src[:, c0:c1], w=[kst])
            d = dst[:, c0:c1]
            s = stg[0:pdim, 0:c1 - c0]
            if gcol is None:
                R.op("pool", lambda e, d=d, s=s: e.tensor_copy(d, s), r=[kst], w=[kdst])
            else:
                R.op("pool", lambda e, d=d, s=s: e.tensor_scalar(out=d, in0=s, scalar1=gcol, scalar2=1.0,
                                                                 op0=ALU.mult, op1=ALU.mult),
                     r=[kst, "gcols"], w=[kdst])
            c0 = c1

    def build(self):
        nc = self.nc
        R = self.R
        dt = nc.dram_tensor
        W = {}
        for nm, shp in [("g_pre_mix", [DEPTH, D]), ("w_in", [DEPTH, D, IN_COLS]), ("g_q_lat", [DEPTH, Q_LORA]),
                        ("w_uq", [DEPTH, Q_LORA, H * DQK]), ("g_kv_lat", [DEPTH, KV_LORA]),
                        ("w_ukv", [DEPTH, KV_LORA, H * 128]), ("g_out_mla", [DEPTH, 512]),
                        ("g_out_dil", [DEPTH, 512]), ("w_out", [DEPTH, D, D]), ("g_post_mix", [DEPTH, D]),
                        ("g_pre_mlp", [DEPTH, D]), ("w_up", [DEPTH, D, DFF]), ("w_down", [DEPTH, DFF, D]),
                        ("g_post_mlp", [DEPTH, D])]:
            W[nm] = dt(nm, shp, F32, kind="ExternalInput").ap()
        self.W = W
        ident_d = dt("ident", [128, 128], F32, kind="ExternalInput").ap()
        bmask_d = dt("bmask", [128, 384], F32, kind="ExternalInput").ap()
        for s in self.seqs:
            S = s["S"]
            nm = s["name"]
            s["x"] = dt("x_" + nm, [S, D], F32, kind="ExternalInput").ap()
            s["rope"] = dt("rope_" + nm, [2, 32, S], F32, kind="ExternalInput").ap()
            s["y"] = dt("y_" + nm, [s["nq_last"], D], F32, kind="ExternalOutput").ap()
            kind = "ExternalOutput" if self.debug else "Internal"
            s["xa"] = dt("xa_" + nm, [S, D], F32, kind=kind).ap()
            s["xb"] = dt("xb_" + nm, [S, D], F32, kind=kind).ap()
            s["a"] = dt("a_" + nm, [S, 512], F32, kind=kind).ap()
            s["b"] = dt("b_" + nm, [S, 512], F32, kind=kind).ap()
            s["qdT"] = dt("qdT_" + nm, [512, S], BF16).ap()
            s["kdT"] = dt("kdT_" + nm, [512, S], BF16).ap()
            s["vd"] = dt("vd_" + nm, [S, 512], BF16).ap()

        with ExitStack() as g:
            self.ident_f = self.sb(g, [128, 128], F32, "identf")
            self.ident_b = self.sb(g, [128, 128], BF16, "identb")
            self.bmask = self.sb(g, [128, 384], F32, "bmask")
            self.neghalf = self.sb(g, [128, 8], F32, "neghalf")
            self.gcols = self.sb(g, [128, DEPTH, 32], F32, "gcols")
            ws = [self.sb(g, [128, 1024], F32, "wstage") for _ in range(2)]
            self.wstage = Ring([(ws[0], "wst0"), (ws[1], "wst1")])
            R.dma("sp", self.ident_f[:, :], ident_d, w=["const"])
            R.dma("sp", self.bmask[:, :], bmask_d, w=["const"])
            R.op("dve", lambda e: e.tensor_copy(self.ident_b[:, :], self.ident_f[:, :]), r=["const"], w=["const"])
            R.op("pool", lambda e: e.memset(self.neghalf[:, :], -0.5), w=["const"])
            for l in range(self.depth):
                for nm, off, k in [("g_pre_mix", 0, 8), ("g_pre_mlp", 8, 8), ("g_out_mla", 16, 4),
                                   ("g_out_dil", 20, 4), ("g_q_lat", 24, 2), ("g_kv_lat", 26, 1)]:
                    src = W[nm][l, :].rearrange("(k p) -> p k", p=128)
                    R.dma("sp", self.gcols[:, l, off:off + k], src, w=["gcols"], noncontig=True)
            R.barrier()

            for l in range(self.depth):
                last = (l == self.depth - 1)
                with ExitStack() as stl:
                    pre_w = None
                    for si, s in enumerate(self.seqs):
                        src = s["x"] if l == 0 else s["xb"]
                        nq = s["nq_last"] if last else s["S"]
                        self.attn_phases(l, s, src, pre_w=pre_w, do_dilated=False)
                        pre_w = None
                        if si + 1 < len(self.seqs) and self.seqs[si + 1]["S"] <= 2048:
                            pre_w = self.load_w_in(stl, l)
                        self.dilated(l, s, nq)
                with ExitStack() as stcd:
                    self.phase_c(l, None)
                    w_up = self.sb(stcd, [128, 8, DFF], BF16, "w_up")
                    self.phase_d(l, last, w_up, load_up=True)
            R.barrier()
        R.emit()

    def load_w_in(self, st, l):
        R, W, gc = self.R, self.W, self.gcols
        w_in = self.sb(st, [128, 8, IN_COLS], BF16, "w_in")
        w_kr = self.sb(st, [128, 8, 96], BF16, "w_kr")
        w_krs = self.sb(st, [128, 8, 96], BF16, "w_krs")
        self.uid += 1
        k = "_%d" % self.uid
        R.op("pool", lambda e: e.memset(w_kr[:, :, :], 0.0), w=["w_kr" + k])
        R.op("pool", lambda e: e.memset(w_krs[:, :, :], 0.0), w=["w_krs" + k])
        for kc in range(8):
            self.load_w(w_in[:, kc, :], W["w_in"][l, kc * 128:(kc + 1) * 128, :], gc[:, l, kc:kc + 1], "w_in" + k)
        R.op("pool", lambda e: e.tensor_copy(w_kr[:, :, 64:96], w_in[:, :, C_KR:C_KR + 32]),
             r=["w_in" + k], w=["w_kr" + k])
        R.op("pool", lambda e: e.tensor_copy(w_krs[:, :, 64:80], w_in[:, :, C_KR + 16:C_KR + 32]),
             r=["w_in" + k], w=["w_krs" + k])
        R.op("pool", lambda e: e.tensor_copy(w_krs[:, :, 80:96], w_in[:, :, C_KR:C_KR + 16]),
             r=["w_in" + k], w=["w_krs" + k])
        return (w_in, w_kr, w_krs, k)

    def attn_phases(self, l, s, xsrc, pre_w=None, do_dilated=True):
        nc, R, W = self.nc, self.R, self.W
        S = s["S"]
        NT = S // 128
        NB = S // 512
        last = (l == self.depth - 1)
        nq = s["nq_last"] if last else S
        gc = self.gcols
        with ExitStack() as st:
            qlatT = self.sb(st, [128, 2, S], BF16, "qlatT")
            kvlatT = self.sb(st, [128, S], BF16, "kvlatT")
            KT = [self.sb(st, [128, S], BF16, "KT%d" % i) for i in range(2)]
            ropet = [self.sb(st, [128, 2, 512], F32, "ropet%d" % i) for i in range(1)]
            ropering = Ring([(ropet[0], "rope0")])
            self.phase_a(l, s, xsrc, st, qlatT, kvlatT, pre_w, KT, ropering)
            self.mla(l, s, nq, st, qlatT, kvlatT, None, KT, None, None, None, None, ropering)
            R.barrier()
        if do_dilated:
            self.dilated(l, s, nq)

    def phase_a(self, l, s, xsrc, st_outer, qlatT, kvlatT, vall, KT, ropering):
        nc, R, W = self.nc, self.R, self.W
        S = s["S"]
        NB = S // 512
        gc = self.gcols
        with ExitStack() as st:
            if vall is None:
                w_in, w_kr, w_krs, wk = self.load_w_in(st, l)
            else:
                w_in, w_kr, w_krs, wk = vall
            kw_in, kw_kr, kw_krs = "w_in" + wk, "w_kr" + wk, "w_krs" + wk
            xin = [self.sb(st, [128, 4, D], F32, "xin%d" % i) for i in range(2)]
            xn = self.sb(st, [128, 4, D], BF16, "xn")
            junk = self.sb(st, [128, 256], BF16, "junk")
            hTs = [self.sb(st, [128, 8, 512], BF16, "hT%d" % i) for i in range(2)]
            stat = self.sb(st, [128, 3, 16], F32, "stat")
            latn = self.sb(st, [128, 4, 384], BF16, "latn")
            qk_st = [self.sb(st, [128, 8, 512], BF16, "qkst%d" % i) for i in range(1)]
            v_st = [self.sb(st, [128, 4, 512], BF16, "vst%d" % i) for i in range(1)]
            kr_t = self.sb(st, [128, 2, 512], F32, "krt")
            psT = [self.ps(st, [128, 1024], BF16, "psT%d" % i) for i in range(2)]
            psL = [self.ps(st, [128, 512], F32, "psL%d" % i) for i in range(2)]
            psM = [self.ps(st, [128, 512], F32, "psM%d" % i) for i in range(3)]
            psLT = self.ps(st, [128, 1024], BF16, "psLT")
            psTr = Ring([(psT[0], "psT0"), (psT[1], "psT1")])
            psLr = Ring([(psL[0], "psL0"), (psL[1], "psL1")])
            psMr = Ring([(psM[i], "psM%d" % i) for i in range(3)])

            evac_eng = Ring(["act", "dve"])
            rope_state = {}

            def item_load(b):
                xb_, kx = xin[b % 2], "xin%d" % (b % 2)
                t0 = b * 512
                R.dma("sp", xb_[:, :, :], xsrc[t0:t0 + 512, :].rearrange("(j p) d -> p j d", p=128),
                      r=[("x", s["name"], b)], w=[kx])

            def item_norm_act(b):
                xb_, kx = xin[b % 2], "xin%d" % (b % 2)
                for j in range(4):
                    R.op("act", lambda e, j=j: e.activation(out=xn[:, j, :], in_=xb_[:, j, :], func=AF.Square,
                                                            accum_out=stat[:, 0, j:j + 1]),
                         r=[kx], w=["xn", "ss_x"])

            def item_norm(b, act=True):
                xb_, kx = xin[b % 2], "xin%d" % (b % 2)
                if act:
                    item_norm_act(b)
                self.rstd(stat[:, 0, 0:4], stat[:, 1, 0:4], stat[:, 2, 0:4], D, "ss_x", "ms_x", "rs_x")
                for j in range(4):
                    R.op("dve", lambda e, j=j: e.tensor_scalar(out=xn[:, j, :], in0=xb_[:, j, :],
                                                               scalar1=stat[:, 2, j:j + 1], scalar2=None,
                                                               op0=ALU.mult), r=[kx, "rs_x"], w=["xn"])

            def item_tr(b, half):
                hT, khT = hTs[b % 2], "hT%d" % (b % 2)
                for c2 in (2 * half, 2 * half + 1):
                    pt, kpt = psTr.next()
                    for cc in range(2):
                        c = c2 * 2 + cc
                        for j in range(4):
                            self.tr(pt[:, cc * 512 + j * 128: cc * 512 + (j + 1) * 128],
                                    xn[:, j, c * 128:(c + 1) * 128], self.ident_b[:, :],
                                    r=["xn", "const"], w=[kpt], inc=(cc == 1 and j == 3))
                    self.copy(evac_eng.next(), hT[:, c2 * 2:c2 * 2 + 2, :],
                              pt[:, :].rearrange("p (c t) -> p c t", c=2), r=[kpt], w=[khT])

            def item_lat(b, j):
                hT, khT = hTs[b % 2], "hT%d" % (b % 2)
                pl, kpl = psLr.next()
                for kc in range(8):
                    self.mm(pl[:, 0:384], hT[:, kc, j * 128:(j + 1) * 128], w_in[:, kc, 0:384],
                            start=(kc == 0), stop=(kc == 7), r=[khT, kw_in], w=[kpl], inc=(kc == 7))
                R.op("act", lambda e: e.activation(out=junk[:, 0:256], in_=pl[:, 0:256], func=AF.Square,
                                                   accum_out=stat[:, 0, 4 + j:5 + j]), r=[kpl], w=["ss_l"])
                R.op("act", lambda e: e.activation(out=junk[:, 0:128], in_=pl[:, 256:384], func=AF.Square,
                                                   accum_out=stat[:, 0, 8 + j:9 + j]), r=[kpl], w=["ss_l"])
                R.op("dve", lambda e: e.tensor_scalar(out=stat[:, 1, 4 + j:5 + j], in0=stat[:, 0, 4 + j:5 + j],
                                                      scalar1=1.0 / Q_LORA, scalar2=EPS, op0=ALU.mult,
                                                      op1=ALU.add), r=["ss_l"], w=["ms_l"])
                R.op("dve", lambda e: e.tensor_scalar(out=stat[:, 1, 8 + j:9 + j], in0=stat[:, 0, 8 + j:9 + j],
                                                      scalar1=1.0 / KV_LORA, scalar2=EPS, op0=ALU.mult,
                                                      op1=ALU.add), r=["ss_l"], w=["ms_l"])
                R.op("pool", lambda e: e.tensor_tensor(out=stat[:, 2, 4 + j:12 + j:4], in0=stat[:, 1, 4 + j:12 + j:4],
                                                       in1=self.neghalf[:, 0:2], op=ALU.pow),
                     r=["ms_l", "const"], w=["rs_l"])
                return (pl, kpl)

            def item_latnorm(b, j, plk):
                pl, kpl = plk
                R.op("dve", lambda e: e.tensor_scalar(out=latn[:, j, 0:256], in0=pl[:, 0:256],
                                                      scalar1=stat[:, 2, 4 + j:5 + j], scalar2=None,
                                                      op0=ALU.mult), r=[kpl, "rs_l"], w=["latn"])
                R.op("dve", lambda e: e.tensor_scalar(out=latn[:, j, 256:384], in0=pl[:, 256:384],
                                                      scalar1=stat[:, 2, 8 + j:9 + j], scalar2=None,
                                                      op0=ALU.mult), r=[kpl, "rs_l"], w=["latn"])

            def item_lattr(b):
                t0 = b * 512
                for c in range(3):
                    for j in range(4):
                        self.tr(psLT[:, j * 128:(j + 1) * 128], latn[:, j, c * 128:(c + 1) * 128], self.ident_b[:, :],
                                r=["latn", "const"], w=["psLT"], inc=(j == 3))
                    dst = qlatT[:, c, t0:t0 + 512] if c < 2 else kvlatT[:, t0:t0 + 512]
                    self.copy(evac_eng.next(), dst, psLT[:, 0:512], r=["psLT"], w=["qlatT" if c < 2 else "kvlatT"])

            def item_krope_mm(b):
                hT, khT = hTs[b % 2], "hT%d" % (b % 2)
                t0 = b * 512
                rt, krt = ropering.next()
                R.dma("sp", rt[64:96, :, :], s["rope"][:, :, t0:t0 + 512].rearrange("a r t -> r a t"), w=[krt])
                pa, kpa = psMr.next()
                pb, kpb = psMr.next()
                for kc in range(8):
                    self.mm(pa[0:96, :], w_kr[:, kc, :], hT[:, kc, :], start=(kc == 0), stop=(kc == 7),
                            r=[khT, kw_kr], w=[kpa], inc=(kc == 7))
                for kc in range(8):
                    self.mm(pb[0:96, :], w_krs[:, kc, :], hT[:, kc, :], start=(kc == 0), stop=(kc == 7),
                            r=[khT, kw_krs], w=[kpb], inc=(kc == 7))
                R.op("dve", lambda e: e.tensor_tensor(out=kr_t[64:96, 0, :], in0=pa[64:96, :],
                                                      in1=rt[64:96, 0, :], op=ALU.mult), r=[kpa, krt], w=["krt0"])
                R.op("dve", lambda e: e.tensor_tensor(out=kr_t[64:96, 1, :], in0=pb[64:96, :],
                                                      in1=rt[64:96, 1, :], op=ALU.mult), r=[kpb, krt], w=["krt1"])

            def item_krope_fin(b):
                t0 = b * 512
                for i in range(2):
                    R.op("dve", lambda e, i=i: e.tensor_tensor(out=KT[i][64:96, t0:t0 + 512], in0=kr_t[64:96, 0, :],
                                                               in1=kr_t[64:96, 1, :], op=ALU.add),
                         r=["krt0", "krt1"], w=["KTr%d" % i])

            def grp_qk(b, oc):
                hT, khT = hTs[b % 2], "hT%d" % (b % 2)
                qs, kqs = qk_st[0], "qkst0"
                t0 = b * 512
                pm, kpm = psMr.next()
                col = C_DQ + oc * 128
                for kc in range(8):
                    self.mm(pm[:, :], w_in[:, kc, col:col + 128], hT[:, kc, :], start=(kc == 0), stop=(kc == 7),
                            r=[khT, kw_in], w=[kpm], inc=(kc == 7))
                self.copy(evac_eng.next(), qs[:, oc, :], pm[:, :], r=[kpm], w=[kqs])
                if oc == 3:
                    R.dma("pool", s["qdT"][:, t0:t0 + 512].rearrange("(c p) t -> p c t", p=128), qs[:, 0:4, :],
                          r=[kqs], w=[("qdT", s["name"])])
                if oc == 7:
                    R.dma("pool", s["kdT"][:, t0:t0 + 512].rearrange("(c p) t -> p c t", p=128), qs[:, 4:8, :],
                          r=[kqs], w=[("kdT", s["name"])])

            def grp_v(b, j):
                hT, khT = hTs[b % 2], "hT%d" % (b % 2)
                vs, kvs = v_st[0], "vst0"
                t0 = b * 512
                pm, kpm = psMr.next()
                for kc in range(8):
                    self.mm(pm[:, :], hT[:, kc, j * 128:(j + 1) * 128], w_in[:, kc, C_DV:C_DV + 512],
                            start=(kc == 0), stop=(kc == 7), r=[khT, kw_in], w=[kpm], inc=(kc == 7))
                self.copy(evac_eng.next(), vs[:, j, :], pm[:, :], r=[kpm], w=[kvs])
                if j == 3:
                    R.dma("pool", s["vd"][t0:t0 + 512, :].rearrange("(j p) c -> p j c", p=128), vs[:, :, :],
                          r=[kvs], w=[("vd", s["name"])])

            item_load(0)
            if NB > 1:
                item_load(1)
            item_norm(0)
            item_tr(0, 0)
            item_tr(0, 1)
            for b in range(NB):
                groups = [lambda oc=oc: grp_qk(b, oc) for oc in range(8)] + [lambda j=j: grp_v(b, j) for j in range(4)]
                pls = {}
                chain = []
                chain.append(lambda: pls.__setitem__(0, item_lat(b, 0)))
                chain.append(lambda: pls.__setitem__(1, item_lat(b, 1)))
                if b + 1 < NB:
                    chain.append(lambda: item_norm_act(b + 1))
                chain.append(lambda: item_latnorm(b, 0, pls[0]))
                chain.append(lambda: (item_latnorm(b, 1, pls[1]), pls.__setitem__(2, item_lat(b, 2))))
                chain.append(lambda: pls.__setitem__(3, item_lat(b, 3)))
                chain.append(lambda: item_krope_mm(b))
                chain.append(lambda: item_latnorm(b, 2, pls[2]))
                chain.append(lambda: (item_latnorm(b, 3, pls[3]), item_krope_fin(b)))
                if b + 1 < NB:
                    chain.append(lambda: item_norm(b + 1, act=False))
                chain.append(lambda: item_lattr(b))
                if b + 1 < NB:
                    chain.append(lambda: item_tr(b + 1, 0))
                if b + 1 < NB:
                    chain.append(lambda: item_tr(b + 1, 1))
                if b + 2 < NB:
                    chain.append(lambda: item_load(b + 2))
                gi = 0
                for ci, c in enumerate(chain):
                    c()
                    if gi < len(groups):
                        groups[gi]()
                        gi += 1
                while gi < len(groups):
                    groups[gi]()
                    gi += 1
            R.barrier()

    def mla(self, l, s, nq, st_outer, qlatT, kvlatT, vall, KT, QT, w_uq, w_uqs, w_ukvk, ropering):
        nc, R, W = self.nc, self.R, self.W
        S = s["S"]
        NT = S // 128
        NB = S // 512
        NQB = nq // 512
        scale = 1.0 / math.sqrt(DQK)
        with ExitStack() as st:
            psS = [self.ps(st, [128, 1024], F32, "psS%d" % i) for i in range(3)]
            psO = self.ps(st, [128, 512], F32, "psO")
            psB = self.ps(st, [128, 512], F32, "psB")
            kpo, kpb = "psO", "psB"
            pT = [self.sb(st, [128, 1024], BF16, "pT%d" % i) for i in range(4)]
            oT = self.sb(st, [128, 512], F32, "oT")
            a_sb = [self.sb(st, [128, 4, 64], F32, "a_sb%d" % i) for i in range(2)]
            rcp = self.sb(st, [128, 4], F32, "rcp")
            qr_t = self.sb(st, [128, 2, 256], F32, "qrt")
            psSr = Ring([(psS[i], "psS%d" % i) for i in range(3)])
            pTr = Ring([(pT[i], "pT%d" % i) for i in range(4)])

            w_uq = self.sb(st, [128, 2, H * DQK], BF16, "w_uq")
            w_uqs = self.sb(st, [128, 2, H * DQK], BF16, "w_uqs")
            w_ukvk = self.sb(st, [128, H, 64], BF16, "w_ukvk")
            w_ukvv = self.sb(st, [128, H, 64], BF16, "w_ukvv")
            gc = self.gcols
            R.op("pool", lambda e: e.memset(w_uqs[:, :, :], 0.0), w=["w_uqs"])
            for c in range(2):
                self.load_w(w_uq[:, c, :], W["w_uq"][l, c * 128:(c + 1) * 128, :], gc[:, l, 24 + c:25 + c], "w_uq")
            w_uq4 = w_uq[:, :, :].rearrange("p c (h d) -> p c h d", d=DQK)
            w_uqs4 = w_uqs[:, :, :].rearrange("p c (h d) -> p c h d", d=DQK)
            for c in range(2):
                R.op("pool", lambda e, c=c: e.tensor_copy(w_uqs4[:, c, :, 64:80], w_uq4[:, c, :, 80:96]),
                     r=["w_uq"], w=["w_uqs"])
                R.op("pool", lambda e, c=c: e.tensor_copy(w_uqs4[:, c, :, 80:96], w_uq4[:, c, :, 64:80]),
                     r=["w_uq"], w=["w_uqs"])
            stg, kst = self.wstage.next()
            R.dma("sp", stg[:, 0:1024], W["w_ukv"][l, :, :], w=[kst])
            stg3 = stg[:, 0:1024].rearrange("p (h d) -> p h d", d=128)
            R.op("pool", lambda e: e.tensor_scalar(out=w_ukvk[:, :, :], in0=stg3[:, :, 0:64], scalar1=gc[:, l, 26:27],
                                                   scalar2=1.0, op0=ALU.mult, op1=ALU.mult),
                 r=[kst, "gcols"], w=["w_ukvk"])
            R.op("pool", lambda e: e.tensor_scalar(out=w_ukvv[:, :, :], in0=stg3[:, :, 64:128], scalar1=gc[:, l, 26:27],
                                                   scalar2=1.0, op0=ALU.mult, op1=ALU.mult),
                 r=[kst, "gcols"], w=["w_ukvv"])

            vall = self.sb(st, [128, NT, 4, 65], BF16, "vall")
            QT = [self.sb(st, [128, S], BF16, "QT%d" % i) for i in range(2)]
            R.op("pool", lambda e: e.memset(vall[:, :, :, 64:65], 1.0), w=["vall"])

            bring = Ring([(psB, "psB"), (psS[0][:, 0:512], "psS0"), (psS[1][:, 0:512], "psS1"),
                          (psS[2][:, 0:512], "psS2")])
            bevac = Ring(["dve", "act"])
            qrts = [qr_t, self.sb(st, [128, 2, 256], F32, "qrtb")]
            qrtr = Ring([(qrts[0], "qrtA"), (qrts[1], "qrtB")])

            def build_v(g):
                for t in range(NT):
                    pt, kpt = bring.next()
                    self.mm(pt[:, 0:256], kvlatT[:, t * 128:(t + 1) * 128],
                            w_ukvv[:, 4 * g:4 * g + 4, :].rearrange("p h d -> p (h d)"),
                            start=True, stop=True, r=["kvlatT", "w_ukvv"], w=[kpt])
                    self.copy(bevac.next(), vall[:, t, :, 0:64], pt[:, 0:256].rearrange("p (h d) -> p h d", d=64),
                              r=[kpt], w=["vall"])

            def head_steps(h, burst=False):
                kt_, kkt = KT[h % 2], "KTn%d" % (h % 2)
                qt_, kqt = QT[h % 2], "QT%d" % (h % 2)
                steps = []

                def kstep(b):
                    pt, kpt = bring.next() if burst else (psB, kpb)
                    self.mm(pt[0:64, :], w_ukvk[:, h, :], kvlatT[:, b * 512:(b + 1) * 512], start=True, stop=True,
                            r=["kvlatT", "w_ukvk"], w=[kpt])
                    self.copy(bevac.next() if burst else "dve", kt_[0:64, b * 512:(b + 1) * 512], pt[0:64, :],
                              r=[kpt], w=[kkt])

                def qstep(b2, state):
                    t0 = b2 * 256
                    if b2 % 2 == 0:
                        rt, krt = ropering.next()
                        R.dma("sp", rt[64:96, :, :], s["rope"][:, :, t0:t0 + 512].rearrange("a r t -> r a t"), w=[krt])
                        state["rt"] = (rt, krt)
                    rt, krt = state["rt"]
                    ro = (b2 % 2) * 256
                    pt, kpt = bring.next() if burst else (psB, kpb)
                    qr, kqr = qrtr.next()
                    pa = pt[0:96, 0:256]
                    pb = pt[0:96, 256:512]
                    for c in range(2):
                        self.mm(pa, w_uq[:, c, h * DQK:(h + 1) * DQK], qlatT[:, c, t0:t0 + 256],
                                start=(c == 0), stop=(c == 1), r=["qlatT", "w_uq"], w=[kpt], inc=False)
                    for c in range(2):
                        self.mm(pb, w_uqs[:, c, h * DQK:(h + 1) * DQK], qlatT[:, c, t0:t0 + 256],
                                start=(c == 0), stop=(c == 1), r=["qlatT", "w_uqs"], w=[kpt], inc=(c == 1))
                    self.copy("act" if burst else "dve", qt_[0:64, t0:t0 + 256], pt[0:64, 0:256], r=[kpt], w=[kqt])
                    R.op("dve", lambda e: e.tensor_tensor(out=qr[64:96, 0, :], in0=pt[64:96, 0:256],
                                                          in1=rt[64:96, 0, ro:ro + 256], op=ALU.mult),
                         r=[kpt, krt], w=[kqr + "0"])
                    R.op("dve", lambda e: e.tensor_tensor(out=qr[64:96, 1, :], in0=pt[64:96, 256:512],
                                                          in1=rt[64:96, 1, ro:ro + 256], op=ALU.mult),
                         r=[kpt, krt], w=[kqr + "1"])
                    R.op("dve", lambda e: e.tensor_tensor(out=qt_[64:96, t0:t0 + 256], in0=qr[64:96, 0, :],
                                                          in1=qr[64:96, 1, :], op=ALU.add),
                         r=[kqr + "0", kqr + "1"], w=[kqt])

                state = {}
                for b in range(NB):
                    steps.append(lambda b=b: kstep(b))
                for b2 in range(2 * NQB):
                    steps.append(lambda b2=b2: qstep(b2, state))
                return steps

            NP = NT // 2
            n_iter = NQB * NP
            for st_ in head_steps(0, burst=True):
                st_()
            for h in range(H):
                if h % 4 == 0:
                    build_v(h // 4)
                burst_next = (S <= 2048)
                nxt = head_steps(h + 1) if (h + 1 < H and not burst_next) else []
                every = max(1, n_iter // (len(nxt) + 1)) if nxt else 0
                kt_ = KT[h % 2]
                qt_ = QT[h % 2]
                kk = ["KTn%d" % (h % 2), "KTr%d" % (h % 2)]
                kq = "QT%d" % (h % 2)

                def issue_s(qb, p):
                    q0 = qb * 512
                    ps_, kps = psSr.next()
                    for i in range(2):
                        k0 = (2 * p + i) * 128
                        self.mm(ps_[:, i * 512:(i + 1) * 512], kt_[0:96, k0:k0 + 128], qt_[0:96, q0:q0 + 512],
                                start=True, stop=True, r=kk + [kq], w=[kps], inc=(i == 1))
                    pt_, kpt = pTr.next()
                    R.op("act", lambda e: e.activation(out=pt_[:, :], in_=ps_[:, :], func=AF.Exp, scale=scale),
                         r=[kps], w=[kpt])
                    return (qb, p, pt_, kpt)

                def issue_o(item):
                    qb, p, pt_, kpt = item
                    for i in range(2):
                        t = 2 * p + i
                        self.mm(psO[0:65, :], vall[:, t, h % 4, :], pt_[:, i * 512:(i + 1) * 512],
                                start=(t == 0), stop=(t == NT - 1), r=["vall", kpt], w=[kpo],
                                inc=(i == 1))
                    if p == NP - 1:
                        epilogue(qb)

                def epilogue(qb):
                    q0 = qb * 512
                    R.op("dve", lambda e: e.tensor_copy(oT[0:65, :], psO[0:65, :]), r=[kpo], w=["oT"])
                    for j in range(4):
                        self.tr(psB[:, j * 65:(j + 1) * 65], oT[0:65, j * 128:(j + 1) * 128], self.ident_f[0:65, 0:65],
                                r=["oT", "const"], w=[kpb], inc=(j == 3))
                    pb3 = psB[:, 0:260].rearrange("p (j d) -> p j d", d=65)
                    R.op("dve", lambda e: e.reciprocal(rcp[:, :], pb3[:, :, 64]), r=[kpb], w=["rcp"])
                    asb, kasb = a_sb[qb % 2], "a_sb%d" % (qb % 2)
                    R.op("dve", lambda e: e.tensor_tensor(
                        out=asb[:, :, :], in0=pb3[:, :, 0:64],
                        in1=rcp[:, :].unsqueeze(2).broadcast_to([128, 4, 64]), op=ALU.mult),
                        r=[kpb, "rcp"], w=[kasb])
                    R.dma("pool", s["a"][q0:q0 + 512, h * 64:(h + 1) * 64].rearrange("(j p) d -> p j d", p=128),
                          asb[:, :, :], r=[kasb], w=[("a", s["name"])])

                pend = []
                it = 0
                for qb in range(NQB):
                    for p in range(NP):
                        pend.append(issue_s(qb, p))
                        if len(pend) > 2:
                            issue_o(pend.pop(0))
                        it += 1
                        if nxt and it % every == 0:
                            nxt.pop(0)()
                while pend:
                    issue_o(pend.pop(0))
                while nxt:
                    nxt.pop(0)()
                if burst_next and h + 1 < H:
                    for st_ in head_steps(h + 1, burst=True):
                        st_()

    def dilated(self, l, s, nq):
        nc, R = self.nc, self.R
        S = s["S"]
        NT = S // 128
        with ExitStack() as st:
            acc = self.sb(st, [128, 2, S], F32, "dacc")
            qds = [self.sb(st, [128, S], BF16, "qd%d" % i) for i in range(1)]
            kds = [self.sb(st, [128, S], BF16, "kd%d" % i) for i in range(1)]
            vres = self.sb(st, [128, 3, NT, 2, 65], BF16, "vres")
            biasT = self.sb(st, [128, 6, 384], BF16, "biasT")
            pexp = [self.sb(st, [128, 2, 384], BF16, "pexp%d" % i) for i in range(3)]
            b_sb = [self.sb(st, [128, 4, 128], F32, "b_sb%d" % i) for i in range(5)]
            rcps = [self.sb(st, [128, 4], F32, "rcpd%d" % i) for i in range(2)]
            psS = [self.ps(st, [128, 2, 512], F32, "dpsS%d" % i) for i in range(2)]
            psO = [self.ps(st, [128, 512], F32, "dpsO%d" % i) for i in range(2)]
            psB = [self.ps(st, [128, 512], F32, "dpsB%d" % i) for i in range(2)]
            psSr = Ring([(psS[i], "dpsS%d" % i) for i in range(2)])
            psOr = Ring([(psO[i], "dpsO%d" % i) for i in range(2)])
            psBr = Ring([(psB[i], "dpsB%d" % i) for i in range(2)])
            per = Ring([(pexp[i], "pexp%d" % i) for i in range(3)])
            finr = Ring([(psB[0], "dpsB0"), (psB[1], "dpsB1"), (psO[0], "dpsO0"), (psO[1], "dpsO1"),
                         (psS[0][:, 0, :], "dpsS0"), (psS[1][:, 0, :], "dpsS1")])
            R.op("pool", lambda e: e.memset(vres[:, :, :, :, 64:65], 1.0), w=["vres0", "vres1", "vres2"])
            for hp in range(4):
                for hh_ in range(2):
                    for bi, d in enumerate(DIL):
                        R.op("dve", lambda e, hh_=hh_, bi=bi, d=d: e.tensor_scalar(
                            out=biasT[:, hh_ * 3 + bi, :], in0=self.bmask[:, :],
                            scalar1=8.0 * d * 2.0 ** (-(hp * 2 + hh_ + 1)),
                            scalar2=None, op0=ALU.mult), r=["const"], w=["biasT"])
                qd, kd = qds[0], kds[0]
                kqd, kkd = "qd0", "kd0"
                R.dma("act", qd[:, :], s["qdT"][hp * 128:(hp + 1) * 128, :], r=[("qdT", s["name"])], w=[kqd])
                R.dma("act", kd[:, :], s["kdT"][hp * 128:(hp + 1) * 128, :], r=[("kdT", s["name"])], w=[kkd])
                for bi, d in enumerate(DIL):
                    L = S // d
                    NU = L // 128
                    for r_ in range(d):
                        src = s["vd"][:, hp * 128:(hp + 1) * 128].rearrange("(u i dd) (hh c) -> dd i u hh c",
                                                                            i=128, dd=d, hh=2)[r_]
                        u0 = 0
                        while u0 < NU:
                            u1 = min(NU, u0 + 16)
                            for hh_ in range(2):
                                R.dma("sp", vres[:, bi, r_ * NU + u0:r_ * NU + u1, hh_, 0:64], src[:, u0:u1, hh_, :],
                                      r=[("vd", s["name"])], w=["vres%d" % bi])
                            u0 = u1
                its = []
                for bi, d in enumerate(DIL):
                    for r_ in range(d):
                        for jb in range((nq // d) // 128):
                            its.append((bi, d, r_, jb))

                def issue_s(it):
                    bi, d, r_, jb = it
                    NU = (S // d) // 128
                    tiles = [u for u in (jb - 1, jb, jb + 1) if 0 <= u < NU]
                    nt = len(tiles)
                    d0 = tiles[0] - (jb - 1)
                    qcols = slice(r_ + d * 128 * jb, r_ + d * 128 * jb + d * 127 + 1, d)
                    ps_, kps = psSr.next()
                    for hh in range(2):
                        h = hp * 2 + hh
                        self.mm(ps_[:, hh, 0:nt * 128], self.ident_b[:, :],
                                biasT[:, hh * 3 + bi, d0 * 128:(d0 + nt) * 128], start=True, stop=False,
                                r=["const", "biasT"], w=[kps], inc=False)
                    for ti, u in enumerate(tiles):
                        kcols = slice(r_ + d * 128 * u, r_ + d * 128 * u + d * 127 + 1, d)
                        for hh in range(2):
                            self.mm(ps_[:, hh, ti * 128:(ti + 1) * 128], kd[hh * 64:(hh + 1) * 64, kcols],
                                    qd[hh * 64:(hh + 1) * 64, qcols], start=False, stop=(ti == nt - 1),
                                    r=[kkd, kqd], w=[kps], inc=(ti == nt - 1 and hh == 1))
                    pe_, kpe = per.next()
                    R.op("act", lambda e: e.activation(out=pe_[:, :, 0:nt * 128], in_=ps_[:, :, 0:nt * 128],
                                                       func=AF.Exp, scale=0.125), r=[kps], w=[kpe])
                    return (it, tiles, pe_, kpe, qcols)

                def issue_pv(ctx):
                    (bi, d, r_, jb), tiles, pe_, kpe, qcols = ctx
                    NU = (S // d) // 128
                    nt = len(tiles)
                    po, kpo = psOr.next()
                    for hh in range(2):
                        for ti, u in enumerate(tiles):
                            self.mm(po[0:65, hh * 128:(hh + 1) * 128], vres[:, bi, r_ * NU + u, hh, :],
                                    pe_[:, hh, ti * 128:(ti + 1) * 128], start=(ti == 0), stop=(ti == nt - 1),
                                    r=["vres%d" % bi, kpe], w=[kpo], inc=(ti == nt - 1 and hh == 1))
                    accv = acc[0:65, :, qcols]
                    pov = po[0:65, 0:256].rearrange("p (h q) -> p h q", h=2)
                    if bi == 0:
                        R.op("dve", lambda e: e.tensor_copy(accv, pov), r=[kpo], w=["acc"])
                    else:
                        R.op("dve", lambda e: e.tensor_tensor(out=accv, in0=accv, in1=pov, op=ALU.add),
                             r=[kpo, "acc"], w=["acc"])

                pend = []
                for it in its:
                    pend.append(issue_s(it))
                    if len(pend) > 1:
                        issue_pv(pend.pop(0))
                while pend:
                    issue_pv(pend.pop(0))
                for qb in range(nq // 512):
                    q0 = qb * 512
                    bsb, kbsb = b_sb[qb % len(b_sb)], "b_sb%d" % (qb % len(b_sb))
                    for hh in range(2):
                        pb, kpb = finr.next()
                        for j in range(4):
                            self.tr(pb[:, j * 65:(j + 1) * 65], acc[0:65, hh, q0 + j * 128:q0 + (j + 1) * 128],
                                    self.ident_f[0:65, 0:65], r=["acc", "const"], w=[kpb], inc=(j == 3))
                        pb3 = pb[:, 0:260].rearrange("p (j d) -> p j d", d=65)
                        rc, krc = rcps[hh], "rcpd%d" % hh
                        R.op("dve", lambda e, pb3=pb3, rc=rc: e.reciprocal(rc[:, :], pb3[:, :, 64]), r=[kpb], w=[krc])
                        R.op("dve", lambda e, pb3=pb3, rc=rc, hh=hh: e.tensor_tensor(
                            out=bsb[:, :, hh * 64:(hh + 1) * 64], in0=pb3[:, :, 0:64],
                            in1=rc[:, :].unsqueeze(2).broadcast_to([128, 4, 64]), op=ALU.mult),
                            r=[kpb, krc], w=[kbsb])
                    R.dma("pool", s["b"][q0:q0 + 512, hp * 128:(hp + 1) * 128].rearrange("(j p) d -> p j d", p=128),
                          bsb[:, :, :], r=[kbsb], w=[("b", s["name"])])
            R.barrier()

    def phase_c(self, l, w_up):
        nc, R, W = self.nc, self.R, self.W
        gc = self.gcols
        last = (l == self.depth - 1)
        with ExitStack() as st:
            w_out = self.sb(st, [128, 8, D], BF16, "w_out")
            grep = self.sb(st, [128, D], F32, "gpm")
            ab = [self.sb(st, [128, 4, D], F32, "ab%d" % i) for i in range(3)]
            xin = [self.sb(st, [128, 4, D], F32, "xc%d" % i) for i in range(3)]
            abn = self.sb(st, [128, 4, D], BF16, "abn")
            junk = self.sb(st, [128, D], BF16, "junkc")
            mTs = [self.sb(st, [128, 8, 512], BF16, "mT%d" % i) for i in range(2)]
            stat = self.sb(st, [128, 3, 16], F32, "statc")
            tmp = self.sb(st, [128, D], F32, "tmpc")
            psT = [self.ps(st, [128, 1024], BF16, "cpsT%d" % i) for i in range(2)]
            psY = [self.ps(st, [128, 1024], F32, "cpsY%d" % i) for i in range(2)]
            psTr = Ring([(psT[0], "cpsT0"), (psT[1], "cpsT1")])
            for kc in range(8):
                g = gc[:, l, 16 + kc:17 + kc]
                self.load_w(w_out[:, kc, :], W["w_out"][l, kc * 128:(kc + 1) * 128, :], g, "w_out")
            R.dma("sp", grep[:, :], W["g_post_mix"][l, :].partition_broadcast(128), w=["grep"])
            evac_eng = Ring(["act", "dve"])
            blocks = []
            for s in self.seqs:
                nq = s["nq_last"] if last else s["S"]
                xsrc = s["x"] if l == 0 else s["xb"]
                for b in range(nq // 512):
                    blocks.append((s, xsrc, b * 512))
            NBk = len(blocks)

            def c_load(i):
                s, xsrc, t0 = blocks[i]
                ab_, kab = ab[i % 3], "ab%d" % (i % 3)
                x_, kx = xin[i % 3], "xc%d" % (i % 3)
                R.dma("sp", ab_[:, :, 0:512], s["a"][t0:t0 + 512, :].rearrange("(j p) d -> p j d", p=128),
                      r=[("a", s["name"])], w=[kab])
                R.dma("sp", ab_[:, :, 512:1024], s["b"][t0:t0 + 512, :].rearrange("(j p) d -> p j d", p=128),
                      r=[("b", s["name"])], w=[kab])
                R.dma("sp", x_[:, :, :], xsrc[t0:t0 + 512, :].rearrange("(j p) d -> p j d", p=128),
                      r=[("xb", s["name"])], w=[kx])

            def c_norm(i):
                ab_, kab = ab[i % 3], "ab%d" % (i % 3)
                for j in range(4):
                    for half in range(2):
                        R.op("act", lambda e, j=j, half=half: e.activation(
                            out=junk[:, 0:512], in_=ab_[:, j, half * 512:(half + 1) * 512], func=AF.Square,
                            accum_out=stat[:, 0, half * 4 + j:half * 4 + j + 1]), r=[kab], w=["ss_c"])
                self.rstd(stat[:, 0, 0:8], stat[:, 1, 0:8], stat[:, 2, 0:8], 512, "ss_c", "ms_c", "rs_c")
                for j in range(4):
                    for half in range(2):
                        R.op("dve", lambda e, j=j, half=half: e.tensor_scalar(
                            out=abn[:, j, half * 512:(half + 1) * 512], in0=ab_[:, j, half * 512:(half + 1) * 512],
                            scalar1=stat[:, 2, half * 4 + j:half * 4 + j + 1], scalar2=None, op0=ALU.mult),
                            r=[kab, "rs_c"], w=["abn"])

            def c_tr(i, half_):
                mT, kmT = mTs[i % 2], "mT%d" % (i % 2)
                for c2 in (2 * half_, 2 * half_ + 1):
                    pt, kpt = psTr.next()
                    for cc in range(2):
                        c = c2 * 2 + cc
                        for j in range(4):
                            self.tr(pt[:, cc * 512 + j * 128: cc * 512 + (j + 1) * 128],
                                    abn[:, j, c * 128:(c + 1) * 128], self.ident_b[:, :],
                                    r=["abn", "const"], w=[kpt], inc=(cc == 1 and j == 3))
                    self.copy(evac_eng.next(), mT[:, c2 * 2:c2 * 2 + 2, :],
                              pt[:, :].rearrange("p (c t) -> p c t", c=2), r=[kpt], w=[kmT])

            def c_mm(i, j):
                mT, kmT = mTs[i % 2], "mT%d" % (i % 2)
                x_, kx = xin[i % 3], "xc%d" % (i % 3)
                py, kpy = psY[j % 2], "cpsY%d" % (j % 2)
                for half in range(2):
                    for kc in range(8):
                        self.mm(py[:, half * 512:(half + 1) * 512], mT[:, kc, j * 128:(j + 1) * 128],
                                w_out[:, kc, half * 512:(half + 1) * 512], start=(kc == 0), stop=(kc == 7),
                                r=[kmT, "w_out"], w=[kpy], inc=(kc == 7 and half == 1))
                R.op("act", lambda e: e.activation(out=junk[:, :], in_=py[:, :], func=AF.Square,
                                                   accum_out=stat[:, 0, 8 + j:9 + j]), r=[kpy], w=["ss_y"])
                self.rstd(stat[:, 0, 8 + j:9 + j], stat[:, 1, 8 + j:9 + j], stat[:, 2, 8 + j:9 + j], D,
                          "ss_y", "ms_y", "rs_y")
                R.op("dve", lambda e: e.scalar_tensor_tensor(
                    out=tmp[:, :], in0=py[:, :], scalar=stat[:, 2, 8 + j:9 + j], in1=grep[:, :],
                    op0=ALU.mult, op1=ALU.mult), r=[kpy, "rs_y", "grep"], w=["tmpc"])
                R.op("dve", lambda e: e.tensor_tensor(out=x_[:, j, :], in0=x_[:, j, :], in1=tmp[:, :],
                                                      op=ALU.add), r=["tmpc", kx], w=[kx])

            def c_store(i):
                s, xsrc, t0 = blocks[i]
                x_, kx = xin[i % 3], "xc%d" % (i % 3)
                R.dma("pool", s["xa"][t0:t0 + 512, :].rearrange("(j p) d -> p j d", p=128), x_[:, :, :],
                      r=[kx], w=[("xa", s["name"])])

            wup_chunks = [(kc, c0) for kc in range(8) for c0 in range(0, DFF, 1024)]
            per_blk = -(-len(wup_chunks) // max(1, NBk - 1))

            def c_wup(n):
                for _ in range(n):
                    if not wup_chunks:
                        return
                    kc, c0 = wup_chunks.pop(0)
                    self.load_w(w_up[:, kc, c0:c0 + 1024], W["w_up"][l, kc * 128:(kc + 1) * 128, c0:c0 + 1024],
                                gc[:, l, 8 + kc:9 + kc], "w_up")

            c_load(0)
            if NBk > 1:
                c_load(1)
            if NBk > 2:
                c_load(2)
            c_norm(0)
            c_tr(0, 0)
            c_tr(0, 1)
            for i in range(NBk):
                c_mm(i, 0)
                if i + 1 < NBk:
                    c_norm(i + 1)
                c_mm(i, 1)
                if i + 1 < NBk:
                    c_tr(i + 1, 0)
                c_mm(i, 2)
                if i + 1 < NBk:
                    c_tr(i + 1, 1)
                c_mm(i, 3)
                c_store(i)
                if i + 3 < NBk:
                    c_load(i + 3)
            if w_up is not None:
                c_wup(len(wup_chunks))
            R.barrier()

    def phase_d(self, l, last, w_up, load_up=False):
        nc, R, W = self.nc, self.R, self.W
        gc = self.gcols
        TB = 256
        NJ = TB // 128
        with ExitStack() as st:
            w_dn = self.sb(st, [128, 32, D], BF16, "w_dn")
            grep = self.sb(st, [128, D], F32, "gpo")
            xin = [self.sb(st, [128, NJ, D], F32, "xd%d" % i) for i in range(2)]
            xn = self.sb(st, [128, NJ, D], BF16, "xnd")
            junk = self.sb(st, [128, D], BF16, "junkd")
            hTs = [self.sb(st, [128, 8, TB], BF16, "hTd%d" % i) for i in range(2)]
            uT = self.sb(st, [128, 32, TB], BF16, "uT")
            rl = [self.sb(st, [128, TB], F32, "rl%d" % i) for i in range(2)]
            stat = self.sb(st, [128, 3, 8], F32, "statd")
            tmp = self.sb(st, [128, D], F32, "tmpd")
            psT = [self.ps(st, [128, 1024], BF16, "dpT%d" % i) for i in range(2)]
            psU = [self.ps(st, [128, 512], F32, "dpU%d" % i) for i in range(2)]
            psY = [self.ps(st, [128, 1024], F32, "dpY%d" % i) for i in range(2)]
            psTr = Ring([(psT[0], "dpT0"), (psT[1], "dpT1")])
            psUr = Ring([(psU[0], "dpU0"), (psU[1], "dpU1")])
            rlr = Ring([(rl[0], "rl0"), (rl[1], "rl1")])
            ws2 = [self.sb(st, [128, 1024], F32, "wstD%d" % i) for i in range(3)]
            old_ring = self.wstage
            self.wstage = Ring(old_ring.items + [(ws2[i], "wstD%d" % i) for i in range(3)])
            dblocks = []
            for s in self.seqs:
                nq_ = s["nq_last"] if last else s["S"]
                for b in range(nq_ // TB):
                    dblocks.append((s, b * TB))
            preloaded = set()
            for i in range(min(2, len(dblocks))):
                s_, t0_ = dblocks[i]
                R.dma("sp", xin[i % 2][:, :, :], s_["xa"][t0_:t0_ + TB, :].rearrange("(j p) d -> p j d", p=128),
                      r=[("xa", s_["name"])], w=["xd%d" % (i % 2)])
                preloaded.add(i)
            if load_up:
                for kc in range(8):
                    self.load_w(w_up[:, kc, :], W["w_up"][l, kc * 128:(kc + 1) * 128, :], gc[:, l, 8 + kc:9 + kc], "w_up")
            for oc in range(32):
                self.load_w(w_dn[:, oc, :], W["w_down"][l, oc * 128:(oc + 1) * 128, :], None, "w_dn")
            self.wstage = old_ring
            R.dma("sp", grep[:, :], W["g_post_mlp"][l, :].partition_broadcast(128), w=["grepd"])
            evac_eng = Ring(["act", "dve"])
            ND = len(dblocks)

            def d_load(i):
                if i in preloaded or i >= ND:
                    return
                s_, t0_ = dblocks[i]
                R.dma("sp", xin[i % 2][:, :, :], s_["xa"][t0_:t0_ + TB, :].rearrange("(j p) d -> p j d", p=128),
                      r=[("xa", s_["name"])], w=["xd%d" % (i % 2)])

            def d_norm(i):
                x_, kx = xin[i % 2], "xd%d" % (i % 2)
                for j in range(NJ):
                    R.op("act", lambda e, j=j: e.activation(out=junk[:, :], in_=x_[:, j, :], func=AF.Square,
                                                            accum_out=stat[:, 0, j:j + 1]), r=[kx], w=["ss_d"])
                self.rstd(stat[:, 0, 0:NJ], stat[:, 1, 0:NJ], stat[:, 2, 0:NJ], D, "ss_d", "ms_d", "rs_d")
                for j in range(NJ):
                    R.op("dve", lambda e, j=j: e.tensor_scalar(out=xn[:, j, :], in0=x_[:, j, :],
                                                               scalar1=stat[:, 2, j:j + 1], scalar2=None,
                                                               op0=ALU.mult), r=[kx, "rs_d"], w=["xnd"])

            def d_tr(i):
                hT, khT = hTs[i % 2], "hTd%d" % (i % 2)
                for c4 in range(2):
                    pt, kpt = psTr.next()
                    for cc in range(4):
                        c = c4 * 4 + cc
                        for j in range(NJ):
                            self.tr(pt[:, cc * TB + j * 128: cc * TB + (j + 1) * 128],
                                    xn[:, j, c * 128:(c + 1) * 128], self.ident_b[:, :],
                                    r=["xnd", "const"], w=[kpt], inc=(cc == 3 and j == NJ - 1))
                    self.copy(evac_eng.next(), hT[:, c4 * 4:c4 * 4 + 4, :],
                              pt[:, :].rearrange("p (c t) -> p c t", c=4), r=[kpt], w=[khT])

            d_norm(0)
            d_tr(0)
            for i in range(ND):
                s, t0 = dblocks[i]
                dst = s["y"] if last else s["xb"]
                x_, kx = xin[i % 2], "xd%d" % (i % 2)
                hT, khT = hTs[i % 2], "hTd%d" % (i % 2)
                d_load(i + 1)
                for oc in range(32):
                    pu, kpu = psUr.next()
                    for kc in range(8):
                        self.mm(pu[:, 0:TB], w_up[:, kc, oc * 128:(oc + 1) * 128], hT[:, kc, :],
                                start=(kc == 0), stop=(kc == 7), r=[khT, "w_up"], w=[kpu], inc=(kc == 7))
                    r_, krl = rlr.next()
                    R.op("act", lambda e: e.activation(out=r_[:, :], in_=pu[:, 0:TB], func=AF.Relu), r=[kpu], w=[krl])
                    R.op("dve", lambda e: e.tensor_tensor(out=uT[:, oc, :], in0=r_[:, :], in1=r_[:, :], op=ALU.mult),
                         r=[krl], w=["uT"])
                    if oc == 12 and i + 1 < ND:
                        d_norm(i + 1)
                if i + 1 < ND:
                    d_tr(i + 1)
                for j in range(NJ):
                    py, kpy = psY[j % 2], "dpY%d" % (j % 2)
                    for half in range(2):
                        for oc in range(32):
                            self.mm(py[:, half * 512:(half + 1) * 512], uT[:, oc, j * 128:(j + 1) * 128],
                                    w_dn[:, oc, half * 512:(half + 1) * 512], start=(oc == 0), stop=(oc == 31),
                                    r=["uT", "w_dn"], w=[kpy], inc=(oc == 31 and half == 1))
                    R.op("act", lambda e: e.activation(out=junk[:, :], in_=py[:, :], func=AF.Square,
                                                       accum_out=stat[:, 0, 4 + j:5 + j]), r=[kpy], w=["ss_y"])
                    self.rstd(stat[:, 0, 4 + j:5 + j], stat[:, 1, 4 + j:5 + j], stat[:, 2, 4 + j:5 + j], D,
                              "ss_y", "ms_y", "rs_y")
                    R.op("dve", lambda e: e.scalar_tensor_tensor(
                        out=tmp[:, :], in0=py[:, :], scalar=stat[:, 2, 4 + j:5 + j], in1=grep[:, :],
                        op0=ALU.mult, op1=ALU.mult), r=[kpy, "rs_y", "grepd"], w=["tmpd"])
                    R.op("dve", lambda e: e.tensor_tensor(out=x_[:, j, :], in0=x_[:, j, :], in1=tmp[:, :],
                                                          op=ALU.add), r=["tmpd", kx], w=[kx])
                R.dma("pool", dst[t0:t0 + TB, :].rearrange("(j p) d -> p j d", p=128), x_[:, :, :],
                      r=[kx], w=[("xb", s["name"])])
            R.barrier()


def rope_tables(S, reverse=False):
    half = ROPE // 2
    inv_freq = (np.float32(10000.0) ** (-np.arange(half, dtype=np.float32) / np.float32(half))).astype(np.float32)
    pos = np.arange(S, dtype=np.float32)
    if reverse:
        pos = pos[::-1].copy()
    ang = (pos[:, None] * inv_freq[None, :]).astype(np.float32)
    c = np.cos(ang).astype(np.float32).T
    sn = np.sin(ang).astype(np.float32).T
    cosT = np.concatenate([c, c], axis=0)
    sinT = np.concatenate([-sn, sn], axis=0)
    return np.ascontiguousarray(np.stack([cosT, sinT], axis=0))


def band_mask():
    k = np.arange(128)[:, None, None]
    dl = np.arange(3)[None, :, None]
    q = np.arange(128)[None, None, :]
    rel = np.abs(128 * (dl - 1) + k - q)
    m = np.where(rel <= 64, -rel.astype(np.float32), np.float32(-1e30)).astype(np.float32)
    return np.ascontiguousarray(m.reshape(128, 384))


_CACHE = {}


def get_prog(seq_cfg, depth, debug=False):
    key = (tuple((s["name"], s["S"], s["nq_last"]) for s in seq_cfg), depth, debug)
    if key not in _CACHE:
        nc = bass.Bass("TRN2", target_bir_lowering=False)
        p = Prog(nc, [dict(s) for s in seq_cfg], depth=depth, debug=debug)
        p.build()
        _CACHE[key] = (nc, p)
    return _CACHE[key]


def kernel(x_prompt, x_sample, g_pre_mix, w_in, g_q_lat, w_uq, g_kv_lat, w_ukv, g_out_mla, g_out_dil,
           w_out, g_post_mix, g_pre_mlp, w_up, w_down, g_post_mlp):
    f = lambda a: np.ascontiguousarray(np.asarray(a, dtype=np.float32))
    x_prompt, x_sample = f(x_prompt), f(x_sample)
    B, S, _ = x_prompt.shape
    BS, SS, _ = x_sample.shape
    seq_cfg = [dict(name="p", S=S, nq_last=S // 2), dict(name="s", S=SS, nq_last=SS)]
    nc, prog = get_prog(seq_cfg, DEPTH)
    shared = {
        "g_pre_mix": f(g_pre_mix), "w_in": f(w_in), "g_q_lat": f(g_q_lat),
        "w_uq": f(w_uq).reshape(DEPTH, Q_LORA, H * DQK), "g_kv_lat": f(g_kv_lat),
        "w_ukv": f(w_ukv).reshape(DEPTH, KV_LORA, H * 128), "g_out_mla": f(g_out_mla), "g_out_dil": f(g_out_dil),
        "w_out": f(w_out), "g_post_mix": f(g_post_mix), "g_pre_mlp": f(g_pre_mlp), "w_up": f(w_up),
        "w_down": f(w_down), "g_post_mlp": f(g_post_mlp),
        "ident": np.eye(128, dtype=np.float32), "bmask": band_mask(),
    }
    rope_f, rope_r, rope_s = rope_tables(S), rope_tables(S, reverse=True), rope_tables(SS)
    in_maps = []
    for c in range(8):
        m = dict(shared)
        xp = x_prompt[c // 2]
        if c % 2 == 1:
            xp = np.ascontiguousarray(xp[::-1])
        m["x_p"] = xp
        m["rope_p"] = rope_r if c % 2 == 1 else rope_f
        m["x_s"] = x_sample[c]
        m["rope_s"] = rope_s
        in_maps.append(m)
    res = run_bass_kernel_spmd(nc, in_maps, core_ids=list(range(8)))
    yp = np.empty((B, S, D), np.float32)
    ys = np.empty((BS, SS, D), np.float32)
    for c in range(8):
        r = res.results[c]
        half = np.asarray(r["y_p"], dtype=np.float32)
        if c % 2 == 0:
            yp[c // 2, :S // 2] = half
        else:
            yp[c // 2, S // 2:] = half[::-1]
        ys[c] = np.asarray(r["y_s"], dtype=np.float32)
    return (yp, ys)
```
